# Optimizing a Trainium2 kernel written in Bass

```python
import math
import jax, jax.numpy as jnp
from jax import lax
import numpy as np

D_MODEL = 2048
BATCH = 4
SEQ = 8192
DEPTH = 1

HEAD_DIM = 128
FOX_HEADS = D_MODEL // (2 * HEAD_DIM)
GDN_HEADS = D_MODEL // (2 * HEAD_DIM)
FOX_W = FOX_HEADS * HEAD_DIM
GDN_W = GDN_HEADS * HEAD_DIM
MIX_W = FOX_W + GDN_W
CONV_W = 4
GDN_CHUNK = 64
Q_BLOCK = 128
D_FF = 256 * ((8 * D_MODEL // 3 + 255) // 256)
N_MOD = 9
MACARON_W = 0.5
EPS = 1e-6

_SIZES = [FOX_W, FOX_W, FOX_W, FOX_HEADS,
          3 * GDN_W, GDN_HEADS, GDN_HEADS, GDN_W]
IN_W = sum(_SIZES)
SPLIT_IDX = tuple(int(v) for v in np.cumsum(_SIZES)[:-1])

kernel_name = "hybrid_fox_gdn_macaron_adaln"


def rmsnorm(x, g):
    xf = x.astype(jnp.float32)
    y = xf * lax.rsqrt(jnp.mean(xf * xf, axis=-1, keepdims=True) + EPS)
    return y * g.astype(jnp.float32)


def ada_in(x, g, shift, scale):
    y = rmsnorm(x, g) * (1.0 + scale[:, None, :].astype(jnp.float32)) + shift[:, None, :].astype(jnp.float32)
    return y.astype(x.dtype)


def swiglu(h, wg, wu, wd):
    return (jax.nn.silu(h @ wg) * (h @ wu)) @ wd


def l2norm(t):
    return t * lax.rsqrt(jnp.sum(t * t, axis=-1, keepdims=True) + EPS)


def causal_dwconv(x, w):
    k = w.shape[0]
    return lax.conv_general_dilated(
        x, w[:, None, :].astype(x.dtype), window_strides=(1,), padding=[(k - 1, 0)],
        dimension_numbers=("NWC", "WIO", "NWC"), feature_group_count=x.shape[-1])


def forgetting_attention(q, k, v, logf):
    b, h, s, d = q.shape
    nb = s // Q_BLOCK
    scale = 1.0 / math.sqrt(d)
    cum = jnp.cumsum(logf, axis=-1)
    qb = jnp.moveaxis(q.reshape(b, h, nb, Q_BLOCK, d), 2, 0)
    cb = jnp.moveaxis(cum.reshape(b, h, nb, Q_BLOCK), 2, 0)
    kpos = jnp.arange(s)

    def one_block(args):
        i, q_i, c_i = args
        logits = jnp.einsum("bhqd,bhkd->bhqk", q_i, k) * scale + c_i[..., None] - cum[:, :, None, :]
        qpos = i * Q_BLOCK + jnp.arange(Q_BLOCK)
        mask = kpos[None, :] <= qpos[:, None]
        p = jax.nn.softmax(jnp.where(mask, logits, -jnp.inf), axis=-1)
        return jnp.einsum("bhqk,bhkd->bhqd", p, v)

    o = lax.map(one_block, (jnp.arange(nb), qb, cb))
    return jnp.moveaxis(o, 0, 2).reshape(b, h, s, d)


def gated_delta_rule(q, k, v, g, beta):
    b, h, s, dk = q.shape
    dv = v.shape[-1]
    c = GDN_CHUNK
    n = s // c
    q = q * (dk ** -0.5)
    kb = k * beta[..., None]
    vb = v * beta[..., None]
    resh = lambda t: t.reshape(b, h, n, c, *t.shape[3:])
    q, k, kb, vb, g = resh(q), resh(k), resh(kb), resh(vb), resh(g)
    g = jnp.cumsum(g, axis=-1)
    incl = jnp.tril(jnp.ones((c, c), dtype=bool))
    strict = jnp.tril(jnp.ones((c, c), dtype=bool), -1)
    decay = jnp.exp(jnp.where(incl, g[..., :, None] - g[..., None, :], -jnp.inf))
    lower = jnp.where(strict, jnp.einsum("bhnid,bhnjd->bhnij", kb, k) * decay, 0.0)
    eye = jnp.eye(c, dtype=q.dtype)
    t_inv = lax.linalg.triangular_solve(eye + lower, jnp.broadcast_to(eye, lower.shape),
                                        left_side=True, lower=True, unit_diagonal=True)
    u = t_inv @ vb
    w = t_inv @ (kb * jnp.exp(g)[..., None])
    a_intra = jnp.where(incl, jnp.einsum("bhnid,bhnjd->bhnij", q, k) * decay, 0.0)

    def step(state, inp):
        q_i, k_i, u_i, w_i, g_i, a_i = inp
        v_new = u_i - w_i @ state
        o = (q_i * jnp.exp(g_i)[..., None]) @ state + a_i @ v_new
        g_last = g_i[..., -1]
        state = state * jnp.exp(g_last)[..., None, None] + jnp.einsum(
            "bhcd,bhce->bhde", k_i * jnp.exp(g_last[..., None] - g_i)[..., None], v_new)
        return state, o

    mv = lambda t: jnp.moveaxis(t, 2, 0)
    state0 = jnp.zeros((b, h, dk, dv), dtype=q.dtype)
    _, o = lax.scan(step, state0, (mv(q), mv(k), mv(u), mv(w), mv(g), mv(a_intra)))
    return jnp.moveaxis(o, 0, 2).reshape(b, h, s, dv)


def hybrid_mixer(h, w_in, w_out, fox_f_bias, fox_out_norm, gdn_conv, gdn_A_log, gdn_dt_bias, gdn_out_norm):
    bsz, s, _ = h.shape
    proj = (h @ w_in).astype(jnp.float32)
    q_f, k_f, v_f, f_f, qkv_g, a_g, b_g, z_g = jnp.split(proj, SPLIT_IDX, axis=-1)
    heads = lambda t, nh: t.reshape(bsz, s, nh, HEAD_DIM).transpose(0, 2, 1, 3)

    logf = jax.nn.log_sigmoid(f_f + fox_f_bias.astype(jnp.float32)).transpose(0, 2, 1)
    o_f = forgetting_attention(heads(q_f, FOX_HEADS), heads(k_f, FOX_HEADS), heads(v_f, FOX_HEADS), logf)
    o_f = rmsnorm(o_f.transpose(0, 2, 1, 3), fox_out_norm).reshape(bsz, s, FOX_W)

    qkv_g = jax.nn.silu(causal_dwconv(qkv_g, gdn_conv.astype(jnp.float32)))
    q_g, k_g, v_g = jnp.split(qkv_g, 3, axis=-1)
    q_g = l2norm(heads(q_g, GDN_HEADS))
    k_g = l2norm(heads(k_g, GDN_HEADS))
    v_g = heads(v_g, GDN_HEADS)
    g_log = (-jnp.exp(gdn_A_log.astype(jnp.float32))
             * jax.nn.softplus(a_g + gdn_dt_bias.astype(jnp.float32))).transpose(0, 2, 1)
    beta = jax.nn.sigmoid(b_g).transpose(0, 2, 1)
    o_g = gated_delta_rule(q_g, k_g, v_g, g_log, beta).transpose(0, 2, 1, 3)
    z = z_g.reshape(bsz, s, GDN_HEADS, HEAD_DIM)
    o_g = (rmsnorm(o_g, gdn_out_norm) * jax.nn.silu(z)).reshape(bsz, s, GDN_W)

    o = jnp.concatenate([o_f, o_g], axis=-1).astype(h.dtype)
    return o @ w_out


def setup_inputs(seed: int = 0) -> dict:
    key = jax.random.key(seed)
    ks = jax.random.split(key, 20)
    nrm = lambda k, shape, s: jax.random.normal(k, shape, jnp.float32) * s
    x = nrm(ks[0], (BATCH, SEQ, D_MODEL), 1.0)
    c = nrm(ks[1], (BATCH, D_MODEL), 1.0)
    ada_w = nrm(ks[2], (DEPTH, D_MODEL, N_MOD * D_MODEL), 0.5 * D_MODEL ** -0.5)
    ada_b = nrm(ks[3], (DEPTH, N_MOD * D_MODEL), 0.1)
    norm_g = 1.0 + nrm(ks[4], (DEPTH, 3, D_MODEL), 0.1)
    ffn_w_gate = nrm(ks[5], (DEPTH, 2, D_MODEL, D_FF), D_MODEL ** -0.5)
    ffn_w_up = nrm(ks[6], (DEPTH, 2, D_MODEL, D_FF), D_MODEL ** -0.5)
    ffn_w_down = nrm(ks[7], (DEPTH, 2, D_FF, D_MODEL), D_FF ** -0.5)
    w_in = nrm(ks[8], (DEPTH, D_MODEL, IN_W), D_MODEL ** -0.5)
    w_out = nrm(ks[9], (DEPTH, MIX_W, D_MODEL), MIX_W ** -0.5)
    fox_f_bias = 3.0 + nrm(ks[10], (DEPTH, FOX_HEADS), 0.5)
    fox_out_norm = 1.0 + nrm(ks[11], (DEPTH, HEAD_DIM), 0.1)
    gdn_conv = nrm(ks[12], (DEPTH, CONV_W, 3 * GDN_W), CONV_W ** -0.5)
    gdn_A_log = jnp.log(jax.random.uniform(ks[13], (DEPTH, GDN_HEADS), jnp.float32, 1.0, 16.0))
    dt = jnp.exp(jax.random.uniform(ks[14], (DEPTH, GDN_HEADS), jnp.float32,
                                    math.log(1e-3), math.log(1e-1)))
    gdn_dt_bias = dt + jnp.log(-jnp.expm1(-dt))
    gdn_out_norm = 1.0 + nrm(ks[15], (DEPTH, HEAD_DIM), 0.1)
    final_norm = 1.0 + nrm(ks[16], (D_MODEL,), 0.1)
    return {"x": x, "c": c, "ada_w": ada_w, "ada_b": ada_b, "norm_g": norm_g,
            "ffn_w_gate": ffn_w_gate, "ffn_w_up": ffn_w_up, "ffn_w_down": ffn_w_down,
            "w_in": w_in, "w_out": w_out, "fox_f_bias": fox_f_bias, "fox_out_norm": fox_out_norm,
            "gdn_conv": gdn_conv, "gdn_A_log": gdn_A_log, "gdn_dt_bias": gdn_dt_bias,
            "gdn_out_norm": gdn_out_norm, "final_norm": final_norm}


def reference(x, c, ada_w, ada_b, norm_g, ffn_w_gate, ffn_w_up, ffn_w_down, w_in, w_out,
              fox_f_bias, fox_out_norm, gdn_conv, gdn_A_log, gdn_dt_bias, gdn_out_norm, final_norm):
    cond = jax.nn.silu(c)
    for l in range(DEPTH):
        mod = cond @ ada_w[l] + ada_b[l]
        sh1, sc1, gt1, sh2, sc2, gt2, sh3, sc3, gt3 = jnp.split(mod, N_MOD, axis=-1)
        h = ada_in(x, norm_g[l, 0], sh1, sc1)
        x = x + MACARON_W * gt1[:, None, :] * swiglu(h, ffn_w_gate[l, 0], ffn_w_up[l, 0], ffn_w_down[l, 0])
        h = ada_in(x, norm_g[l, 1], sh2, sc2)
        x = x + gt2[:, None, :] * hybrid_mixer(h, w_in[l], w_out[l], fox_f_bias[l], fox_out_norm[l],
                                               gdn_conv[l], gdn_A_log[l], gdn_dt_bias[l], gdn_out_norm[l])
        h = ada_in(x, norm_g[l, 2], sh3, sc3)
        x = x + MACARON_W * gt3[:, None, :] * swiglu(h, ffn_w_gate[l, 1], ffn_w_up[l, 1], ffn_w_down[l, 1])
    return rmsnorm(x, final_norm).astype(x.dtype)
```

```python
import math
from collections import deque
from contextlib import ExitStack
import numpy as np
import concourse.bass as bass
import concourse.mybir as mybir
from concourse.bass_utils import run_bass_kernel_spmd

F32 = mybir.dt.float32
BF16 = mybir.dt.bfloat16
AF = mybir.ActivationFunctionType
ALU = mybir.AluOpType
EPS = 1e-6
HD = 128
GDN_LIMIT = 99
GDN_SUB = 99
GDN_SUB2 = 99
GDN_SUB3 = 99


class Cfg:
    def __init__(self, D=2048, TP=0, TO=8192, dbg=False):
        self.D = D
        self.FF = 256 * ((8 * D // 3 + 255) // 256)
        self.NH = D // 256
        self.TP, self.TO = TP, TO
        self.TT = TP + TO
        self.DC = D // 128
        self.FC = self.FF // 128
        self.TILE = 512
        self.NT = self.TT // 512
        self.NTP = TP // 512
        self.NTO = TO // 512
        self.NB = self.TT // 128
        self.NBP = TP // 128
        self.NBO = TO // 128
        self.VW = min(512, self.NH * 128)
        self.NGV = self.NH * 128 // self.VW
        self.NGF = 6 * self.NH * 128 // 512
        self.NFG = self.FF // 512
        self.NGO = D // 512
        self.NGA = 9 * D // 512
        self.WELEMS = max(self.DC * 512, self.FC * 128)
        self.dbg = dbg


class Reg:
    __slots__ = ("name", "lw", "rd")

    def __init__(self, name=""):
        self.name = name
        self.lw = None
        self.rd = {}


class Ctx:
    SEM_LIMIT = 30000
    NDMA = 40

    def __init__(self, nc):
        self.nc = nc
        self.eng = {"pe": nc.tensor, "act": nc.scalar, "dve": nc.vector, "pool": nc.gpsimd, "sp": nc.sync}
        self.sems = {}
        self.csem = {}
        self.ccount = {}
        self.nsem = 0
        for e in ("pe", "act", "dve", "pool"):
            self._new_csem(e)
        self.waited = {}
        self.dma_pool = []
        for i in range(self.NDMA):
            self.dma_pool.append([self._alloc(f"dma{i}"), 0])
        self.dma_next = 0
        self.sw_pool = []
        for i in range(8):
            self.sw_pool.append([self._alloc(f"swdma{i}"), 0])
        self.sw_next = 0
        self.n_inst = 0
        self.n_wait = 0

    def _alloc(self, name):
        self.nsem += 1
        key = f"{name}_{self.nsem}"
        self.sems[key] = self.nc.alloc_semaphore(key)
        return key

    def _new_csem(self, e):
        self.csem[e] = self._alloc(f"c_{e}")
        self.ccount[e] = 0

    def _wait(self, e, deps):
        for (k, v) in deps:
            if e == "pe" and k.startswith("c_pe"):
                continue
            if self.waited.get((e, k), 0) >= v:
                continue
            self.eng[e].wait_ge(self.sems[k], v)
            self.waited[(e, k)] = v
            self.n_wait += 1

    @staticmethod
    def _deps(reads, writes):
        deps = []
        for r in reads:
            if r.lw is not None:
                deps.append(r.lw)
        for w in writes:
            if w.lw is not None:
                deps.append(w.lw)
            deps.extend(w.rd.items())
        return deps

    @staticmethod
    def _mark(tok, reads, writes):
        k, v = tok
        for r in reads:
            if r.rd.get(k, 0) < v:
                r.rd[k] = v
        for w in writes:
            w.lw = tok
            w.rd = {}

    def op(self, e, fn, reads=(), writes=(), signal=True):
        pr = [r for r in reads if r.name.startswith("ps")]
        if pr:
            reads = [r for r in reads if not r.name.startswith("ps")]
            writes = list(writes) + pr
        self._wait(e, self._deps(reads, writes))
        ins = fn()
        self.n_inst += 1
        if signal:
            self.ccount[e] += 1
            ins.then_inc(self.sems[self.csem[e]], 1)
            self._mark((self.csem[e], self.ccount[e]), reads, writes)
            if self.ccount[e] >= self.SEM_LIMIT:
                self._new_csem(e)
        else:
            self._mark((self.csem[e], self.ccount[e] + 1), reads, writes)
        return ins

    def dma(self, q, out, in_, reads=(), writes=(), **kw):
        if q == "pool":
            slot = self.sw_pool[self.sw_next]
            self.sw_next = (self.sw_next + 1) % len(self.sw_pool)
        else:
            slot = self.dma_pool[self.dma_next]
            self.dma_next = (self.dma_next + 1) % self.NDMA
        k, tot = slot
        deps = self._deps(reads, writes)
        if tot > 0:
            deps.append((k, tot))
        self._wait(q, deps)
        ins = self.eng[q].dma_start(out=out, in_=in_, **kw)
        ins.then_inc(self.sems[k], 16)
        slot[1] = tot + 16
        self.n_inst += 1
        self._mark((k, tot + 16), reads, writes)
        return ins

    def _all_tokens(self):
        toks = []
        for e in ("pe", "act", "dve", "pool"):
            if self.ccount[e] > 0:
                toks.append((self.csem[e], self.ccount[e]))
        for k, tot in self.dma_pool + self.sw_pool:
            if tot > 0:
                toks.append((k, tot))
        return toks

    def barrier(self):
        toks = self._all_tokens()
        for e in ("pe", "act", "dve", "pool", "sp"):
            for (k, v) in toks:
                if self.waited.get((e, k), 0) >= v:
                    continue
                self.eng[e].wait_ge(self.sems[k], v)
                self.waited[(e, k)] = v
                self.n_wait += 1

    def final_wait(self, e="sp"):
        for (k, v) in self._all_tokens():
            if self.waited.get((e, k), 0) >= v:
                continue
            self.eng[e].wait_ge(self.sems[k], v)
            self.waited[(e, k)] = v


def _fm_groups(w, gw):
    K, Fd = w.shape
    return np.ascontiguousarray(w.reshape(K // 128, 128, Fd // gw, gw).transpose(2, 1, 0, 3))


def _vec_fm(v):
    return np.ascontiguousarray(v.reshape(-1, 128).T)


def host_weights(cfg, inp):
    NH = cfg.NH
    FW = NH * 128
    w = {}
    w["adaw"] = _fm_groups(inp["ada_w"][0], 512)
    w["adab"] = _vec_fm(inp["ada_b"][0])
    w["normg"] = _vec_fm(inp["norm_g"][0].reshape(-1))
    w["finalg"] = _vec_fm(inp["final_norm"])
    for i in range(2):
        w[f"wg{i}"] = _fm_groups(inp["ffn_w_gate"][0, i], 512)
        w[f"wu{i}"] = _fm_groups(inp["ffn_w_up"][0, i], 512)
        w[f"wd{i}"] = _fm_groups(inp["ffn_w_down"][0, i], 128)
    win = inp["w_in"][0]
    o = 0
    q_f = win[:, o:o + FW]; o += FW
    k_f = win[:, o:o + FW]; o += FW
    v_f = win[:, o:o + FW]; o += FW
    f_f = win[:, o:o + NH]; o += NH
    qkv = win[:, o:o + 3 * FW]; o += 3 * FW
    a_g = win[:, o:o + NH]; o += NH
    b_g = win[:, o:o + NH]; o += NH
    z_g = win[:, o:o + FW]; o += FW
    assert o == win.shape[1]
    w["winf"] = _fm_groups(np.concatenate([q_f, k_f, qkv, z_g], axis=1), 512)
    w["winv"] = _fm_groups(v_f, cfg.VW)
    sm = np.concatenate([f_f, a_g, b_g], axis=1)
    w["wins"] = np.ascontiguousarray(sm.reshape(cfg.DC, 128, 3 * NH).transpose(1, 0, 2))
    w["wout"] = _fm_groups(inp["w_out"][0], 512)
    bc = lambda v: np.ascontiguousarray(np.broadcast_to(v[None, :], (128, v.shape[0])))
    w["fbias"] = bc(inp["fox_f_bias"][0])
    w["alog"] = bc(inp["gdn_A_log"][0])
    w["dtb"] = bc(inp["gdn_dt_bias"][0])
    cw = inp["gdn_conv"][0]
    w["convw"] = np.ascontiguousarray(cw.reshape(4, 3 * NH, 128).transpose(2, 1, 0))
    w["foxg"] = np.ascontiguousarray(inp["fox_out_norm"][0].reshape(128, 1))
    w["gdng"] = np.ascontiguousarray(inp["gdn_out_norm"][0].reshape(128, 1))
    return {k: np.ascontiguousarray(v, dtype=np.float32) for k, v in w.items()}


def host_x(cfg, xs):
    return np.ascontiguousarray(xs.reshape(cfg.NT, 512, cfg.DC, 128).transpose(0, 3, 2, 1))


def host_unx(cfg, yT):
    return np.ascontiguousarray(yT.transpose(0, 3, 2, 1).reshape(cfg.TO, cfg.D))


WSHAPES = None


def weight_shapes(cfg):
    D, DC, FC, NH = cfg.D, cfg.DC, cfg.FC, cfg.NH
    s = {
        "adaw": [cfg.NGA, 128, DC, 512], "adab": [128, 9 * DC], "normg": [128, 3 * DC], "finalg": [128, DC],
        "winf": [cfg.NGF, 128, DC, 512], "winv": [cfg.NGV, 128, DC, cfg.VW], "wins": [128, DC, 3 * NH],
        "wout": [cfg.NGO, 128, DC, 512], "fbias": [128, NH], "alog": [128, NH], "dtb": [128, NH],
        "convw": [128, 3 * NH, 4], "foxg": [128, 1], "gdng": [128, 1],
    }
    for i in range(2):
        s[f"wg{i}"] = [cfg.NFG, 128, DC, 512]
        s[f"wu{i}"] = [cfg.NFG, 128, DC, 512]
        s[f"wd{i}"] = [DC, 128, FC, 128]
    return s


CAST = ["wg0", "wu0", "wd0", "wg1", "wu1", "wd1", "winf", "winv", "wins", "wout"]


def build(cfg, stop_after=99, skip_gdn=False):
    nc = bass.Bass("TRN2", target_bir_lowering=False)
    c = Ctx(nc)
    D, DC, FC, NH, NT, NTP, NTO = cfg.D, cfg.DC, cfg.FC, cfg.NH, cfg.NT, cfg.NTP, cfg.NTO
    TT, TO, TP, NB, NBP, NBO = cfg.TT, cfg.TO, cfg.TP, cfg.NB, cfg.NBP, cfg.NBO
    FW = NH * 128

    def dram(name, shape, dt, kind="Internal"):
        return nc.dram_tensor(name, list(shape), dt, kind=kind).ap()

    skind = "ExternalOutput" if cfg.dbg else "Internal"
    xT = dram("xT", [NT, 128, DC, 512], F32, "ExternalInput")
    cT = dram("cT", [128, DC], F32, "ExternalInput")
    pm = dram("pmask", [128, 1], F32, "ExternalInput")
    W = {k: dram(k, s, F32, "ExternalInput") for k, s in weight_shapes(cfg).items()}
    yT = dram("yT", [NTO, 128, DC, 512], F32, "ExternalOutput")
    WB = {k: dram(k + "_bf", weight_shapes(cfg)[k], BF16) for k in CAST}
    x1s = dram("x1s", [NTO, 128, DC, 512], F32, skind)
    QfT = dram("QfT", [NH, 128, TO], BF16, skind)
    KfT = dram("KfT", [NH, 128, TT], BF16, skind)
    Vf = dram("Vf", [TT, FW], BF16, skind)
    GqT = dram("GqT", [NH, 128, TT], BF16, skind)
    GkT = dram("GkT", [NH, 128, TT], BF16, skind)
    GvT = dram("GvT", [NH, 128, TT], BF16, skind)
    ZsT = dram("ZsT", [NH, 128, TO], BF16, skind)
    gat = dram("gat", [TT, 3 * NH], F32, skind)
    gatT = dram("gatT", [3 * NH, TT], F32, skind)
    oT = dram("oT", [NTO, 128, DC, 512], BF16, skind)

    def sb(name, shape, dt):
        return nc.alloc_sbuf_tensor(name, list(shape), dt).ap()

    ones_bf = sb("ones_bf", [128, 128], BF16)
    ident_f = sb("ident_f", [128, 128], F32)
    ident_bf = sb("ident_bf", [128, 128], BF16)
    Rc = Reg("const")
    c.op("pool", lambda: nc.gpsimd.memset(ones_bf, 1.0), writes=[Rc])
    c.op("pool", lambda: nc.gpsimd.memset(ident_f, 1.0), writes=[Rc])
    c.op("pool", lambda: nc.gpsimd.affine_select(out=ident_f, in_=ident_f, pattern=[[-1, 128]], compare_op=ALU.is_equal,
                                                 fill=0.0, base=0, channel_multiplier=1), reads=[Rc], writes=[Rc])
    c.op("pool", lambda: nc.gpsimd.tensor_copy(ident_bf, ident_f), reads=[Rc], writes=[Rc])
    small = {}
    for k in ["adab", "normg", "finalg", "fbias", "alog", "dtb", "convw", "foxg", "gdng"]:
        small[k] = sb("s_" + k, weight_shapes(cfg)[k], F32)
        c.dma("sp", small[k], W[k], writes=[Rc])
    cond = sb("cond", [128, DC], F32)
    c.dma("sp", cond, cT, writes=[Rc])
    pmask = sb("pmask_sb", [128, 1], F32)
    c.dma("sp", pmask, pm, writes=[Rc])

    for k in CAST:
        shp = weight_shapes(cfg)[k]
        n = int(np.prod(shp))
        letters = "abcd"[:len(shp)]
        pat = " ".join(letters)
        cw = 2048
        while n % cw:
            cw //= 2
        src = W[k].rearrange(f"{pat} -> ({pat})").rearrange("(r c) -> r c", c=cw)
        dst = WB[k].rearrange(f"{pat} -> ({pat})").rearrange("(r c) -> r c", c=cw)
        rows = n // cw
        for r0 in range(0, rows, 4096):
            r1 = min(rows, r0 + 4096)
            c.dma("pool", dst[r0:r1, :], src[r0:r1, :])

    PS = [nc.alloc_psum_tensor(f"ps{i}", [128, 512], F32).ap() for i in range(8)]
    PR = [Reg(f"ps{i}") for i in range(8)]

    mod = sb("mod", [128, 9 * DC], F32)
    Rmod = Reg("mod")
    gs = sb("gs", [128, 3 * DC], F32)
    gate = sb("gate", [128, 3 * DC], F32)
    shift = sb("shift", [128, 3 * DC], F32)
    c.op("act", lambda: nc.scalar.activation(cond, cond, AF.Silu), reads=[Rc], writes=[Rc])
    es0 = ExitStack()
    adabuf = [es0.enter_context(nc.sbuf_tensor(f"adabuf{i}", [128, DC, 512], F32)).ap() for i in range(2)]
    adar = [Reg("adabuf0"), Reg("adabuf1")]
    for g in range(cfg.NGA):
        bi = g % 2
        c.dma("sp", adabuf[bi], W["adaw"][g], writes=[adar[bi]])
        for j in range(4):
            col = g * 4 + j
            for kc in range(DC):
                last = kc == DC - 1
                c.op("pe", lambda: nc.tensor.matmul(PS[7][:, col:col + 1], adabuf[bi][:, kc, j * 128:(j + 1) * 128],
                                                    cond[:, kc:kc + 1], start=(kc == 0), stop=last),
                     reads=[adar[bi], Rc], writes=[PR[7]], signal=last)
    c.op("dve", lambda: nc.vector.tensor_tensor(mod, PS[7][:, 0:9 * DC], small["adab"], op=ALU.add),
         reads=[PR[7], Rc], writes=[Rmod])
    for i in range(3):
        sh = mod[:, (3 * i) * DC:(3 * i + 1) * DC]
        sc = mod[:, (3 * i + 1) * DC:(3 * i + 2) * DC]
        gt = mod[:, (3 * i + 2) * DC:(3 * i + 3) * DC]
        sl = slice(i * DC, (i + 1) * DC)
        c.op("dve", lambda: nc.vector.scalar_tensor_tensor(gs[:, sl], sc, 1.0, small["normg"][:, sl], op0=ALU.add, op1=ALU.mult),
             reads=[Rmod, Rc], writes=[Rmod])
        c.op("dve", lambda: nc.vector.tensor_copy(shift[:, sl], sh), reads=[Rmod], writes=[Rmod])
        mw = 1.0 if i == 1 else 0.5
        c.op("dve", lambda: nc.vector.tensor_scalar(gate[:, sl], gt, mw, None, op0=ALU.mult), reads=[Rmod], writes=[Rmod])
    c.barrier()
    es0.close()
    if stop_after <= 0:
        c.dma("sp", yT[0, :, 0, 0:9 * DC], mod, reads=[Rmod])
        c.dma("sp", yT[0, :, 1, 0:3 * DC], gs, reads=[Rmod])
        c.final_wait("sp")
        return nc

    def tile_phase(which):
        es = ExitStack()
        def sb(name, shape, dt):
            return es.enter_context(nc.sbuf_tensor(f'{name}_p{which}', list(shape), dt)).ap()
        xt = sb("xt", [128, DC, 512], F32)
        hT = sb("hT", [128, DC, 512], BF16)
        aT = sb("aT", [128, max(FC, DC), 512], BF16)
        Rx = [Reg(f"x{k}") for k in range(DC)]
        Rh = [Reg(f"h{k}") for k in range(DC)]
        Ra = [Reg(f"a{k}") for k in range(max(FC, DC))]
        NWB = 4
        wbuf = [sb(f"wbuf{i}", [128, cfg.WELEMS], BF16) for i in range(NWB)]
        wreg = [Reg(f"wbuf{i}") for i in range(NWB)]
        tmpf = [sb(f"tmpf{i}", [128, 512], F32) for i in range(4)]
        Rt = [Reg(f"tmpf{i}") for i in range(4)]
        rstd = sb("rstd", [128, 512], F32)
        Rrstd = Reg("rstd")
        stg = [sb(f"stg{i}", [128, 512], BF16) for i in range(4)]
        Rstg = [Reg(f"stg{i}") for i in range(4)]
        cnt = {"tmp": 0, "stg": 0, "w": 0}

        def nxt(kind, n):
            i = cnt[kind] % n
            cnt[kind] += 1
            return i

        class WStream:
            def __init__(self, srcs, pf=3):
                self.srcs = srcs
                self.pf = pf
                self.issued = 0
                self.taken = 0
                self.slots = deque()

            def _issue(self):
                src, n = self.srcs[self.issued]
                bi = nxt("w", NWB)
                c.dma("sp", wbuf[bi][:, 0:n], src, writes=[wreg[bi]])
                self.slots.append(bi)
                self.issued += 1

            def next(self):
                while self.issued < len(self.srcs) and self.issued < self.taken + self.pf:
                    self._issue()
                bi = self.slots.popleft()
                self.taken += 1
                return wbuf[bi], wreg[bi]

        def flat(ap):
            return ap.rearrange("p a b -> p (a b)")

        def norm_mod(i):
            xf = xt.rearrange("p k t -> p (k t)")
            sq = aT[:, 0:DC, :].rearrange("p k t -> p (k t)")
            c.op("act", lambda: nc.scalar.activation(sq, xf, AF.Square), reads=Rx, writes=Ra[0:DC])
            for kc in range(DC):
                last = kc == DC - 1
                c.op("pe", lambda: nc.tensor.matmul(PS[6], ones_bf, aT[:, kc, :], start=(kc == 0), stop=last),
                     reads=[Rc, Ra[kc]], writes=[PR[6]], signal=last)
            c.op("act", lambda: nc.scalar.activation(rstd, PS[6], AF.Ln, scale=1.0 / D, bias=EPS), reads=[PR[6]], writes=[Rrstd])
            c.op("act", lambda: nc.scalar.activation(rstd, rstd, AF.Exp, scale=-0.5), reads=[Rrstd], writes=[Rrstd])
            for kc in range(DC):
                ti = nxt("tmp", 4)
                col = i * DC + kc
                c.op("dve", lambda: nc.vector.scalar_tensor_tensor(tmpf[ti], xt[:, kc, :], gs[:, col:col + 1], rstd,
                                                                   op0=ALU.mult, op1=ALU.mult),
                     reads=[Rx[kc], Rrstd, Rmod], writes=[Rt[ti]])
                c.op("act", lambda: nc.scalar.activation(hT[:, kc, :], tmpf[ti], AF.Identity, bias=shift[:, col:col + 1], scale=1.0),
                     reads=[Rt[ti], Rmod], writes=[Rh[kc]])

        def ffn_srcs(i):
            s = []
            for fg in range(cfg.NFG):
                s.append((flat(WB[f"wg{i}"][fg]), DC * 512))
                s.append((flat(WB[f"wu{i}"][fg]), DC * 512))
            for dc in range(DC):
                s.append((flat(WB[f"wd{i}"][dc]), FC * 128))
            return s

        def ffn(ws, gi):
            for fg in range(cfg.NFG):
                wg, rg = ws.next()
                wu, ru = ws.next()
                wgv = wg[:, 0:DC * 512].rearrange("p (k f) -> p k f", f=512)
                wuv = wu[:, 0:DC * 512].rearrange("p (k f) -> p k f", f=512)
                for j in range(4):
                    fc = fg * 4 + j
                    pg, pu = fc % 2, 2 + fc % 2
                    for kc in range(DC):
                        last = kc == DC - 1
                        c.op("pe", lambda: nc.tensor.matmul(PS[pg], wgv[:, kc, j * 128:(j + 1) * 128], hT[:, kc, :],
                                                            start=(kc == 0), stop=last),
                             reads=[rg, Rh[kc]], writes=[PR[pg]], signal=last)
                    for kc in range(DC):
                        last = kc == DC - 1
                        c.op("pe", lambda: nc.tensor.matmul(PS[pu], wuv[:, kc, j * 128:(j + 1) * 128], hT[:, kc, :],
                                                            start=(kc == 0), stop=last),
                             reads=[ru, Rh[kc]], writes=[PR[pu]], signal=last)
                    ti = nxt("tmp", 4)
                    c.op("act", lambda: nc.scalar.activation(tmpf[ti], PS[pg], AF.Silu), reads=[PR[pg]], writes=[Rt[ti]])
                    c.op("dve", lambda: nc.vector.tensor_tensor(aT[:, fc, :], tmpf[ti], PS[pu], op=ALU.mult),
                         reads=[Rt[ti], PR[pu]], writes=[Ra[fc]])
            for dc in range(DC):
                wd, rd = ws.next()
                wdv = wd[:, 0:FC * 128].rearrange("p (k f) -> p k f", f=128)
                pd = 4 + dc % 2
                for fc in range(FC):
                    last = fc == FC - 1
                    c.op("pe", lambda: nc.tensor.matmul(PS[pd], wdv[:, fc, :], aT[:, fc, :], start=(fc == 0), stop=last),
                         reads=[rd, Ra[fc]], writes=[PR[pd]], signal=last)
                col = gi * DC + dc
                c.op("dve", lambda: nc.vector.scalar_tensor_tensor(xt[:, dc, :], PS[pd], gate[:, col:col + 1], xt[:, dc, :],
                                                                   op0=ALU.mult, op1=ALU.add),
                     reads=[PR[pd], Rmod, Rx[dc]], writes=[Rx[dc]])

        if which == 1:
            halo = sb("halo", [128, 3 * NH, 4], F32)
            Rhalo = [Reg(f"halo{i}") for i in range(3 * NH)]
            c.op("pool", lambda: nc.gpsimd.memset(halo, 0.0), writes=Rhalo)
            cb = [sb(f"cb{i}", [128, 516], F32) for i in range(2)]
            Rcb = [Reg("cb0"), Reg("cb1")]
            wins_sb = sb("wins_sb", [128, DC, 3 * NH], BF16)
            c.dma("sp", wins_sb, WB["wins"], writes=[Rc])
            gsm = sb("gsm", [128, 4, 3 * NH], F32)
            gtmp = [sb(f"gtmp{i}", [128, 4, 3 * NH], F32) for i in range(3)]
            Rg = Reg("gsm")
            nA = sb("nA", [128, NH], F32)
            c.op("act", lambda: nc.scalar.activation(nA, small["alog"], AF.Exp), reads=[Rc], writes=[Rc])
            c.op("dve", lambda: nc.vector.tensor_scalar(nA, nA, -1.0, None, op0=ALU.mult), reads=[Rc], writes=[Rc])
            gT_sb = sb("gT_sb", [3 * NH, 512], F32)
            RgT = Reg("gT")

            srcs = []
            for t in range(NT):
                srcs += ffn_srcs(0)
                for g in range(cfg.NGF):
                    srcs.append((flat(WB["winf"][g]), DC * 512))
                for g in range(cfg.NGV):
                    srcs.append((flat(WB["winv"][g]), DC * cfg.VW))
            ws = WStream(srcs)
            inv_sqrt_d = 1.0 / math.sqrt(HD)

            for t in range(NT):
                own = t >= NTP
                to = t - NTP
                tok0 = t * 512
                for kc in range(DC):
                    c.dma("sp", xt[:, kc, :], xT[t, :, kc, :], writes=[Rx[kc]])
                norm_mod(0)
                ffn(ws, 0)
                if own:
                    for kc in range(DC):
                        c.dma("sp", x1s[to, :, kc, :], xt[:, kc, :], reads=[Rx[kc]])
                if stop_after <= 1:
                    continue
                norm_mod(1)
                if NTP > 0 and t == NTP:
                    c.op("dve", lambda: nc.vector.tensor_scalar(halo.rearrange("p a b -> p (a b)"), halo.rearrange("p a b -> p (a b)"),
                                                                pmask[:, 0:1], None, op0=ALU.mult), reads=[Rc], writes=Rhalo)
                for g in range(cfg.NGF):
                    wf, rf = ws.next()
                    wfv = wf[:, 0:DC * 512].rearrange("p (k f) -> p k f", f=512)
                    for j in range(4):
                        ch = g * 4 + j
                        typ, h = ch // NH, ch % NH
                        need = own or typ in (1, 2, 3, 4)
                        if not need:
                            continue
                        pp = ch % 2
                        for kc in range(DC):
                            last = kc == DC - 1
                            c.op("pe", lambda: nc.tensor.matmul(PS[pp], wfv[:, kc, j * 128:(j + 1) * 128], hT[:, kc, :],
                                                                start=(kc == 0), stop=last),
                                 reads=[rf, Rh[kc]], writes=[PR[pp]], signal=last)
                        si = nxt("stg", 4)
                        if typ == 0:
                            c.op("act", lambda: nc.scalar.activation(stg[si], PS[pp], AF.Copy, scale=inv_sqrt_d),
                                 reads=[PR[pp]], writes=[Rstg[si]])
                            c.dma("sp", QfT[h, :, to * 512:(to + 1) * 512], stg[si], reads=[Rstg[si]])
                        elif typ == 1:
                            c.op("act", lambda: nc.scalar.copy(stg[si], PS[pp]), reads=[PR[pp]], writes=[Rstg[si]])
                            c.dma("sp", KfT[h, :, tok0:tok0 + 512], stg[si], reads=[Rstg[si]])
                        elif typ == 5:
                            c.op("act", lambda: nc.scalar.activation(stg[si], PS[pp], AF.Silu), reads=[PR[pp]], writes=[Rstg[si]])
                            c.dma("sp", ZsT[h, :, to * 512:(to + 1) * 512], stg[si], reads=[Rstg[si]])
                        else:
                            cch = ch - 2 * NH
                            ci = cch % 2
                            cbi, rcb = cb[ci], Rcb[ci]
                            c.op("act", lambda: nc.scalar.copy(cbi[:, 4:516], PS[pp]), reads=[PR[pp]], writes=[rcb])
                            c.op("pool", lambda: nc.gpsimd.tensor_copy(cbi[:, 0:4], halo[:, cch, :]), reads=[Rhalo[cch]], writes=[rcb])
                            c.op("pool", lambda: nc.gpsimd.tensor_copy(halo[:, cch, :], cbi[:, 512:516]), reads=[rcb], writes=[Rhalo[cch]])
                            ti = nxt("tmp", 4)
                            acc = tmpf[ti]
                            c.op("dve", lambda: nc.vector.tensor_scalar(acc, cbi[:, 1:513], small["convw"][:, cch, 0:1], None, op0=ALU.mult),
                                 reads=[rcb, Rc], writes=[Rt[ti]])
                            for k in range(1, 4):
                                c.op("dve", lambda: nc.vector.scalar_tensor_tensor(acc, cbi[:, 1 + k:513 + k], small["convw"][:, cch, k:k + 1], acc,
                                                                                   op0=ALU.mult, op1=ALU.add),
                                     reads=[rcb, Rc, Rt[ti]], writes=[Rt[ti]])
                            c.op("act", lambda: nc.scalar.activation(acc, acc, AF.Silu), reads=[Rt[ti]], writes=[Rt[ti]])
                            if typ == 4:
                                c.op("dve", lambda: nc.vector.tensor_copy(stg[si], acc), reads=[Rt[ti]], writes=[Rstg[si]])
                                c.dma("sp", GvT[h, :, tok0:tok0 + 512], stg[si], reads=[Rstg[si]])
                            else:
                                s2 = nxt("stg", 4)
                                c.op("act", lambda: nc.scalar.activation(stg[s2], acc, AF.Square), reads=[Rt[ti]], writes=[Rstg[s2]])
                                c.op("pe", lambda: nc.tensor.matmul(PS[6], ones_bf, stg[s2], start=True, stop=True),
                                     reads=[Rc, Rstg[s2]], writes=[PR[6]])
                                t2 = nxt("tmp", 4)
                                c.op("act", lambda: nc.scalar.activation(tmpf[t2], PS[6], AF.Ln, bias=EPS), reads=[PR[6]], writes=[Rt[t2]])
                                c.op("act", lambda: nc.scalar.activation(tmpf[t2], tmpf[t2], AF.Exp, scale=-0.5), reads=[Rt[t2]], writes=[Rt[t2]])
                                sc_ = inv_sqrt_d if typ == 2 else 1.0
                                c.op("dve", lambda: nc.vector.scalar_tensor_tensor(stg[si], acc, sc_, tmpf[t2], op0=ALU.mult, op1=ALU.mult),
                                     reads=[Rt[ti], Rt[t2]], writes=[Rstg[si]])
                                dst = GqT if typ == 2 else GkT
                                c.dma("sp", dst[h, :, tok0:tok0 + 512], stg[si], reads=[Rstg[si]])
                for g in range(cfg.NGV):
                    wv, rv = ws.next()
                    wvv = wv[:, 0:DC * cfg.VW].rearrange("p (k f) -> p k f", f=cfg.VW)
                    for blk in range(4):
                        pp = blk % 2
                        for kc in range(DC):
                            last = kc == DC - 1
                            c.op("pe", lambda: nc.tensor.matmul(PS[pp][:, 0:cfg.VW], hT[:, kc, blk * 128:(blk + 1) * 128], wvv[:, kc, :],
                                                                start=(kc == 0), stop=last),
                                 reads=[rv, Rh[kc]], writes=[PR[pp]], signal=last)
                        si = nxt("stg", 4)
                        c.op("act", lambda: nc.scalar.copy(stg[si][:, 0:cfg.VW], PS[pp][:, 0:cfg.VW]), reads=[PR[pp]], writes=[Rstg[si]])
                        c.dma("sp", Vf[tok0 + blk * 128:tok0 + (blk + 1) * 128, g * cfg.VW:(g + 1) * cfg.VW], stg[si][:, 0:cfg.VW],
                              reads=[Rstg[si]])
                G3 = 3 * NH
                for blk in range(4):
                    for kc in range(DC):
                        last = kc == DC - 1
                        c.op("pe", lambda: nc.tensor.matmul(PS[7][:, blk * G3:(blk + 1) * G3], hT[:, kc, blk * 128:(blk + 1) * 128],
                                                            wins_sb[:, kc, :], start=(kc == 0), stop=last),
                             reads=[Rc, Rh[kc]], writes=[PR[7]], signal=last)
                psg = PS[7][:, 0:4 * G3].rearrange("p (b g) -> p b g", g=G3)
                yb, ab, lb = gtmp
                for blk in range(4):
                    c.op("dve", lambda: nc.vector.tensor_tensor(yb[:, blk, 0:NH], psg[:, blk, 0:NH], small["fbias"], op=ALU.add),
                         reads=[PR[7], Rc], writes=[Rg])
                    c.op("dve", lambda: nc.vector.tensor_tensor(yb[:, blk, NH:2 * NH], psg[:, blk, NH:2 * NH], small["dtb"], op=ALU.add),
                         reads=[PR[7], Rc], writes=[Rg])
                c.op("act", lambda: nc.scalar.activation(gsm[:, :, 2 * NH:3 * NH], psg[:, :, 2 * NH:3 * NH], AF.Sigmoid),
                     reads=[PR[7]], writes=[Rg])
                y2 = yb[:, :, 0:2 * NH]
                c.op("act", lambda: nc.scalar.activation(ab[:, :, 0:2 * NH], y2, AF.Abs), reads=[Rg], writes=[Rg])
                c.op("act", lambda: nc.scalar.activation(ab[:, :, 0:2 * NH], ab[:, :, 0:2 * NH], AF.Exp, scale=-1.0), reads=[Rg], writes=[Rg])
                c.op("act", lambda: nc.scalar.activation(lb[:, :, 0:2 * NH], ab[:, :, 0:2 * NH], AF.Ln, bias=1.0), reads=[Rg], writes=[Rg])
                c.op("dve", lambda: nc.vector.scalar_tensor_tensor(gsm[:, :, 0:NH], yb[:, :, 0:NH], 0.0, lb[:, :, 0:NH], op0=ALU.min, op1=ALU.subtract),
                     reads=[Rg], writes=[Rg])
                c.op("dve", lambda: nc.vector.scalar_tensor_tensor(ab[:, :, NH:2 * NH], yb[:, :, NH:2 * NH], 0.0, lb[:, :, NH:2 * NH], op0=ALU.max, op1=ALU.add),
                     reads=[Rg], writes=[Rg])
                for blk in range(4):
                    c.op("dve", lambda: nc.vector.tensor_tensor(gsm[:, blk, NH:2 * NH], ab[:, blk, NH:2 * NH], nA, op=ALU.mult),
                         reads=[Rg, Rc], writes=[Rg])
                c.dma("sp", gat[tok0:tok0 + 512, :].rearrange("(b p) g -> p b g", p=128), gsm, reads=[Rg])
                for blk in range(4):
                    c.op("pe", lambda: nc.tensor.transpose(PS[6][0:G3, blk * 128:(blk + 1) * 128], gsm[:, blk, :], ident_f),
                         reads=[Rg, Rc], writes=[PR[6]], signal=(blk == 3))
                c.op("act", lambda: nc.scalar.copy(gT_sb, PS[6][0:G3, :]), reads=[PR[6]], writes=[RgT])
                c.dma("sp", gatT[:, tok0:tok0 + 512], gT_sb, reads=[RgT])
        else:
            srcs = []
            for to in range(NTO):
                for g in range(cfg.NGO):
                    srcs.append((flat(WB["wout"][g]), DC * 512))
                srcs += ffn_srcs(1)
            ws = WStream(srcs)
            for to in range(NTO):
                for kc in range(DC):
                    c.dma("sp", xt[:, kc, :], x1s[to, :, kc, :], writes=[Rx[kc]])
                    c.dma("sp", hT[:, kc, :], oT[to, :, kc, :], writes=[Rh[kc]])
                for g in range(cfg.NGO):
                    wo, ro = ws.next()
                    wov = wo[:, 0:DC * 512].rearrange("p (k f) -> p k f", f=512)
                    for j in range(4):
                        dc = g * 4 + j
                        pd = 4 + dc % 2
                        for kc in range(DC):
                            last = kc == DC - 1
                            c.op("pe", lambda: nc.tensor.matmul(PS[pd], wov[:, kc, j * 128:(j + 1) * 128], hT[:, kc, :],
                                                                start=(kc == 0), stop=last),
                                 reads=[ro, Rh[kc]], writes=[PR[pd]], signal=last)
                        col = 1 * DC + dc
                        c.op("dve", lambda: nc.vector.scalar_tensor_tensor(xt[:, dc, :], PS[pd], gate[:, col:col + 1], xt[:, dc, :],
                                                                           op0=ALU.mult, op1=ALU.add),
                             reads=[PR[pd], Rmod, Rx[dc]], writes=[Rx[dc]])
                norm_mod(2)
                ffn(ws, 2)
                xf = xt.rearrange("p k t -> p (k t)")
                sq = aT[:, 0:DC, :].rearrange("p k t -> p (k t)")
                c.op("act", lambda: nc.scalar.activation(sq, xf, AF.Square), reads=Rx, writes=Ra[0:DC])
                for kc in range(DC):
                    last = kc == DC - 1
                    c.op("pe", lambda: nc.tensor.matmul(PS[6], ones_bf, aT[:, kc, :], start=(kc == 0), stop=last),
                         reads=[Rc, Ra[kc]], writes=[PR[6]], signal=last)
                c.op("act", lambda: nc.scalar.activation(rstd, PS[6], AF.Ln, scale=1.0 / D, bias=EPS), reads=[PR[6]], writes=[Rrstd])
                c.op("act", lambda: nc.scalar.activation(rstd, rstd, AF.Exp, scale=-0.5), reads=[Rrstd], writes=[Rrstd])
                for kc in range(DC):
                    ti = nxt("tmp", 4)
                    c.op("dve", lambda: nc.vector.scalar_tensor_tensor(tmpf[ti], xt[:, kc, :], small["finalg"][:, kc:kc + 1], rstd,
                                                                       op0=ALU.mult, op1=ALU.mult),
                         reads=[Rx[kc], Rrstd, Rc], writes=[Rt[ti]])
                    c.dma("sp", yT[to, :, kc, :], tmpf[ti], reads=[Rt[ti]])
        c.barrier()
        es.close()

    def phase_gdn():
        es = ExitStack()

        def sb(name, shape, dt):
            return es.enter_context(nc.sbuf_tensor(name + "_gd", list(shape), dt)).ap()
        R0 = Reg("gdnconst")
        maskI = sb("maskI", [128, 128], F32)
        maskU = sb("maskU", [128, 128], F32)
        Mlast = sb("Mlast", [128, 128], F32)
        selc = [sb(f"selc{i}", [128, 128], F32) for i in range(2)]
        ones_f = sb("ones_f", [128, 128], F32)
        for m_, cmp_ in ((maskI, ALU.is_ge), (maskU, ALU.is_gt)):
            c.op("pool", lambda: nc.gpsimd.memset(m_, 1.0), writes=[R0])
            c.op("pool", lambda: nc.gpsimd.affine_select(out=m_, in_=m_, pattern=[[1, 128]], compare_op=cmp_, fill=0.0, base=0,
                                                         channel_multiplier=-1), reads=[R0], writes=[R0])
            c.op("pool", lambda: nc.gpsimd.memset(m_[0:64, 64:128], 0.0), reads=[R0], writes=[R0])
        c.op("pool", lambda: nc.gpsimd.memset(Mlast, 1.0), writes=[R0])
        c.op("pool", lambda: nc.gpsimd.affine_select(out=Mlast.rearrange("p (a b) -> p a b", b=64), in_=Mlast.rearrange("p (a b) -> p a b", b=64),
                                                     pattern=[[-64, 2], [0, 64]], compare_op=ALU.is_equal, fill=0.0, base=-63,
                                                     channel_multiplier=1), reads=[R0], writes=[R0])
        for i in range(2):
            c.op("pool", lambda: nc.gpsimd.memset(selc[i], 1.0), writes=[R0])
            c.op("pool", lambda: nc.gpsimd.affine_select(out=selc[i], in_=selc[i], pattern=[[0, 128]], compare_op=ALU.is_equal, fill=0.0,
                                                         base=-(64 * i + 63), channel_multiplier=1), reads=[R0], writes=[R0])
        c.op("pool", lambda: nc.gpsimd.memset(ones_f, 1.0), writes=[R0])
        KTb = [sb(f"KTb{i}", [128, NH, 128], BF16) for i in range(2)]
        QTb = [sb(f"QTb{i}", [128, NH, 128], BF16) for i in range(2)]
        VTb = [sb(f"VTb{i}", [128, NH, 128], BF16) for i in range(2)]
        ZSb = [sb(f"ZSb{i}", [128, NH, 128], BF16) for i in range(2)]
        gb = [sb(f"gb{i}", [128, 3 * NH], F32) for i in range(2)]
        Rld = [Reg(), Reg()]
        gc = sb("gc", [128, NH], F32); eg = sb("eg", [128, NH], F32); kdec = sb("kdec", [128, NH], F32)
        egl = sb("egl", [128, 2, NH], F32); gcl = sb("gcl", [128, NH], F32); sc1 = sb("sc1", [128, NH], F32)
        Rsm = Reg("gdnsmall")
        ostg = [sb(f"ostg{i}", [128, NH, 128], BF16) for i in range(2)]
        Ros = [Reg(), Reg()]
        H = []
        for h in range(NH):
            d = {}
            for nm, shp, dt in (("Kbg", [128, 128], BF16), ("Kd", [128, 128], BF16), ("Vb", [128, 128], BF16), ("E", [128, 256], F32),
                                ("X", [128, 256], F32), ("PT", [128, 128], F32), ("AT", [128, 128], BF16), ("N", [128, 128], BF16),
                                ("u", [128, 128], F32), ("wT", [128, 128], BF16), ("vn", [128, 128], BF16), ("ot", [128, 128], F32),
                                ("o", [128, 128], F32), ("Sf", [128, 128], F32), ("Sb", [128, 128], BF16), ("dg", [128, 256], F32),
                                ("ss", [128, 1], F32)):
                d[nm] = sb(f"{nm}{h}", shp, dt)
                d["R" + nm] = Reg(f"{nm}{h}")
            H.append(d)
            c.op("pool", lambda: nc.gpsimd.memset(d["Sf"], 0.0), writes=[d["RSf"]])
            c.op("pool", lambda: nc.gpsimd.memset(d["Sb"], 0.0), writes=[d["RSb"]])

        def load(nb):
            bi = nb % 2
            sl = slice(nb * 128, (nb + 1) * 128)
            c.dma("sp", KTb[bi], GkT[:, :, sl].rearrange("h p t -> p h t"), writes=[Rld[bi]])
            c.dma("sp", QTb[bi], GqT[:, :, sl].rearrange("h p t -> p h t"), writes=[Rld[bi]])
            c.dma("sp", VTb[bi], GvT[:, :, sl].rearrange("h p t -> p h t"), writes=[Rld[bi]])
            c.dma("sp", gb[bi], gat[sl, :], writes=[Rld[bi]])
            if nb >= NBP:
                so = slice((nb - NBP) * 128, (nb - NBP + 1) * 128)
                c.dma("sp", ZSb[bi], ZsT[:, :, so].rearrange("h p t -> p h t"), writes=[Rld[bi]])

        load(0)
        for nb in range(NB):
            bi = nb % 2
            own = nb >= NBP
            if nb + 1 < NB:
                load(nb + 1)
            rl = Rld[bi]
            if NBP > 0 and nb == NBP:
                for h in range(NH):
                    d = H[h]
                    c.op("dve", lambda: nc.vector.tensor_scalar(d["Sf"], d["Sf"], pmask[:, 0:1], None, op0=ALU.mult), reads=[Rc], writes=[d["RSf"]])
                    c.op("act", lambda: nc.scalar.copy(d["Sb"], d["Sf"]), reads=[d["RSf"]], writes=[d["RSb"]])
            g_ = gb[bi][:, NH:2 * NH]
            beta = gb[bi][:, 2 * NH:3 * NH]
            c.op("pe", lambda: nc.tensor.matmul(PS[0][:, 0:NH], maskI, g_, start=True, stop=True), reads=[R0, rl], writes=[PR[0]])
            c.op("act", lambda: nc.scalar.copy(gc, PS[0][:, 0:NH]), reads=[PR[0]], writes=[Rsm])
            c.op("pe", lambda: nc.tensor.matmul(PS[0][:, NH:2 * NH], Mlast, gc, start=True, stop=True), reads=[R0, Rsm], writes=[PR[0]], signal=False)
            c.op("pe", lambda: nc.tensor.matmul(PS[0][:, 2 * NH:3 * NH], selc[0], gc, start=True, stop=True), reads=[R0, Rsm], writes=[PR[0]], signal=False)
            c.op("pe", lambda: nc.tensor.matmul(PS[0][:, 3 * NH:4 * NH], selc[1], gc, start=True, stop=True), reads=[R0, Rsm], writes=[PR[0]])
            c.op("act", lambda: nc.scalar.activation(eg, gc, AF.Exp), reads=[Rsm], writes=[Rsm])
            c.op("dve", lambda: nc.vector.tensor_tensor(kdec, PS[0][:, NH:2 * NH], gc, op=ALU.subtract), reads=[PR[0], Rsm], writes=[Rsm])
            c.op("act", lambda: nc.scalar.activation(kdec, kdec, AF.Exp), reads=[Rsm], writes=[Rsm])
            c.op("act", lambda: nc.scalar.activation(egl.rearrange("p a h -> p (a h)"), PS[0][:, 2 * NH:4 * NH], AF.Exp), reads=[PR[0]], writes=[Rsm])
            c.op("act", lambda: nc.scalar.activation(gcl, beta, AF.Ln), reads=[rl], writes=[Rsm])
            c.op("dve", lambda: nc.vector.tensor_tensor(gcl, gcl, gc, op=ALU.add), reads=[Rsm], writes=[Rsm])
            c.op("dve", lambda: nc.vector.tensor_tensor(sc1, beta, eg, op=ALU.mult), reads=[rl, Rsm], writes=[Rsm])
            if GDN_LIMIT <= 0:
                continue
            for h in range(NH):
                d = H[h]
                c.op("pe", lambda: nc.tensor.matmul(PS[h][:, 0:128], KTb[bi][:, h, :], ident_bf, start=True, stop=True), reads=[rl, Rc], writes=[PR[h]], signal=False)
                c.op("pe", lambda: nc.tensor.matmul(PS[h][:, 128:256], VTb[bi][:, h, :], ident_bf, start=True, stop=True), reads=[rl, Rc], writes=[PR[h]])
                if GDN_SUB >= 2:
                    c.op("dve", lambda: nc.vector.tensor_scalar(d["Kbg"], PS[h][:, 0:128], sc1[:, h:h + 1], None, op0=ALU.mult), reads=[PR[h], Rsm], writes=[d["RKbg"]])
                if GDN_SUB >= 3:
                    c.op("dve", lambda: nc.vector.tensor_scalar(d["Kd"], PS[h][:, 0:128], kdec[:, h:h + 1], None, op0=ALU.mult), reads=[PR[h], Rsm], writes=[d["RKd"]])
                if GDN_SUB >= 4:
                    c.op("dve", lambda: nc.vector.tensor_scalar(d["Vb"], PS[h][:, 128:256], beta[:, h:h + 1], None, op0=ALU.mult), reads=[PR[h], rl], writes=[d["RVb"]])
                if GDN_SUB >= 5:
                    c.op("dve", lambda: nc.vector.tensor_scalar(d["dg"][:, 0:128], ident_f, gcl[:, h:h + 1], None, op0=ALU.mult), reads=[Rc, Rsm], writes=[d["Rdg"]])
                    c.op("dve", lambda: nc.vector.tensor_scalar(d["dg"][:, 128:256], ident_f, gc[:, h:h + 1], None, op0=ALU.mult), reads=[Rc, Rsm], writes=[d["Rdg"]])
            if GDN_LIMIT <= 1:
                continue
            for h in range(NH):
                d = H[h]
                c.op("pe", lambda: nc.tensor.matmul(PS[h][:, 128:256], KTb[bi][:, h, :], KTb[bi][:, h, :], start=True, stop=True), reads=[rl], writes=[PR[h]], signal=False)
                c.op("pe", lambda: nc.tensor.matmul(PS[h][:, 256:512], ones_f, d["dg"], start=True, stop=True), reads=[R0, d["Rdg"]], writes=[PR[h]])
                c.op("dve", lambda: nc.vector.tensor_scalar(d["E"], PS[h][:, 256:512], gc[:, h:h + 1], 0.0, op0=ALU.subtract, op1=ALU.min), reads=[PR[h], Rsm], writes=[d["RE"]])
                c.op("act", lambda: nc.scalar.activation(d["E"], d["E"], AF.Exp), reads=[d["RE"]], writes=[d["RE"]])
                c.op("pool", lambda: nc.gpsimd.tensor_tensor(d["E"][:, 0:128], d["E"][:, 0:128], maskU, op=ALU.mult), reads=[d["RE"], R0], writes=[d["RE"]])
                c.op("pool", lambda: nc.gpsimd.tensor_tensor(d["E"][:, 128:256], d["E"][:, 128:256], maskI, op=ALU.mult), reads=[d["RE"], R0], writes=[d["RE"]])
                c.op("dve", lambda: nc.vector.tensor_tensor(d["X"][:, 0:128], d["E"][:, 0:128], PS[h][:, 128:256], op=ALU.mult), reads=[d["RE"], PR[h]], writes=[d["RX"]])
                c.op("dve", lambda: nc.vector.tensor_tensor(d["X"][:, 128:256], ident_f, d["X"][:, 0:128], op=ALU.subtract), reads=[Rc, d["RX"]], writes=[d["RX"]])
            if GDN_LIMIT <= 2:
                continue
            for h in range(NH):
                d = H[h]
                c.op("pe", lambda: nc.tensor.matmul(PS[h][:, 0:128], KTb[bi][:, h, :], QTb[bi][:, h, :], start=True, stop=True), reads=[rl], writes=[PR[h]])
                if GDN_SUB2 >= 2:
                    c.op("dve", lambda: nc.vector.tensor_tensor(d["AT"], d["E"][:, 128:256], PS[h][:, 0:128], op=ALU.mult), reads=[d["RE"], PR[h]], writes=[d["RAT"]])
                if GDN_SUB2 >= 3:
                    c.op("pe", lambda: nc.tensor.matmul(PS[h][:, 128:256], d["X"][:, 0:128], ident_f, start=True, stop=True), reads=[d["RX"], Rc], writes=[PR[h]])
                if GDN_SUB2 >= 4:
                    c.op("act", lambda: nc.scalar.copy(d["PT"], PS[h][:, 128:256]), reads=[PR[h]], writes=[d["RPT"]])
            if GDN_LIMIT <= 3:
                continue
            for k in range(5 if GDN_SUB3 >= 4 else 1):
                for h in range(NH):
                    d = H[h]
                    wid = 128 if k == 0 else 256
                    c.op("pe", lambda: nc.tensor.matmul(PS[h][:, 256:256 + wid], d["PT"], d["X"][:, 0:wid], start=True, stop=True), reads=[d["RPT"], d["RX"]], writes=[PR[h]])
                    if GDN_SUB3 >= 2:
                        c.op("pe", lambda: nc.tensor.matmul(PS[h][:, 128:256], d["X"][:, 0:128], d["PT"], start=True, stop=True), reads=[d["RPT"], d["RX"]], writes=[PR[h]])
                    if GDN_SUB3 >= 3:
                        c.op("dve", lambda: nc.vector.tensor_copy(d["X"][:, 0:128], PS[h][:, 256:384]), reads=[PR[h]], writes=[d["RX"]])
                        if k >= 1:
                            c.op("dve", lambda: nc.vector.tensor_tensor(d["X"][:, 128:256], d["X"][:, 128:256], PS[h][:, 384:512], op=ALU.add), reads=[PR[h], d["RX"]], writes=[d["RX"]])
                        c.op("act", lambda: nc.scalar.copy(d["PT"], PS[h][:, 128:256]), reads=[PR[h]], writes=[d["RPT"]])
            for h in range(NH):
                d = H[h]
                c.op("pe", lambda: nc.tensor.matmul(PS[h][:, 384:512], d["PT"], d["X"][:, 128:256], start=True, stop=True), reads=[d["RPT"], d["RX"]], writes=[PR[h]])
                c.op("dve", lambda: nc.vector.tensor_tensor(d["N"], d["X"][:, 128:256], PS[h][:, 384:512], op=ALU.add), reads=[PR[h], d["RX"]], writes=[d["RN"]])
            if GDN_LIMIT <= 4:
                continue
            for h in range(NH):
                d = H[h]
                c.op("pe", lambda: nc.tensor.matmul(PS[h][:, 0:128], d["N"], d["Vb"], start=True, stop=True), reads=[d["RN"], d["RVb"]], writes=[PR[h]], signal=False)
                c.op("pe", lambda: nc.tensor.matmul(PS[h][:, 128:256], d["Kbg"], d["N"], start=True, stop=True), reads=[d["RN"], d["RKbg"]], writes=[PR[h]])
                c.op("act", lambda: nc.scalar.copy(d["u"], PS[h][:, 0:128]), reads=[PR[h]], writes=[d["Ru"]])
                c.op("dve", lambda: nc.vector.tensor_copy(d["wT"], PS[h][:, 128:256]), reads=[PR[h]], writes=[d["RwT"]])
            if GDN_LIMIT <= 5:
                continue
            for ck in range(2):
                r = slice(64 * ck, 64 * ck + 64)
                for h in range(NH):
                    d = H[h]
                    c.op("pe", lambda: nc.tensor.matmul(PS[h][r, 256:384], d["wT"][:, r], d["Sb"], start=True, stop=True), reads=[d["RwT"], d["RSb"]], writes=[PR[h]], signal=not own)
                    if own:
                        c.op("pe", lambda: nc.tensor.matmul(PS[h][r, 384:512], QTb[bi][:, h, r], d["Sb"], start=True, stop=True), reads=[rl, d["RSb"]], writes=[PR[h]])
                    c.op("dve", lambda: nc.vector.tensor_tensor(d["vn"][r, :], d["u"][r, :], PS[h][r, 256:384], op=ALU.subtract), reads=[d["Ru"], PR[h]], writes=[d["Rvn"]])
                    if own:
                        c.op("dve", lambda: nc.vector.tensor_scalar(d["ot"][r, :], PS[h][r, 384:512], eg[r, h:h + 1], None, op0=ALU.mult), reads=[PR[h], Rsm], writes=[d["Rot"]])
                for h in range(NH):
                    d = H[h]
                    if own:
                        c.op("pe", lambda: nc.tensor.matmul(PS[h][r, 0:128], d["AT"][r, r], d["vn"][r, :], start=True, stop=True), reads=[d["RAT"], d["Rvn"]], writes=[PR[h]], signal=False)
                    c.op("pe", lambda: nc.tensor.matmul(PS[h][:, 128:256], d["Kd"][r, :], d["vn"][r, :], start=True, stop=True), reads=[d["RKd"], d["Rvn"]], writes=[PR[h]])
                    if own:
                        c.op("dve", lambda: nc.vector.tensor_tensor(d["o"][r, :], d["ot"][r, :], PS[h][r, 0:128], op=ALU.add), reads=[d["Rot"], PR[h]], writes=[d["Ro"]])
                    c.op("dve", lambda: nc.vector.scalar_tensor_tensor(d["Sf"], d["Sf"], egl[:, ck, h:h + 1], PS[h][:, 128:256], op0=ALU.mult, op1=ALU.add),
                         reads=[d["RSf"], Rsm, PR[h]], writes=[d["RSf"]])
                    c.op("act", lambda: nc.scalar.copy(d["Sb"], d["Sf"]), reads=[d["RSf"]], writes=[d["RSb"]])
            if GDN_LIMIT <= 6:
                continue
            if own:
                nbo = nb - NBP
                oi = nbo % 2
                for h in range(NH):
                    d = H[h]
                    c.op("act", lambda: nc.scalar.activation(d["ot"], d["o"], AF.Square, accum_out=d["ss"]), reads=[d["Ro"]], writes=[d["Rot"], d["Rss"]])
                    c.op("act", lambda: nc.scalar.activation(d["ss"], d["ss"], AF.Ln, scale=1.0 / 128, bias=EPS), reads=[d["Rss"]], writes=[d["Rss"]])
                    c.op("act", lambda: nc.scalar.activation(d["ss"], d["ss"], AF.Exp, scale=-0.5), reads=[d["Rss"]], writes=[d["Rss"]])
                    c.op("dve", lambda: nc.vector.tensor_scalar(d["o"], d["o"], d["ss"], None, op0=ALU.mult), reads=[d["Rss"], d["Ro"]], writes=[d["Ro"]])
                    c.op("pe", lambda: nc.tensor.matmul(PS[h][:, 256:384], d["o"], ident_f, start=True, stop=True), reads=[d["Ro"], Rc], writes=[PR[h]])
                    c.op("act", lambda: nc.scalar.activation(d["ot"], PS[h][:, 256:384], AF.Identity, scale=small["gdng"][:, 0:1]), reads=[PR[h], Rc], writes=[d["Rot"]])
                    c.op("dve", lambda: nc.vector.tensor_tensor(ostg[oi][:, h, :], d["ot"], ZSb[bi][:, h, :], op=ALU.mult), reads=[d["Rot"], rl], writes=[Ros[oi]])
                c.dma("sp", oT[nbo // 4, :, NH:2 * NH, (nbo % 4) * 128:(nbo % 4 + 1) * 128], ostg[oi], reads=[Ros[oi]])
        c.barrier()
        es.close()

    def phase_fox():
        es = ExitStack()

        def sb(name, shape, dt):
            return es.enter_context(nc.sbuf_tensor(name + "_fx", list(shape), dt)).ap()
        lf = sb("lf", [NH, TT], F32)
        cum = sb("cum", [NH, TT], F32)
        Rcum = Reg("cum")
        ones_c = sb("ones_c", [128, 1], F32)
        c.op("pool", lambda: nc.gpsimd.memset(ones_c, 1.0), writes=[Rcum])
        c.dma("sp", lf, gatT[0:NH, :], writes=[Rcum])
        c.op("dve", lambda: nc.vector.tensor_tensor_scan(cum, ones_c[0:NH, :].to_broadcast([NH, TT]), lf, 0.0, op0=ALU.mult, op1=ALU.add),
             reads=[Rcum], writes=[Rcum])
        negcum = sb("negcum", [128, NB, NH], F32)
        for blk in range(NB):
            c.op("pe", lambda: nc.tensor.transpose(PS[7][:, blk * NH:(blk + 1) * NH], cum[0:NH, blk * 128:(blk + 1) * 128], ident_f[0:NH, 0:NH]),
                 reads=[Rcum, Rc], writes=[PR[7]], signal=(blk == NB - 1))
        c.op("act", lambda: nc.scalar.activation(negcum.rearrange("p b h -> p (b h)"), PS[7][:, 0:NB * NH], AF.Copy, scale=-1.0),
             reads=[PR[7]], writes=[Rcum])
        sel = sb("sel", [NH, NH, 128], F32)
        c.op("pool", lambda: nc.gpsimd.memset(sel, 1.0), writes=[Rcum])
        c.op("pool", lambda: nc.gpsimd.affine_select(out=sel, in_=sel, pattern=[[-1, NH], [0, 128]], compare_op=ALU.is_equal,
                                                     fill=0.0, base=0, channel_multiplier=1), reads=[Rcum], writes=[Rcum])
        crefB = sb("crefB", [128, NH, NBO], F32)
        for h in range(NH):
            c.op("pe", lambda: nc.tensor.matmul(PS[6][:, h * NBO:(h + 1) * NBO], sel[:, h, :], cum[0:NH, TP + 127:TT:128], start=True, stop=True),
                 reads=[Rcum], writes=[PR[6]], signal=(h == NH - 1))
        c.op("act", lambda: nc.scalar.copy(crefB.rearrange("p h b -> p (h b)"), PS[6][:, 0:NH * NBO]), reads=[PR[6]], writes=[Rcum])
        maskT = sb("maskT", [128, 128], BF16)
        c.op("pool", lambda: nc.gpsimd.memset(maskT, 1.0), writes=[Rcum])
        c.op("pool", lambda: nc.gpsimd.affine_select(out=maskT, in_=maskT, pattern=[[1, 128]], compare_op=ALU.is_ge,
                                                     fill=0.0, base=0, channel_multiplier=-1), reads=[Rcum], writes=[Rcum])
        pneg = sb("pneg", [128, 1], F32)
        c.op("dve", lambda: nc.vector.tensor_scalar(pneg, pmask, -1.0, 30000.0, op0=ALU.add, op1=ALU.mult), reads=[Rc], writes=[Rcum])
        Kh = [sb(f"Kh{i}", [128, TT], BF16) for i in range(2)]
        Qh = [sb(f"Qh{i}", [128, TO], BF16) for i in range(2)]
        Va = [sb(f"Va{i}", [128, NB, 130], BF16) for i in range(2)]
        RK = [Reg(), Reg()]; RQ = [Reg(), Reg()]; RV = [Reg(), Reg()]
        for i in range(2):
            c.op("pool", lambda: nc.gpsimd.memset(Va[i][:, :, 128:130], 1.0), writes=[RV[i]])
        bqs = [sb(f"bq{i}", [128, NB], F32) for i in range(2)]
        Rbq = [Reg(), Reg()]
        Pb = [sb(f"Pb{i}", [128, 128], BF16) for i in range(4)]
        RP = [Reg() for _ in range(4)]
        Rslot = [Reg() for _ in range(8)]
        rs_ = [sb(f"rs{i}", [128, 1], F32) for i in range(2)]
        ss_ = [sb(f"ss{i}", [128, 1], F32) for i in range(2)]
        on_ = [sb(f"on{i}", [128, 128], F32) for i in range(2)]
        junk = sb("junk", [128, 128], F32)
        Re = [Reg(), Reg()]
        Rj = Reg()
        ostg = [sb(f"ostg{i}", [128, 512], BF16) for i in range(2)]
        Ros = [Reg(), Reg()]
        Rtr = [Reg() for _ in range(4)]

        def epilogue(h, qb):
            e = qb % 2
            Ob, RO = PS[4 + e], PR[4 + e]
            c.op("dve", lambda: nc.vector.reciprocal(rs_[e], Ob[:, 128:129]), reads=[RO], writes=[Re[e]])
            c.op("dve", lambda: nc.vector.tensor_scalar(on_[e], Ob[:, 0:128], rs_[e], None, op0=ALU.mult), reads=[RO, Re[e]], writes=[Re[e]])
            c.op("act", lambda: nc.scalar.activation(junk, on_[e], AF.Square, accum_out=ss_[e]), reads=[Re[e]], writes=[Re[e], Rj])
            c.op("act", lambda: nc.scalar.activation(ss_[e], ss_[e], AF.Ln, scale=1.0 / 128, bias=EPS), reads=[Re[e]], writes=[Re[e]])
            c.op("act", lambda: nc.scalar.activation(ss_[e], ss_[e], AF.Exp, scale=-0.5), reads=[Re[e]], writes=[Re[e]])
            c.op("dve", lambda: nc.vector.tensor_scalar(on_[e], on_[e], ss_[e], None, op0=ALU.mult), reads=[Re[e]], writes=[Re[e]])
            q4 = qb % 4
            c.op("pe", lambda: nc.tensor.transpose(PS[6][:, q4 * 128:(q4 + 1) * 128], on_[e], ident_f), reads=[Re[e], Rc], writes=[PR[6]])
            oi = (qb // 4) % 2
            c.op("act", lambda: nc.scalar.activation(ostg[oi][:, q4 * 128:(q4 + 1) * 128], PS[6][:, q4 * 128:(q4 + 1) * 128], AF.Identity,
                                                     scale=small["foxg"][:, 0:1]), reads=[PR[6], Rc], writes=[Ros[oi]])
            if q4 == 3:
                c.dma("sp", oT[qb // 4, :, h, :], ostg[oi], reads=[Ros[oi]])

        cnts = {"s": 0, "p": 0}
        for h in range(NH):
            bi = h % 2
            c.dma("sp", Kh[bi], KfT[h], writes=[RK[bi]])
            c.dma("sp", Qh[bi], QfT[h], writes=[RQ[bi]])
            c.dma("sp", Va[bi][:, :, 0:128], Vf[:, h * 128:(h + 1) * 128].rearrange("(b p) d -> p b d", p=128), writes=[RV[bi]])
            units = [(qb, kb) for qb in range(NBO) for kb in range(NBP + qb + 1)]
            LA = 3
            sl_of = {}

            def emit_S(u):
                qb, kb = units[u]
                sl = cnts["s"] % 4
                cnts["s"] += 1
                sl_of[u] = sl
                S = PS[sl][:, 0:128]
                c.op("pe", lambda: nc.tensor.matmul(S, Kh[bi][:, kb * 128:(kb + 1) * 128], Qh[bi][:, qb * 128:(qb + 1) * 128], start=True, stop=True),
                     reads=[RK[bi], RQ[bi]], writes=[Rslot[sl]])
            for u in range(min(LA, len(units))):
                emit_S(u)
            for u, (qb, kb) in enumerate(units):
                nkb = NBP + qb + 1
                e = qb % 2
                if kb == 0:
                    c.op("dve", lambda: nc.vector.tensor_scalar(bqs[e][:, 0:nkb], negcum[:, 0:nkb, h], crefB[:, h, qb:qb + 1], None, op0=ALU.add),
                         reads=[Rcum], writes=[Rbq[e]])
                    if NBP > 0:
                        c.op("dve", lambda: nc.vector.tensor_scalar(bqs[e][:, 0:NBP], bqs[e][:, 0:NBP], pneg[:, 0:1], None, op0=ALU.add),
                             reads=[Rcum, Rbq[e]], writes=[Rbq[e]])
                sl = sl_of.pop(u)
                S = PS[sl][:, 0:128]
                pi = cnts["p"] % 4
                cnts["p"] += 1
                c.op("act", lambda: nc.scalar.activation(Pb[pi], S, AF.Exp, bias=bqs[e][:, kb:kb + 1], scale=1.0),
                     reads=[Rslot[sl], Rbq[e]], writes=[RP[pi]])
                if kb == nkb - 1:
                    c.op("pool", lambda: nc.gpsimd.tensor_tensor(Pb[pi], Pb[pi], maskT, op=ALU.mult), reads=[RP[pi], Rcum], writes=[RP[pi]])
                c.op("pe", lambda: nc.tensor.matmul(PS[4 + e][:, 0:130], Pb[pi], Va[bi][:, kb, :], start=(kb == 0), stop=(kb == nkb - 1)),
                     reads=[RP[pi], RV[bi]], writes=[PR[4 + e]], signal=(kb == nkb - 1))
                if u + LA < len(units):
                    emit_S(u + LA)
                if kb == nkb - 1:
                    epilogue(h, qb)
        c.barrier()
        es.close()

    tile_phase(1)
    if stop_after <= 2:
        for to in range(NTO):
            c.dma("sp", yT[to], x1s[to])
        c.final_wait("sp")
        return nc
    if stop_after >= 3 and not skip_gdn:
        phase_gdn()
    phase_fox()
    if stop_after <= 3:
        c.final_wait("sp")
        return nc
    tile_phase(4)
    c.final_wait("sp")
    return nc


_CACHE = {}


def kernel(**inputs):
    inp = {k: np.asarray(v) for k, v in inputs.items()}
    B, S, D = inp["x"].shape
    half = S // 2
    cfg = Cfg(D=D, TP=half, TO=half)
    key = (D, S)
    if key not in _CACHE:
        _CACHE[key] = build(cfg)
    nc = _CACHE[key]
    hw = host_weights(cfg, inp)
    in_maps = []
    for b in range(B):
        for sidx in range(2):
            m = dict(hw)
            if sidx == 1:
                xs = inp["x"][b]
            else:
                xs = np.concatenate([inp["x"][b, :half], inp["x"][b, :half]], axis=0)
            m["xT"] = host_x(cfg, xs)
            m["cT"] = np.ascontiguousarray(inp["c"][b].reshape(cfg.DC, 128).T)
            m["pmask"] = np.full((128, 1), float(sidx), np.float32)
            in_maps.append(m)
    res = run_bass_kernel_spmd(nc, in_maps, core_ids=list(range(2 * B)))
    out = np.empty((B, S, D), np.float32)
    for b in range(B):
        for sidx in range(2):
            out[b, sidx * half:(sidx + 1) * half] = host_unx(cfg, np.asarray(res.results[2 * b + sidx]["yT"]))
    return out
```

```python
import math
from collections import deque
from contextlib import ExitStack
import numpy as np
import concourse.bass as bass
import concourse.mybir as mybir
from concourse.bass_utils import run_bass_kernel_spmd

F32 = mybir.dt.float32
BF16 = mybir.dt.bfloat16
AF = mybir.ActivationFunctionType
ALU = mybir.AluOpType
EPS = 1e-6
HD = 128
GDN_LIMIT = 99
GDN_SUB = 99
GDN_SUB2 = 99
GDN_SUB3 = 99


class Cfg:
    def __init__(self, D=2048, TP=0, TO=8192, dbg=False):
        self.D = D
        self.FF = 256 * ((8 * D // 3 + 255) // 256)
        self.NH = D // 256
        self.TP, self.TO = TP, TO
        self.TT = TP + TO
        self.DC = D // 128
        self.FC = self.FF // 128
        self.TILE = 512
        self.NT = self.TT // 512
        self.NTP = TP // 512
        self.NTO = TO // 512
        self.NB = self.TT // 128
        self.NBP = TP // 128
        self.NBO = TO // 128
        self.VW = min(512, self.NH * 128)
        self.NGV = self.NH * 128 // self.VW
        self.NGF = 6 * self.NH * 128 // 512
        self.NFG = self.FF // 512
        self.NGO = D // 512
        self.NGA = 9 * D // 512
        self.WELEMS = max(self.DC * 512, self.FC * 128)
        self.dbg = dbg


class Reg:
    __slots__ = ("name", "lw", "rd")

    def __init__(self, name=""):
        self.name = name
        self.lw = None
        self.rd = {}


class Ctx:
    SEM_LIMIT = 30000
    NDMA = 40

    def __init__(self, nc):
        self.nc = nc
        self.eng = {"pe": nc.tensor, "act": nc.scalar, "dve": nc.vector, "pool": nc.gpsimd, "sp": nc.sync}
        self.sems = {}
        self.csem = {}
        self.ccount = {}
        self.nsem = 0
        for e in ("pe", "act", "dve", "pool"):
            self._new_csem(e)
        self.waited = {}
        self.dma_pool = []
        for i in range(self.NDMA):
            self.dma_pool.append([self._alloc(f"dma{i}"), 0])
        self.dma_next = 0
        self.sw_pool = []
        for i in range(8):
            self.sw_pool.append([self._alloc(f"swdma{i}"), 0])
        self.sw_next = 0
        self.n_inst = 0
        self.n_wait = 0

    def _alloc(self, name):
        self.nsem += 1
        key = f"{name}_{self.nsem}"
        self.sems[key] = self.nc.alloc_semaphore(key)
        return key

    def _new_csem(self, e):
        self.csem[e] = self._alloc(f"c_{e}")
        self.ccount[e] = 0

    def _wait(self, e, deps):
        for (k, v) in deps:
            if e == "pe" and k.startswith("c_pe"):
                continue
            if self.waited.get((e, k), 0) >= v:
                continue
            self.eng[e].wait_ge(self.sems[k], v)
            self.waited[(e, k)] = v
            self.n_wait += 1

    @staticmethod
    def _deps(reads, writes):
        deps = []
        for r in reads:
            if r.lw is not None:
                deps.append(r.lw)
        for w in writes:
            if w.lw is not None:
                deps.append(w.lw)
            deps.extend(w.rd.items())
        return deps

    @staticmethod
    def _mark(tok, reads, writes):
        k, v = tok
        for r in reads:
            if r.rd.get(k, 0) < v:
                r.rd[k] = v
        for w in writes:
            w.lw = tok
            w.rd = {}

    def op(self, e, fn, reads=(), writes=(), signal=True):
        pr = [r for r in reads if r.name.startswith("ps")]
        if pr:
            reads = [r for r in reads if not r.name.startswith("ps")]
            writes = list(writes) + pr
        self._wait(e, self._deps(reads, writes))
        ins = fn()
        self.n_inst += 1
        if signal:
            self.ccount[e] += 1
            ins.then_inc(self.sems[self.csem[e]], 1)
            self._mark((self.csem[e], self.ccount[e]), reads, writes)
            if self.ccount[e] >= self.SEM_LIMIT:
                self._new_csem(e)
        else:
            self._mark((self.csem[e], self.ccount[e] + 1), reads, writes)
        return ins

    def dma(self, q, out, in_, reads=(), writes=(), **kw):
        if q == "pool":
            slot = self.sw_pool[self.sw_next]
            self.sw_next = (self.sw_next + 1) % len(self.sw_pool)
        else:
            slot = self.dma_pool[self.dma_next]
            self.dma_next = (self.dma_next + 1) % self.NDMA
        k, tot = slot
        deps = self._deps(reads, writes)
        if tot > 0:
            deps.append((k, tot))
        self._wait(q, deps)
        ins = self.eng[q].dma_start(out=out, in_=in_, **kw)
        ins.then_inc(self.sems[k], 16)
        slot[1] = tot + 16
        self.n_inst += 1
        self._mark((k, tot + 16), reads, writes)
        return ins

    def _all_tokens(self):
        toks = []
        for e in ("pe", "act", "dve", "pool"):
            if self.ccount[e] > 0:
                toks.append((self.csem[e], self.ccount[e]))
        for k, tot in self.dma_pool + self.sw_pool:
            if tot > 0:
                toks.append((k, tot))
        return toks

    def barrier(self):
        toks = self._all_tokens()
        for e in ("pe", "act", "dve", "pool", "sp"):
            for (k, v) in toks:
                if self.waited.get((e, k), 0) >= v:
                    continue
                self.eng[e].wait_ge(self.sems[k], v)
                self.waited[(e, k)] = v
                self.n_wait += 1

    def final_wait(self, e="sp"):
        for (k, v) in self._all_tokens():
            if self.waited.get((e, k), 0) >= v:
                continue
            self.eng[e].wait_ge(self.sems[k], v)
            self.waited[(e, k)] = v


def _fm_groups(w, gw):
    K, Fd = w.shape
    return np.ascontiguousarray(w.reshape(K // 128, 128, Fd // gw, gw).transpose(2, 1, 0, 3))


def _vec_fm(v):
    return np.ascontiguousarray(v.reshape(-1, 128).T)


def host_weights(cfg, inp):
    NH = cfg.NH
    FW = NH * 128
    w = {}
    w["adaw"] = _fm_groups(inp["ada_w"][0], 512)
    w["adab"] = _vec_fm(inp["ada_b"][0])
    w["normg"] = _vec_fm(inp["norm_g"][0].reshape(-1))
    w["finalg"] = _vec_fm(inp["final_norm"])
    for i in range(2):
        w[f"wg{i}"] = _fm_groups(inp["ffn_w_gate"][0, i], 512)
        w[f"wu{i}"] = _fm_groups(inp["ffn_w_up"][0, i], 512)
        w[f"wd{i}"] = _fm_groups(inp["ffn_w_down"][0, i], 128)
    win = inp["w_in"][0]
    o = 0
    q_f = win[:, o:o + FW]; o += FW
    k_f = win[:, o:o + FW]; o += FW
    v_f = win[:, o:o + FW]; o += FW
    f_f = win[:, o:o + NH]; o += NH
    qkv = win[:, o:o + 3 * FW]; o += 3 * FW
    a_g = win[:, o:o + NH]; o += NH
    b_g = win[:, o:o + NH]; o += NH
    z_g = win[:, o:o + FW]; o += FW
    assert o == win.shape[1]
    w["winf"] = _fm_groups(np.concatenate([q_f, k_f, qkv, z_g], axis=1), 512)
    w["winv"] = _fm_groups(v_f, cfg.VW)
    sm = np.concatenate([f_f, a_g, b_g], axis=1)
    w["wins"] = np.ascontiguousarray(sm.reshape(cfg.DC, 128, 3 * NH).transpose(1, 0, 2))
    w["wout"] = _fm_groups(inp["w_out"][0], 512)
    bc = lambda v: np.ascontiguousarray(np.broadcast_to(v[None, :], (128, v.shape[0])))
    w["fbias"] = bc(inp["fox_f_bias"][0])
    w["alog"] = bc(inp["gdn_A_log"][0])
    w["dtb"] = bc(inp["gdn_dt_bias"][0])
    cw = inp["gdn_conv"][0]
    w["convw"] = np.ascontiguousarray(cw.reshape(4, 3 * NH, 128).transpose(2, 1, 0))
    w["foxg"] = np.ascontiguousarray(inp["fox_out_norm"][0].reshape(128, 1))
    w["gdng"] = np.ascontiguousarray(inp["gdn_out_norm"][0].reshape(128, 1))
    return {k: np.ascontiguousarray(v, dtype=np.float32) for k, v in w.items()}


def host_x(cfg, xs):
    return np.ascontiguousarray(xs.reshape(cfg.NT, 512, cfg.DC, 128).transpose(0, 3, 2, 1))


def host_unx(cfg, yT):
    return np.ascontiguousarray(yT.transpose(0, 3, 2, 1).reshape(cfg.TO, cfg.D))


WSHAPES = None


def weight_shapes(cfg):
    D, DC, FC, NH = cfg.D, cfg.DC, cfg.FC, cfg.NH
    s = {
        "adaw": [cfg.NGA, 128, DC, 512], "adab": [128, 9 * DC], "normg": [128, 3 * DC], "finalg": [128, DC],
        "winf": [cfg.NGF, 128, DC, 512], "winv": [cfg.NGV, 128, DC, cfg.VW], "wins": [128, DC, 3 * NH],
        "wout": [cfg.NGO, 128, DC, 512], "fbias": [128, NH], "alog": [128, NH], "dtb": [128, NH],
        "convw": [128, 3 * NH, 4], "foxg": [128, 1], "gdng": [128, 1],
    }
    for i in range(2):
        s[f"wg{i}"] = [cfg.NFG, 128, DC, 512]
        s[f"wu{i}"] = [cfg.NFG, 128, DC, 512]
        s[f"wd{i}"] = [DC, 128, FC, 128]
    return s


CAST = ["wg0", "wu0", "wd0", "wg1", "wu1", "wd1", "winf", "winv", "wins", "wout"]


def build(cfg, stop_after=99, skip_gdn=False):
    nc = bass.Bass("TRN2", target_bir_lowering=False)
    c = Ctx(nc)
    D, DC, FC, NH, NT, NTP, NTO = cfg.D, cfg.DC, cfg.FC, cfg.NH, cfg.NT, cfg.NTP, cfg.NTO
    TT, TO, TP, NB, NBP, NBO = cfg.TT, cfg.TO, cfg.TP, cfg.NB, cfg.NBP, cfg.NBO
    FW = NH * 128

    def dram(name, shape, dt, kind="Internal"):
        return nc.dram_tensor(name, list(shape), dt, kind=kind).ap()

    skind = "ExternalOutput" if cfg.dbg else "Internal"
    xT = dram("xT", [NT, 128, DC, 512], F32, "ExternalInput")
    cT = dram("cT", [128, DC], F32, "ExternalInput")
    pm = dram("pmask", [128, 1], F32, "ExternalInput")
    W = {k: dram(k, s, F32, "ExternalInput") for k, s in weight_shapes(cfg).items()}
    yT = dram("yT", [NTO, 128, DC, 512], F32, "ExternalOutput")
    WB = {k: dram(k + "_bf", weight_shapes(cfg)[k], BF16) for k in CAST}
    x1s = dram("x1s", [NTO, 128, DC, 512], F32, skind)
    QfT = dram("QfT", [NH, 128, TO], BF16, skind)
    KfT = dram("KfT", [NH, 128, TT], BF16, skind)
    Vf = dram("Vf", [TT, FW], BF16, skind)
    GqT = dram("GqT", [NH, 128, TT], BF16, skind)
    GkT = dram("GkT", [NH, 128, TT], BF16, skind)
    GvT = dram("GvT", [NH, 128, TT], BF16, skind)
    ZsT = dram("ZsT", [NH, 128, TO], BF16, skind)
    gat = dram("gat", [TT, 3 * NH], F32, skind)
    gatT = dram("gatT", [3 * NH, TT], F32, skind)
    oT = dram("oT", [NTO, 128, DC, 512], BF16, skind)

    def sb(name, shape, dt):
        return nc.alloc_sbuf_tensor(name, list(shape), dt).ap()

    ones_bf = sb("ones_bf", [128, 128], BF16)
    ident_f = sb("ident_f", [128, 128], F32)
    ident_bf = sb("ident_bf", [128, 128], BF16)
    Rc = Reg("const")
    c.op("pool", lambda: nc.gpsimd.memset(ones_bf, 1.0), writes=[Rc])
    c.op("pool", lambda: nc.gpsimd.memset(ident_f, 1.0), writes=[Rc])
    c.op("pool", lambda: nc.gpsimd.affine_select(out=ident_f, in_=ident_f, pattern=[[-1, 128]], compare_op=ALU.is_equal,
                                                 fill=0.0, base=0, channel_multiplier=1), reads=[Rc], writes=[Rc])
    c.op("pool", lambda: nc.gpsimd.tensor_copy(ident_bf, ident_f), reads=[Rc], writes=[Rc])
    small = {}
    for k in ["adab", "normg", "finalg", "fbias", "alog", "dtb", "convw", "foxg", "gdng"]:
        small[k] = sb("s_" + k, weight_shapes(cfg)[k], F32)
        c.dma("sp", small[k], W[k], writes=[Rc])
    cond = sb("cond", [128, DC], F32)
    c.dma("sp", cond, cT, writes=[Rc])
    pmask = sb("pmask_sb", [128, 1], F32)
    c.dma("sp", pmask, pm, writes=[Rc])

    for k in CAST:
        shp = weight_shapes(cfg)[k]
        n = int(np.prod(shp))
        letters = "abcd"[:len(shp)]
        pat = " ".join(letters)
        cw = 2048
        while n % cw:
            cw //= 2
        src = W[k].rearrange(f"{pat} -> ({pat})").rearrange("(r c) -> r c", c=cw)
        dst = WB[k].rearrange(f"{pat} -> ({pat})").rearrange("(r c) -> r c", c=cw)
        rows = n // cw
        for r0 in range(0, rows, 4096):
            r1 = min(rows, r0 + 4096)
            c.dma("pool", dst[r0:r1, :], src[r0:r1, :])

    PS = [nc.alloc_psum_tensor(f"ps{i}", [128, 512], F32).ap() for i in range(8)]
    PR = [Reg(f"ps{i}") for i in range(8)]

    mod = sb("mod", [128, 9 * DC], F32)
    Rmod = Reg("mod")
    gs = sb("gs", [128, 3 * DC], F32)
    gate = sb("gate", [128, 3 * DC], F32)
    shift = sb("shift", [128, 3 * DC], F32)
    c.op("act", lambda: nc.scalar.activation(cond, cond, AF.Silu), reads=[Rc], writes=[Rc])
    es0 = ExitStack()
    adabuf = [es0.enter_context(nc.sbuf_tensor(f"adabuf{i}", [128, DC, 512], F32)).ap() for i in range(2)]
    adar = [Reg("adabuf0"), Reg("adabuf1")]
    for g in range(cfg.NGA):
        bi = g % 2
        c.dma("sp", adabuf[bi], W["adaw"][g], writes=[adar[bi]])
        for j in range(4):
            col = g * 4 + j
            for kc in range(DC):
                last = kc == DC - 1
                c.op("pe", lambda: nc.tensor.matmul(PS[7][:, col:col + 1], adabuf[bi][:, kc, j * 128:(j + 1) * 128],
                                                    cond[:, kc:kc + 1], start=(kc == 0), stop=last),
                     reads=[adar[bi], Rc], writes=[PR[7]], signal=last)
    c.op("dve", lambda: nc.vector.tensor_tensor(mod, PS[7][:, 0:9 * DC], small["adab"], op=ALU.add),
         reads=[PR[7], Rc], writes=[Rmod])
    for i in range(3):
        sh = mod[:, (3 * i) * DC:(3 * i + 1) * DC]
        sc = mod[:, (3 * i + 1) * DC:(3 * i + 2) * DC]
        gt = mod[:, (3 * i + 2) * DC:(3 * i + 3) * DC]
        sl = slice(i * DC, (i + 1) * DC)
        c.op("dve", lambda: nc.vector.scalar_tensor_tensor(gs[:, sl], sc, 1.0, small["normg"][:, sl], op0=ALU.add, op1=ALU.mult),
             reads=[Rmod, Rc], writes=[Rmod])
        c.op("dve", lambda: nc.vector.tensor_copy(shift[:, sl], sh), reads=[Rmod], writes=[Rmod])
        mw = 1.0 if i == 1 else 0.5
        c.op("dve", lambda: nc.vector.tensor_scalar(gate[:, sl], gt, mw, None, op0=ALU.mult), reads=[Rmod], writes=[Rmod])
    c.barrier()
    es0.close()
    if stop_after <= 0:
        c.dma("sp", yT[0, :, 0, 0:9 * DC], mod, reads=[Rmod])
        c.dma("sp", yT[0, :, 1, 0:3 * DC], gs, reads=[Rmod])
        c.final_wait("sp")
        return nc

    def tile_phase(which):
        es = ExitStack()
        def sb(name, shape, dt):
            return es.enter_context(nc.sbuf_tensor(f'{name}_p{which}', list(shape), dt)).ap()
        xt = sb("xt", [128, DC, 512], F32)
        hT = sb("hT", [128, DC, 512], BF16)
        aT = sb("aT", [128, max(FC, DC), 512], BF16)
        Rx = [Reg(f"x{k}") for k in range(DC)]
        Rh = [Reg(f"h{k}") for k in range(DC)]
        Ra = [Reg(f"a{k}") for k in range(max(FC, DC))]
        NWB = 4
        wbuf = [sb(f"wbuf{i}", [128, cfg.WELEMS], BF16) for i in range(NWB)]
        wreg = [Reg(f"wbuf{i}") for i in range(NWB)]
        tmpf = [sb(f"tmpf{i}", [128, 512], F32) for i in range(4)]
        Rt = [Reg(f"tmpf{i}") for i in range(4)]
        rstd = sb("rstd", [128, 512], F32)
        Rrstd = Reg("rstd")
        stg = [sb(f"stg{i}", [128, 512], BF16) for i in range(4)]
        Rstg = [Reg(f"stg{i}") for i in range(4)]
        cnt = {"tmp": 0, "stg": 0, "w": 0}

        def nxt(kind, n):
            i = cnt[kind] % n
            cnt[kind] += 1
            return i

        class WStream:
            def __init__(self, srcs, pf=3):
                self.srcs = srcs
                self.pf = pf
                self.issued = 0
                self.taken = 0
                self.slots = deque()

            def _issue(self):
                src, n = self.srcs[self.issued]
                bi = nxt("w", NWB)
                c.dma("sp", wbuf[bi][:, 0:n], src, writes=[wreg[bi]])
                self.slots.append(bi)
                self.issued += 1

            def next(self):
                while self.issued < len(self.srcs) and self.issued < self.taken + self.pf:
                    self._issue()
                bi = self.slots.popleft()
                self.taken += 1
                return wbuf[bi], wreg[bi]

        def flat(ap):
            return ap.rearrange("p a b -> p (a b)")

        def norm_mod(i):
            xf = xt.rearrange("p k t -> p (k t)")
            sq = aT[:, 0:DC, :].rearrange("p k t -> p (k t)")
            c.op("act", lambda: nc.scalar.activation(sq, xf, AF.Square), reads=Rx, writes=Ra[0:DC])
            for kc in range(DC):
                last = kc == DC - 1
                c.op("pe", lambda: nc.tensor.matmul(PS[6], ones_bf, aT[:, kc, :], start=(kc == 0), stop=last),
                     reads=[Rc, Ra[kc]], writes=[PR[6]], signal=last)
            c.op("act", lambda: nc.scalar.activation(rstd, PS[6], AF.Ln, scale=1.0 / D, bias=EPS), reads=[PR[6]], writes=[Rrstd])
            c.op("act", lambda: nc.scalar.activation(rstd, rstd, AF.Exp, scale=-0.5), reads=[Rrstd], writes=[Rrstd])
            for kc in range(DC):
                ti = nxt("tmp", 4)
                col = i * DC + kc
                c.op("dve", lambda: nc.vector.scalar_tensor_tensor(tmpf[ti], xt[:, kc, :], gs[:, col:col + 1], rstd,
                                                                   op0=ALU.mult, op1=ALU.mult),
                     reads=[Rx[kc], Rrstd, Rmod], writes=[Rt[ti]])
                c.op("act", lambda: nc.scalar.activation(hT[:, kc, :], tmpf[ti], AF.Identity, bias=shift[:, col:col + 1], scale=1.0),
                     reads=[Rt[ti], Rmod], writes=[Rh[kc]])

        def ffn_srcs(i):
            s = []
            for fg in range(cfg.NFG):
                s.append((flat(WB[f"wg{i}"][fg]), DC * 512))
                s.append((flat(WB[f"wu{i}"][fg]), DC * 512))
            for dc in range(DC):
                s.append((flat(WB[f"wd{i}"][dc]), FC * 128))
            return s

        def ffn(ws, gi):
            for fg in range(cfg.NFG):
                wg, rg = ws.next()
                wu, ru = ws.next()
                wgv = wg[:, 0:DC * 512].rearrange("p (k f) -> p k f", f=512)
                wuv = wu[:, 0:DC * 512].rearrange("p (k f) -> p k f", f=512)
                for j in range(4):
                    fc = fg * 4 + j
                    pg, pu = fc % 2, 2 + fc % 2
                    for kc in range(DC):
                        last = kc == DC - 1
                        c.op("pe", lambda: nc.tensor.matmul(PS[pg], wgv[:, kc, j * 128:(j + 1) * 128], hT[:, kc, :],
                                                            start=(kc == 0), stop=last),
                             reads=[rg, Rh[kc]], writes=[PR[pg]], signal=last)
                    for kc in range(DC):
                        last = kc == DC - 1
                        c.op("pe", lambda: nc.tensor.matmul(PS[pu], wuv[:, kc, j * 128:(j + 1) * 128], hT[:, kc, :],
                                                            start=(kc == 0), stop=last),
                             reads=[ru, Rh[kc]], writes=[PR[pu]], signal=last)
                    ti = nxt("tmp", 4)
                    c.op("act", lambda: nc.scalar.activation(tmpf[ti], PS[pg], AF.Silu), reads=[PR[pg]], writes=[Rt[ti]])
                    c.op("dve", lambda: nc.vector.tensor_tensor(aT[:, fc, :], tmpf[ti], PS[pu], op=ALU.mult),
                         reads=[Rt[ti], PR[pu]], writes=[Ra[fc]])
            for dc in range(DC):
                wd, rd = ws.next()
                wdv = wd[:, 0:FC * 128].rearrange("p (k f) -> p k f", f=128)
                pd = 4 + dc % 2
                for fc in range(FC):
                    last = fc == FC - 1
                    c.op("pe", lambda: nc.tensor.matmul(PS[pd], wdv[:, fc, :], aT[:, fc, :], start=(fc == 0), stop=last),
                         reads=[rd, Ra[fc]], writes=[PR[pd]], signal=last)
                col = gi * DC + dc
                c.op("dve", lambda: nc.vector.scalar_tensor_tensor(xt[:, dc, :], PS[pd], gate[:, col:col + 1], xt[:, dc, :],
                                                                   op0=ALU.mult, op1=ALU.add),
                     reads=[PR[pd], Rmod, Rx[dc]], writes=[Rx[dc]])

        if which == 1:
            halo = sb("halo", [128, 3 * NH, 4], F32)
            Rhalo = [Reg(f"halo{i}") for i in range(3 * NH)]
            c.op("pool", lambda: nc.gpsimd.memset(halo, 0.0), writes=Rhalo)
            cb = [sb(f"cb{i}", [128, 516], F32) for i in range(2)]
            Rcb = [Reg("cb0"), Reg("cb1")]
            wins_sb = sb("wins_sb", [128, DC, 3 * NH], BF16)
            c.dma("sp", wins_sb, WB["wins"], writes=[Rc])
            gsm = sb("gsm", [128, 4, 3 * NH], F32)
            gtmp = [sb(f"gtmp{i}", [128, 4, 3 * NH], F32) for i in range(3)]
            Rg = Reg("gsm")
            nA = sb("nA", [128, NH], F32)
            c.op("act", lambda: nc.scalar.activation(nA, small["alog"], AF.Exp), reads=[Rc], writes=[Rc])
            c.op("dve", lambda: nc.vector.tensor_scalar(nA, nA, -1.0, None, op0=ALU.mult), reads=[Rc], writes=[Rc])
            gT_sb = sb("gT_sb", [3 * NH, 512], F32)
            RgT = Reg("gT")

            srcs = []
            for t in range(NT):
                srcs += ffn_srcs(0)
                for g in range(cfg.NGF):
                    srcs.append((flat(WB["winf"][g]), DC * 512))
                for g in range(cfg.NGV):
                    srcs.append((flat(WB["winv"][g]), DC * cfg.VW))
            ws = WStream(srcs)
            inv_sqrt_d = 1.0 / math.sqrt(HD)

            for t in range(NT):
                own = t >= NTP
                to = t - NTP
                tok0 = t * 512
                for kc in range(DC):
                    c.dma("sp", xt[:, kc, :], xT[t, :, kc, :], writes=[Rx[kc]])
                norm_mod(0)
                ffn(ws, 0)
                if own:
                    for kc in range(DC):
                        c.dma("sp", x1s[to, :, kc, :], xt[:, kc, :], reads=[Rx[kc]])
                if stop_after <= 1:
                    continue
                norm_mod(1)
                if NTP > 0 and t == NTP:
                    c.op("dve", lambda: nc.vector.tensor_scalar(halo.rearrange("p a b -> p (a b)"), halo.rearrange("p a b -> p (a b)"),
                                                                pmask[:, 0:1], None, op0=ALU.mult), reads=[Rc], writes=Rhalo)
                for g in range(cfg.NGF):
                    wf, rf = ws.next()
                    wfv = wf[:, 0:DC * 512].rearrange("p (k f) -> p k f", f=512)
                    for j in range(4):
                        ch = g * 4 + j
                        typ, h = ch // NH, ch % NH
                        need = own or typ in (1, 2, 3, 4)
                        if not need:
                            continue
                        pp = ch % 2
                        for kc in range(DC):
                            last = kc == DC - 1
                            c.op("pe", lambda: nc.tensor.matmul(PS[pp], wfv[:, kc, j * 128:(j + 1) * 128], hT[:, kc, :],
                                                                start=(kc == 0), stop=last),
                                 reads=[rf, Rh[kc]], writes=[PR[pp]], signal=last)
                        si = nxt("stg", 4)
                        if typ == 0:
                            c.op("act", lambda: nc.scalar.activation(stg[si], PS[pp], AF.Copy, scale=inv_sqrt_d),
                                 reads=[PR[pp]], writes=[Rstg[si]])
                            c.dma("sp", QfT[h, :, to * 512:(to + 1) * 512], stg[si], reads=[Rstg[si]])
                        elif typ == 1:
                            c.op("act", lambda: nc.scalar.copy(stg[si], PS[pp]), reads=[PR[pp]], writes=[Rstg[si]])
                            c.dma("sp", KfT[h, :, tok0:tok0 + 512], stg[si], reads=[Rstg[si]])
                        elif typ == 5:
                            c.op("act", lambda: nc.scalar.activation(stg[si], PS[pp], AF.Silu), reads=[PR[pp]], writes=[Rstg[si]])
                            c.dma("sp", ZsT[h, :, to * 512:(to + 1) * 512], stg[si], reads=[Rstg[si]])
                        else:
                            cch = ch - 2 * NH
                            ci = cch % 2
                            cbi, rcb = cb[ci], Rcb[ci]
                            c.op("act", lambda: nc.scalar.copy(cbi[:, 4:516], PS[pp]), reads=[PR[pp]], writes=[rcb])
                            c.op("pool", lambda: nc.gpsimd.tensor_copy(cbi[:, 0:4], halo[:, cch, :]), reads=[Rhalo[cch]], writes=[rcb])
                            c.op("pool", lambda: nc.gpsimd.tensor_copy(halo[:, cch, :], cbi[:, 512:516]), reads=[rcb], writes=[Rhalo[cch]])
                            ti = nxt("tmp", 4)
                            acc = tmpf[ti]
                            c.op("dve", lambda: nc.vector.tensor_scalar(acc, cbi[:, 1:513], small["convw"][:, cch, 0:1], None, op0=ALU.mult),
                                 reads=[rcb, Rc], writes=[Rt[ti]])
                            for k in range(1, 4):
                                c.op("dve", lambda: nc.vector.scalar_tensor_tensor(acc, cbi[:, 1 + k:513 + k], small["convw"][:, cch, k:k + 1], acc,
                                                                                   op0=ALU.mult, op1=ALU.add),
                                     reads=[rcb, Rc, Rt[ti]], writes=[Rt[ti]])
                            c.op("act", lambda: nc.scalar.activation(acc, acc, AF.Silu), reads=[Rt[ti]], writes=[Rt[ti]])
                            if typ == 4:
                                c.op("dve", lambda: nc.vector.tensor_copy(stg[si], acc), reads=[Rt[ti]], writes=[Rstg[si]])
                                c.dma("sp", GvT[h, :, tok0:tok0 + 512], stg[si], reads=[Rstg[si]])
                            else:
                                s2 = nxt("stg", 4)
                                c.op("act", lambda: nc.scalar.activation(stg[s2], acc, AF.Square), reads=[Rt[ti]], writes=[Rstg[s2]])
                                c.op("pe", lambda: nc.tensor.matmul(PS[6], ones_bf, stg[s2], start=True, stop=True),
                                     reads=[Rc, Rstg[s2]], writes=[PR[6]])
                                t2 = nxt("tmp", 4)
                                c.op("act", lambda: nc.scalar.activation(tmpf[t2], PS[6], AF.Ln, bias=EPS), reads=[PR[6]], writes=[Rt[t2]])
                                c.op("act", lambda: nc.scalar.activation(tmpf[t2], tmpf[t2], AF.Exp, scale=-0.5), reads=[Rt[t2]], writes=[Rt[t2]])
                                sc_ = inv_sqrt_d if typ == 2 else 1.0
                                c.op("dve", lambda: nc.vector.scalar_tensor_tensor(stg[si], acc, sc_, tmpf[t2], op0=ALU.mult, op1=ALU.mult),
                                     reads=[Rt[ti], Rt[t2]], writes=[Rstg[si]])
                                dst = GqT if typ == 2 else GkT
                                c.dma("sp", dst[h, :, tok0:tok0 + 512], stg[si], reads=[Rstg[si]])
                for g in range(cfg.NGV):
                    wv, rv = ws.next()
                    wvv = wv[:, 0:DC * cfg.VW].rearrange("p (k f) -> p k f", f=cfg.VW)
                    for blk in range(4):
                        pp = blk % 2
                        for kc in range(DC):
                            last = kc == DC - 1
                            c.op("pe", lambda: nc.tensor.matmul(PS[pp][:, 0:cfg.VW], hT[:, kc, blk * 128:(blk + 1) * 128], wvv[:, kc, :],
                                                                start=(kc == 0), stop=last),
                                 reads=[rv, Rh[kc]], writes=[PR[pp]], signal=last)
                        si = nxt("stg", 4)
                        c.op("act", lambda: nc.scalar.copy(stg[si][:, 0:cfg.VW], PS[pp][:, 0:cfg.VW]), reads=[PR[pp]], writes=[Rstg[si]])
                        c.dma("sp", Vf[tok0 + blk * 128:tok0 + (blk + 1) * 128, g * cfg.VW:(g + 1) * cfg.VW], stg[si][:, 0:cfg.VW],
                              reads=[Rstg[si]])
                G3 = 3 * NH
                for blk in range(4):
                    for kc in range(DC):
                        last = kc == DC - 1
                        c.op("pe", lambda: nc.tensor.matmul(PS[7][:, blk * G3:(blk + 1) * G3], hT[:, kc, blk * 128:(blk + 1) * 128],
                                                            wins_sb[:, kc, :], start=(kc == 0), stop=last),
                             reads=[Rc, Rh[kc]], writes=[PR[7]], signal=last)
                psg = PS[7][:, 0:4 * G3].rearrange("p (b g) -> p b g", g=G3)
                yb, ab, lb = gtmp
                for blk in range(4):
                    c.op("dve", lambda: nc.vector.tensor_tensor(yb[:, blk, 0:NH], psg[:, blk, 0:NH], small["fbias"], op=ALU.add),
                         reads=[PR[7], Rc], writes=[Rg])
                    c.op("dve", lambda: nc.vector.tensor_tensor(yb[:, blk, NH:2 * NH], psg[:, blk, NH:2 * NH], small["dtb"], op=ALU.add),
                         reads=[PR[7], Rc], writes=[Rg])
                c.op("act", lambda: nc.scalar.activation(gsm[:, :, 2 * NH:3 * NH], psg[:, :, 2 * NH:3 * NH], AF.Sigmoid),
                     reads=[PR[7]], writes=[Rg])
                y2 = yb[:, :, 0:2 * NH]
                c.op("act", lambda: nc.scalar.activation(ab[:, :, 0:2 * NH], y2, AF.Abs), reads=[Rg], writes=[Rg])
                c.op("act", lambda: nc.scalar.activation(ab[:, :, 0:2 * NH], ab[:, :, 0:2 * NH], AF.Exp, scale=-1.0), reads=[Rg], writes=[Rg])
                c.op("act", lambda: nc.scalar.activation(lb[:, :, 0:2 * NH], ab[:, :, 0:2 * NH], AF.Ln, bias=1.0), reads=[Rg], writes=[Rg])
                c.op("dve", lambda: nc.vector.scalar_tensor_tensor(gsm[:, :, 0:NH], yb[:, :, 0:NH], 0.0, lb[:, :, 0:NH], op0=ALU.min, op1=ALU.subtract),
                     reads=[Rg], writes=[Rg])
                c.op("dve", lambda: nc.vector.scalar_tensor_tensor(ab[:, :, NH:2 * NH], yb[:, :, NH:2 * NH], 0.0, lb[:, :, NH:2 * NH], op0=ALU.max, op1=ALU.add),
                     reads=[Rg], writes=[Rg])
                for blk in range(4):
                    c.op("dve", lambda: nc.vector.tensor_tensor(gsm[:, blk, NH:2 * NH], ab[:, blk, NH:2 * NH], nA, op=ALU.mult),
                         reads=[Rg, Rc], writes=[Rg])
                c.dma("sp", gat[tok0:tok0 + 512, :].rearrange("(b p) g -> p b g", p=128), gsm, reads=[Rg])
                for blk in range(4):
                    c.op("pe", lambda: nc.tensor.transpose(PS[6][0:G3, blk * 128:(blk + 1) * 128], gsm[:, blk, :], ident_f),
                         reads=[Rg, Rc], writes=[PR[6]], signal=(blk == 3))
                c.op("act", lambda: nc.scalar.copy(gT_sb, PS[6][0:G3, :]), reads=[PR[6]], writes=[RgT])
                c.dma("sp", gatT[:, tok0:tok0 + 512], gT_sb, reads=[RgT])
        else:
            srcs = []
            for to in range(NTO):
                for g in range(cfg.NGO):
                    srcs.append((flat(WB["wout"][g]), DC * 512))
                srcs += ffn_srcs(1)
            ws = WStream(srcs)
            for to in range(NTO):
                for kc in range(DC):
                    c.dma("sp", xt[:, kc, :], x1s[to, :, kc, :], writes=[Rx[kc]])
                    c.dma("sp", hT[:, kc, :], oT[to, :, kc, :], writes=[Rh[kc]])
                for g in range(cfg.NGO):
                    wo, ro = ws.next()
                    wov = wo[:, 0:DC * 512].rearrange("p (k f) -> p k f", f=512)
                    for j in range(4):
                        dc = g * 4 + j
                        pd = 4 + dc % 2
                        for kc in range(DC):
                            last = kc == DC - 1
                            c.op("pe", lambda: nc.tensor.matmul(PS[pd], wov[:, kc, j * 128:(j + 1) * 128], hT[:, kc, :],
                                                                start=(kc == 0), stop=last),
                                 reads=[ro, Rh[kc]], writes=[PR[pd]], signal=last)
                        col = 1 * DC + dc
                        c.op("dve", lambda: nc.vector.scalar_tensor_tensor(xt[:, dc, :], PS[pd], gate[:, col:col + 1], xt[:, dc, :],
                                                                           op0=ALU.mult, op1=ALU.add),
                             reads=[PR[pd], Rmod, Rx[dc]], writes=[Rx[dc]])
                norm_mod(2)
                ffn(ws, 2)
                xf = xt.rearrange("p k t -> p (k t)")
                sq = aT[:, 0:DC, :].rearrange("p k t -> p (k t)")
                c.op("act", lambda: nc.scalar.activation(sq, xf, AF.Square), reads=Rx, writes=Ra[0:DC])
                for kc in range(DC):
                    last = kc == DC - 1
                    c.op("pe", lambda: nc.tensor.matmul(PS[6], ones_bf, aT[:, kc, :], start=(kc == 0), stop=last),
                         reads=[Rc, Ra[kc]], writes=[PR[6]], signal=last)
                c.op("act", lambda: nc.scalar.activation(rstd, PS[6], AF.Ln, scale=1.0 / D, bias=EPS), reads=[PR[6]], writes=[Rrstd])
                c.op("act", lambda: nc.scalar.activation(rstd, rstd, AF.Exp, scale=-0.5), reads=[Rrstd], writes=[Rrstd])
                for kc in range(DC):
                    ti = nxt("tmp", 4)
                    c.op("dve", lambda: nc.vector.scalar_tensor_tensor(tmpf[ti], xt[:, kc, :], small["finalg"][:, kc:kc + 1], rstd,
                                                                       op0=ALU.mult, op1=ALU.mult),
                         reads=[Rx[kc], Rrstd, Rc], writes=[Rt[ti]])
                    c.dma("sp", yT[to, :, kc, :], tmpf[ti], reads=[Rt[ti]])
        c.barrier()
        es.close()

    def phase_gdn():
        es = ExitStack()

        def sb(name, shape, dt):
            return es.enter_context(nc.sbuf_tensor(name + "_gd", list(shape), dt)).ap()
        R0 = Reg("gdnconst")
        maskI = sb("maskI", [128, 128], F32)
        maskU = sb("maskU", [128, 128], F32)
        Mlast = sb("Mlast", [128, 128], F32)
        selc = [sb(f"selc{i}", [128, 128], F32) for i in range(2)]
        ones_f = sb("ones_f", [128, 128], F32)
        for m_, cmp_ in ((maskI, ALU.is_ge), (maskU, ALU.is_gt)):
            c.op("pool", lambda: nc.gpsimd.memset(m_, 1.0), writes=[R0])
            c.op("pool", lambda: nc.gpsimd.affine_select(out=m_, in_=m_, pattern=[[1, 128]], compare_op=cmp_, fill=0.0, base=0,
                                                         channel_multiplier=-1), reads=[R0], writes=[R0])
            c.op("pool", lambda: nc.gpsimd.memset(m_[0:64, 64:128], 0.0), reads=[R0], writes=[R0])
        c.op("pool", lambda: nc.gpsimd.memset(Mlast, 1.0), writes=[R0])
        c.op("pool", lambda: nc.gpsimd.affine_select(out=Mlast.rearrange("p (a b) -> p a b", b=64), in_=Mlast.rearrange("p (a b) -> p a b", b=64),
                                                     pattern=[[-64, 2], [0, 64]], compare_op=ALU.is_equal, fill=0.0, base=-63,
                                                     channel_multiplier=1), reads=[R0], writes=[R0])
        for i in range(2):
            c.op("pool", lambda: nc.gpsimd.memset(selc[i], 1.0), writes=[R0])
            c.op("pool", lambda: nc.gpsimd.affine_select(out=selc[i], in_=selc[i], pattern=[[0, 128]], compare_op=ALU.is_equal, fill=0.0,
                                                         base=-(64 * i + 63), channel_multiplier=1), reads=[R0], writes=[R0])
        c.op("pool", lambda: nc.gpsimd.memset(ones_f, 1.0), writes=[R0])
        KTb = [sb(f"KTb{i}", [128, NH, 128], BF16) for i in range(2)]
        QTb = [sb(f"QTb{i}", [128, NH, 128], BF16) for i in range(2)]
        VTb = [sb(f"VTb{i}", [128, NH, 128], BF16) for i in range(2)]
        ZSb = [sb(f"ZSb{i}", [128, NH, 128], BF16) for i in range(2)]
        gb = [sb(f"gb{i}", [128, 3 * NH], F32) for i in range(2)]
        Rld = [Reg(), Reg()]
        gc = sb("gc", [128, NH], F32); eg = sb("eg", [128, NH], F32); kdec = sb("kdec", [128, NH], F32)
        egl = sb("egl", [128, 2, NH], F32); gcl = sb("gcl", [128, NH], F32); sc1 = sb("sc1", [128, NH], F32)
        Rsm = Reg("gdnsmall")
        ostg = [sb(f"ostg{i}", [128, NH, 128], BF16) for i in range(2)]
        Ros = [Reg(), Reg()]
        H = []
        for h in range(NH):
            d = {}
            for nm, shp, dt in (("Kbg", [128, 128], BF16), ("Kd", [128, 128], BF16), ("Vb", [128, 128], BF16), ("E", [128, 256], F32),
                                ("X", [128, 256], F32), ("PT", [128, 128], F32), ("AT", [128, 128], BF16), ("N", [128, 128], BF16),
                                ("u", [128, 128], F32), ("wT", [128, 128], BF16), ("vn", [128, 128], BF16), ("ot", [128, 128], F32),
                                ("o", [128, 128], F32), ("Sf", [128, 128], F32), ("Sb", [128, 128], BF16), ("dg", [128, 256], F32),
                                ("ss", [128, 1], F32)):
                d[nm] = sb(f"{nm}{h}", shp, dt)
                d["R" + nm] = Reg(f"{nm}{h}")
            H.append(d)
            c.op("pool", lambda: nc.gpsimd.memset(d["Sf"], 0.0), writes=[d["RSf"]])
            c.op("pool", lambda: nc.gpsimd.memset(d["Sb"], 0.0), writes=[d["RSb"]])

        def load(nb):
            bi = nb % 2
            sl = slice(nb * 128, (nb + 1) * 128)
            c.dma("sp", KTb[bi], GkT[:, :, sl].rearrange("h p t -> p h t"), writes=[Rld[bi]])
            c.dma("sp", QTb[bi], GqT[:, :, sl].rearrange("h p t -> p h t"), writes=[Rld[bi]])
            c.dma("sp", VTb[bi], GvT[:, :, sl].rearrange("h p t -> p h t"), writes=[Rld[bi]])
            c.dma("sp", gb[bi], gat[sl, :], writes=[Rld[bi]])
            if nb >= NBP:
                so = slice((nb - NBP) * 128, (nb - NBP + 1) * 128)
                c.dma("sp", ZSb[bi], ZsT[:, :, so].rearrange("h p t -> p h t"), writes=[Rld[bi]])

        load(0)
        for nb in range(NB):
            bi = nb % 2
            own = nb >= NBP
            if nb + 1 < NB:
                load(nb + 1)
            rl = Rld[bi]
            if NBP > 0 and nb == NBP:
                for h in range(NH):
                    d = H[h]
                    c.op("dve", lambda: nc.vector.tensor_scalar(d["Sf"], d["Sf"], pmask[:, 0:1], None, op0=ALU.mult), reads=[Rc], writes=[d["RSf"]])
                    c.op("act", lambda: nc.scalar.copy(d["Sb"], d["Sf"]), reads=[d["RSf"]], writes=[d["RSb"]])
            g_ = gb[bi][:, NH:2 * NH]
            beta = gb[bi][:, 2 * NH:3 * NH]
            c.op("pe", lambda: nc.tensor.matmul(PS[0][:, 0:NH], maskI, g_, start=True, stop=True), reads=[R0, rl], writes=[PR[0]])
            c.op("act", lambda: nc.scalar.copy(gc, PS[0][:, 0:NH]), reads=[PR[0]], writes=[Rsm])
            c.op("pe", lambda: nc.tensor.matmul(PS[0][:, NH:2 * NH], Mlast, gc, start=True, stop=True), reads=[R0, Rsm], writes=[PR[0]], signal=False)
            c.op("pe", lambda: nc.tensor.matmul(PS[0][:, 2 * NH:3 * NH], selc[0], gc, start=True, stop=True), reads=[R0, Rsm], writes=[PR[0]], signal=False)
            c.op("pe", lambda: nc.tensor.matmul(PS[0][:, 3 * NH:4 * NH], selc[1], gc, start=True, stop=True), reads=[R0, Rsm], writes=[PR[0]])
            c.op("act", lambda: nc.scalar.activation(eg, gc, AF.Exp), reads=[Rsm], writes=[Rsm])
            c.op("dve", lambda: nc.vector.tensor_tensor(kdec, PS[0][:, NH:2 * NH], gc, op=ALU.subtract), reads=[PR[0], Rsm], writes=[Rsm])
            c.op("act", lambda: nc.scalar.activation(kdec, kdec, AF.Exp), reads=[Rsm], writes=[Rsm])
            c.op("act", lambda: nc.scalar.activation(egl.rearrange("p a h -> p (a h)"), PS[0][:, 2 * NH:4 * NH], AF.Exp), reads=[PR[0]], writes=[Rsm])
            c.op("act", lambda: nc.scalar.activation(gcl, beta, AF.Ln), reads=[rl], writes=[Rsm])
            c.op("dve", lambda: nc.vector.tensor_tensor(gcl, gcl, gc, op=ALU.add), reads=[Rsm], writes=[Rsm])
            c.op("dve", lambda: nc.vector.tensor_tensor(sc1, beta, eg, op=ALU.mult), reads=[rl, Rsm], writes=[Rsm])
            if GDN_LIMIT <= 0:
                continue
            for h in range(NH):
                d = H[h]
                c.op("pe", lambda: nc.tensor.matmul(PS[h][:, 0:128], KTb[bi][:, h, :], ident_bf, start=True, stop=True), reads=[rl, Rc], writes=[PR[h]], signal=False)
                c.op("pe", lambda: nc.tensor.matmul(PS[h][:, 128:256], VTb[bi][:, h, :], ident_bf, start=True, stop=True), reads=[rl, Rc], writes=[PR[h]])
                if GDN_SUB >= 2:
                    c.op("dve", lambda: nc.vector.tensor_scalar(d["Kbg"], PS[h][:, 0:128], sc1[:, h:h + 1], None, op0=ALU.mult), reads=[PR[h], Rsm], writes=[d["RKbg"]])
                if GDN_SUB >= 3:
                    c.op("dve", lambda: nc.vector.tensor_scalar(d["Kd"], PS[h][:, 0:128], kdec[:, h:h + 1], None, op0=ALU.mult), reads=[PR[h], Rsm], writes=[d["RKd"]])
                if GDN_SUB >= 4:
                    c.op("dve", lambda: nc.vector.tensor_scalar(d["Vb"], PS[h][:, 128:256], beta[:, h:h + 1], None, op0=ALU.mult), reads=[PR[h], rl], writes=[d["RVb"]])
                if GDN_SUB >= 5:
                    c.op("dve", lambda: nc.vector.tensor_scalar(d["dg"][:, 0:128], ident_f, gcl[:, h:h + 1], None, op0=ALU.mult), reads=[Rc, Rsm], writes=[d["Rdg"]])
                    c.op("dve", lambda: nc.vector.tensor_scalar(d["dg"][:, 128:256], ident_f, gc[:, h:h + 1], None, op0=ALU.mult), reads=[Rc, Rsm], writes=[d["Rdg"]])
            if GDN_LIMIT <= 1:
                continue
            for h in range(NH):
                d = H[h]
                c.op("pe", lambda: nc.tensor.matmul(PS[h][:, 128:256], KTb[bi][:, h, :], KTb[bi][:, h, :], start=True, stop=True), reads=[rl], writes=[PR[h]], signal=False)
                c.op("pe", lambda: nc.tensor.matmul(PS[h][:, 256:512], ones_f, d["dg"], start=True, stop=True), reads=[R0, d["Rdg"]], writes=[PR[h]])
                c.op("dve", lambda: nc.vector.tensor_scalar(d["E"], PS[h][:, 256:512], gc[:, h:h + 1], 0.0, op0=ALU.subtract, op1=ALU.min), reads=[PR[h], Rsm], writes=[d["RE"]])
                c.op("act", lambda: nc.scalar.activation(d["E"], d["E"], AF.Exp), reads=[d["RE"]], writes=[d["RE"]])
                c.op("pool", lambda: nc.gpsimd.tensor_tensor(d["E"][:, 0:128], d["E"][:, 0:128], maskU, op=ALU.mult), reads=[d["RE"], R0], writes=[d["RE"]])
                c.op("pool", lambda: nc.gpsimd.tensor_tensor(d["E"][:, 128:256], d["E"][:, 128:256], maskI, op=ALU.mult), reads=[d["RE"], R0], writes=[d["RE"]])
                c.op("dve", lambda: nc.vector.tensor_tensor(d["X"][:, 0:128], d["E"][:, 0:128], PS[h][:, 128:256], op=ALU.mult), reads=[d["RE"], PR[h]], writes=[d["RX"]])
                c.op("dve", lambda: nc.vector.tensor_tensor(d["X"][:, 128:256], ident_f, d["X"][:, 0:128], op=ALU.subtract), reads=[Rc, d["RX"]], writes=[d["RX"]])
            if GDN_LIMIT <= 2:
                continue
            for h in range(NH):
                d = H[h]
                c.op("pe", lambda: nc.tensor.matmul(PS[h][:, 0:128], KTb[bi][:, h, :], QTb[bi][:, h, :], start=True, stop=True), reads=[rl], writes=[PR[h]])
                if GDN_SUB2 >= 2:
                    c.op("dve", lambda: nc.vector.tensor_tensor(d["AT"], d["E"][:, 128:256], PS[h][:, 0:128], op=ALU.mult), reads=[d["RE"], PR[h]], writes=[d["RAT"]])
                if GDN_SUB2 >= 3:
                    c.op("pe", lambda: nc.tensor.matmul(PS[h][:, 128:256], d["X"][:, 0:128], ident_f, start=True, stop=True), reads=[d["RX"], Rc], writes=[PR[h]])
                if GDN_SUB2 >= 4:
                    c.op("act", lambda: nc.scalar.copy(d["PT"], PS[h][:, 128:256]), reads=[PR[h]], writes=[d["RPT"]])
            if GDN_LIMIT <= 3:
                continue
            for k in range(5 if GDN_SUB3 >= 4 else 1):
                for h in range(NH):
                    d = H[h]
                    wid = 128 if k == 0 else 256
                    c.op("pe", lambda: nc.tensor.matmul(PS[h][:, 256:256 + wid], d["PT"], d["X"][:, 0:wid], start=True, stop=True), reads=[d["RPT"], d["RX"]], writes=[PR[h]])
                    if GDN_SUB3 >= 2:
                        c.op("pe", lambda: nc.tensor.matmul(PS[h][:, 128:256], d["X"][:, 0:128], d["PT"], start=True, stop=True), reads=[d["RPT"], d["RX"]], writes=[PR[h]])
                    if GDN_SUB3 >= 3:
                        c.op("dve", lambda: nc.vector.tensor_copy(d["X"][:, 0:128], PS[h][:, 256:384]), reads=[PR[h]], writes=[d["RX"]])
                        if k >= 1:
                            c.op("dve", lambda: nc.vector.tensor_tensor(d["X"][:, 128:256], d["X"][:, 128:256], PS[h][:, 384:512], op=ALU.add), reads=[PR[h], d["RX"]], writes=[d["RX"]])
                        c.op("act", lambda: nc.scalar.copy(d["PT"], PS[h][:, 128:256]), reads=[PR[h]], writes=[d["RPT"]])
            for h in range(NH):
                d = H[h]
                c.op("pe", lambda: nc.tensor.matmul(PS[h][:, 384:512], d["PT"], d["X"][:, 128:256], start=True, stop=True), reads=[d["RPT"], d["RX"]], writes=[PR[h]])
                c.op("dve", lambda: nc.vector.tensor_tensor(d["N"], d["X"][:, 128:256], PS[h][:, 384:512], op=ALU.add), reads=[PR[h], d["RX"]], writes=[d["RN"]])
            if GDN_LIMIT <= 4:
                continue
            for h in range(NH):
                d = H[h]
                c.op("pe", lambda: nc.tensor.matmul(PS[h][:, 0:128], d["N"], d["Vb"], start=True, stop=True), reads=[d["RN"], d["RVb"]], writes=[PR[h]], signal=False)
                c.op("pe", lambda: nc.tensor.matmul(PS[h][:, 128:256], d["Kbg"], d["N"], start=True, stop=True), reads=[d["RN"], d["RKbg"]], writes=[PR[h]])
                c.op("act", lambda: nc.scalar.copy(d["u"], PS[h][:, 0:128]), reads=[PR[h]], writes=[d["Ru"]])
                c.op("dve", lambda: nc.vector.tensor_copy(d["wT"], PS[h][:, 128:256]), reads=[PR[h]], writes=[d["RwT"]])
            if GDN_LIMIT <= 5:
                continue
            for ck in range(2):
                r = slice(64 * ck, 64 * ck + 64)
                for h in range(NH):
                    d = H[h]
                    c.op("pe", lambda: nc.tensor.matmul(PS[h][r, 256:384], d["wT"][:, r], d["Sb"], start=True, stop=True), reads=[d["RwT"], d["RSb"]], writes=[PR[h]], signal=not own)
                    if own:
                        c.op("pe", lambda: nc.tensor.matmul(PS[h][r, 384:512], QTb[bi][:, h, r], d["Sb"], start=True, stop=True), reads=[rl, d["RSb"]], writes=[PR[h]])
                    c.op("dve", lambda: nc.vector.tensor_tensor(d["vn"][r, :], d["u"][r, :], PS[h][r, 256:384], op=ALU.subtract), reads=[d["Ru"], PR[h]], writes=[d["Rvn"]])
                    if own:
                        c.op("dve", lambda: nc.vector.tensor_scalar(d["ot"][r, :], PS[h][r, 384:512], eg[r, h:h + 1], None, op0=ALU.mult), reads=[PR[h], Rsm], writes=[d["Rot"]])
                for h in range(NH):
                    d = H[h]
                    if own:
                        c.op("pe", lambda: nc.tensor.matmul(PS[h][r, 0:128], d["AT"][r, r], d["vn"][r, :], start=True, stop=True), reads=[d["RAT"], d["Rvn"]], writes=[PR[h]], signal=False)
                    c.op("pe", lambda: nc.tensor.matmul(PS[h][:, 128:256], d["Kd"][r, :], d["vn"][r, :], start=True, stop=True), reads=[d["RKd"], d["Rvn"]], writes=[PR[h]])
                    if own:
                        c.op("dve", lambda: nc.vector.tensor_tensor(d["o"][r, :], d["ot"][r, :], PS[h][r, 0:128], op=ALU.add), reads=[d["Rot"], PR[h]], writes=[d["Ro"]])
                    c.op("dve", lambda: nc.vector.scalar_tensor_tensor(d["Sf"], d["Sf"], egl[:, ck, h:h + 1], PS[h][:, 128:256], op0=ALU.mult, op1=ALU.add),
                         reads=[d["RSf"], Rsm, PR[h]], writes=[d["RSf"]])
                    c.op("act", lambda: nc.scalar.copy(d["Sb"], d["Sf"]), reads=[d["RSf"]], writes=[d["RSb"]])
            if GDN_LIMIT <= 6:
                continue
            if own:
                nbo = nb - NBP
                oi = nbo % 2
                for h in range(NH):
                    d = H[h]
                    c.op("act", lambda: nc.scalar.activation(d["ot"], d["o"], AF.Square, accum_out=d["ss"]), reads=[d["Ro"]], writes=[d["Rot"], d["Rss"]])
                    c.op("act", lambda: nc.scalar.activation(d["ss"], d["ss"], AF.Ln, scale=1.0 / 128, bias=EPS), reads=[d["Rss"]], writes=[d["Rss"]])
                    c.op("act", lambda: nc.scalar.activation(d["ss"], d["ss"], AF.Exp, scale=-0.5), reads=[d["Rss"]], writes=[d["Rss"]])
                    c.op("dve", lambda: nc.vector.tensor_scalar(d["o"], d["o"], d["ss"], None, op0=ALU.mult), reads=[d["Rss"], d["Ro"]], writes=[d["Ro"]])
                    c.op("pe", lambda: nc.tensor.matmul(PS[h][:, 256:384], d["o"], ident_f, start=True, stop=True), reads=[d["Ro"], Rc], writes=[PR[h]])
                    c.op("act", lambda: nc.scalar.activation(d["ot"], PS[h][:, 256:384], AF.Identity, scale=small["gdng"][:, 0:1]), reads=[PR[h], Rc], writes=[d["Rot"]])
                    c.op("dve", lambda: nc.vector.tensor_tensor(ostg[oi][:, h, :], d["ot"], ZSb[bi][:, h, :], op=ALU.mult), reads=[d["Rot"], rl], writes=[Ros[oi]])
                c.dma("sp", oT[nbo // 4, :, NH:2 * NH, (nbo % 4) * 128:(nbo % 4 + 1) * 128], ostg[oi], reads=[Ros[oi]])
        c.barrier()
        es.close()

    def phase_fox():
        es = ExitStack()

        def sb(name, shape, dt):
            return es.enter_context(nc.sbuf_tensor(name + "_fx", list(shape), dt)).ap()
        lf = sb("lf", [NH, TT], F32)
        cum = sb("cum", [NH, TT], F32)
        Rcum = Reg("cum")
        ones_c = sb("ones_c", [128, 1], F32)
        c.op("pool", lambda: nc.gpsimd.memset(ones_c, 1.0), writes=[Rcum])
        c.dma("sp", lf, gatT[0:NH, :], writes=[Rcum])
        c.op("dve", lambda: nc.vector.tensor_tensor_scan(cum, ones_c[0:NH, :].to_broadcast([NH, TT]), lf, 0.0, op0=ALU.mult, op1=ALU.add),
             reads=[Rcum], writes=[Rcum])
        negcum = sb("negcum", [128, NB, NH], F32)
        for blk in range(NB):
            c.op("pe", lambda: nc.tensor.transpose(PS[7][:, blk * NH:(blk + 1) * NH], cum[0:NH, blk * 128:(blk + 1) * 128], ident_f[0:NH, 0:NH]),
                 reads=[Rcum, Rc], writes=[PR[7]], signal=(blk == NB - 1))
        c.op("act", lambda: nc.scalar.activation(negcum.rearrange("p b h -> p (b h)"), PS[7][:, 0:NB * NH], AF.Copy, scale=-1.0),
             reads=[PR[7]], writes=[Rcum])
        sel = sb("sel", [NH, NH, 128], F32)
        c.op("pool", lambda: nc.gpsimd.memset(sel, 1.0), writes=[Rcum])
        c.op("pool", lambda: nc.gpsimd.affine_select(out=sel, in_=sel, pattern=[[-1, NH], [0, 128]], compare_op=ALU.is_equal,
                                                     fill=0.0, base=0, channel_multiplier=1), reads=[Rcum], writes=[Rcum])
        crefB = sb("crefB", [128, NH, NBO], F32)
        for h in range(NH):
            c.op("pe", lambda: nc.tensor.matmul(PS[6][:, h * NBO:(h + 1) * NBO], sel[:, h, :], cum[0:NH, TP + 127:TT:128], start=True, stop=True),
                 reads=[Rcum], writes=[PR[6]], signal=(h == NH - 1))
        c.op("act", lambda: nc.scalar.copy(crefB.rearrange("p h b -> p (h b)"), PS[6][:, 0:NH * NBO]), reads=[PR[6]], writes=[Rcum])
        maskT = sb("maskT", [128, 128], BF16)
        c.op("pool", lambda: nc.gpsimd.memset(maskT, 1.0), writes=[Rcum])
        c.op("pool", lambda: nc.gpsimd.affine_select(out=maskT, in_=maskT, pattern=[[1, 128]], compare_op=ALU.is_ge,
                                                     fill=0.0, base=0, channel_multiplier=-1), reads=[Rcum], writes=[Rcum])
        pneg = sb("pneg", [128, 1], F32)
        c.op("dve", lambda: nc.vector.tensor_scalar(pneg, pmask, -1.0, 30000.0, op0=ALU.add, op1=ALU.mult), reads=[Rc], writes=[Rcum])
        Kh = [sb(f"Kh{i}", [128, TT], BF16) for i in range(2)]
        Qh = [sb(f"Qh{i}", [128, TO], BF16) for i in range(2)]
        Va = [sb(f"Va{i}", [128, NB, 130], BF16) for i in range(2)]
        RK = [Reg(), Reg()]; RQ = [Reg(), Reg()]; RV = [Reg(), Reg()]
        for i in range(2):
            c.op("pool", lambda: nc.gpsimd.memset(Va[i][:, :, 128:130], 1.0), writes=[RV[i]])
        bqs = [sb(f"bq{i}", [128, NB], F32) for i in range(2)]
        Rbq = [Reg(), Reg()]
        Pb = [sb(f"Pb{i}", [128, 128], BF16) for i in range(4)]
        RP = [Reg() for _ in range(4)]
        Rslot = [Reg() for _ in range(8)]
        rs_ = [sb(f"rs{i}", [128, 1], F32) for i in range(2)]
        ss_ = [sb(f"ss{i}", [128, 1], F32) for i in range(2)]
        on_ = [sb(f"on{i}", [128, 128], F32) for i in range(2)]
        junk = sb("junk", [128, 128], F32)
        Re = [Reg(), Reg()]
        Rj = Reg()
        ostg = [sb(f"ostg{i}", [128, 512], BF16) for i in range(2)]
        Ros = [Reg(), Reg()]
        Rtr = [Reg() for _ in range(4)]

        def epilogue(h, qb):
            e = qb % 2
            Ob, RO = PS[4 + e], PR[4 + e]
            c.op("dve", lambda: nc.vector.reciprocal(rs_[e], Ob[:, 128:129]), reads=[RO], writes=[Re[e]])
            c.op("dve", lambda: nc.vector.tensor_scalar(on_[e], Ob[:, 0:128], rs_[e], None, op0=ALU.mult), reads=[RO, Re[e]], writes=[Re[e]])
            c.op("act", lambda: nc.scalar.activation(junk, on_[e], AF.Square, accum_out=ss_[e]), reads=[Re[e]], writes=[Re[e], Rj])
            c.op("act", lambda: nc.scalar.activation(ss_[e], ss_[e], AF.Ln, scale=1.0 / 128, bias=EPS), reads=[Re[e]], writes=[Re[e]])
            c.op("act", lambda: nc.scalar.activation(ss_[e], ss_[e], AF.Exp, scale=-0.5), reads=[Re[e]], writes=[Re[e]])
            c.op("dve", lambda: nc.vector.tensor_scalar(on_[e], on_[e], ss_[e], None, op0=ALU.mult), reads=[Re[e]], writes=[Re[e]])
            q4 = qb % 4
            c.op("pe", lambda: nc.tensor.transpose(PS[6][:, q4 * 128:(q4 + 1) * 128], on_[e], ident_f), reads=[Re[e], Rc], writes=[PR[6]])
            oi = (qb // 4) % 2
            c.op("act", lambda: nc.scalar.activation(ostg[oi][:, q4 * 128:(q4 + 1) * 128], PS[6][:, q4 * 128:(q4 + 1) * 128], AF.Identity,
                                                     scale=small["foxg"][:, 0:1]), reads=[PR[6], Rc], writes=[Ros[oi]])
            if q4 == 3:
                c.dma("sp", oT[qb // 4, :, h, :], ostg[oi], reads=[Ros[oi]])

        Pe = [sb(f"Pe{i}", [128, 4, 128], BF16) for i in range(4)]
        RPe = [[Reg() for _ in range(4)] for _ in range(4)]
        cnts = {"s": 0, "p": 0}
        for h in range(NH):
            bi = h % 2
            c.dma("sp", Kh[bi], KfT[h], writes=[RK[bi]])
            c.dma("sp", Qh[bi], QfT[h], writes=[RQ[bi]])
            c.dma("sp", Va[bi][:, :, 0:128], Vf[:, h * 128:(h + 1) * 128].rearrange("(b p) d -> p b d", p=128), writes=[RV[bi]])
            groups = []
            for qb in range(NBO):
                nkb = NBP + qb + 1
                for kb0 in range(0, nkb, 4):
                    groups.append((qb, kb0, min(4, nkb - kb0)))
            LA = 3
            sl_of = {}

            def emit_S(gi):
                qb, kb0, n = groups[gi]
                sl = cnts["s"] % 4
                cnts["s"] += 1
                sl_of[gi] = sl
                for j in range(n):
                    kb = kb0 + j
                    c.op("pe", lambda: nc.tensor.matmul(PS[sl][:, j * 128:(j + 1) * 128], Kh[bi][:, kb * 128:(kb + 1) * 128],
                                                        Qh[bi][:, qb * 128:(qb + 1) * 128], start=True, stop=True),
                         reads=[RK[bi], RQ[bi]], writes=[Rslot[sl]], signal=(j == n - 1))
            for gi in range(min(LA, len(groups))):
                emit_S(gi)
            for gi, (qb, kb0, n) in enumerate(groups):
                nkb = NBP + qb + 1
                e = qb % 2
                if kb0 == 0:
                    c.op("dve", lambda: nc.vector.tensor_scalar(bqs[e][:, 0:nkb], negcum[:, 0:nkb, h], crefB[:, h, qb:qb + 1], None, op0=ALU.add),
                         reads=[Rcum], writes=[Rbq[e]])
                    if NBP > 0:
                        c.op("dve", lambda: nc.vector.tensor_scalar(bqs[e][:, 0:NBP], bqs[e][:, 0:NBP], pneg[:, 0:1], None, op0=ALU.add),
                             reads=[Rcum, Rbq[e]], writes=[Rbq[e]])
                    c.op("act", lambda: nc.scalar.activation(bqs[e][:, 0:nkb], bqs[e][:, 0:nkb], AF.Exp), reads=[Rbq[e]], writes=[Rbq[e]])
                sl = sl_of.pop(gi)
                pi = cnts["p"] % 4
                cnts["p"] += 1
                c.op("act", lambda: nc.scalar.activation(Pe[pi][:, 0:n, :].rearrange("p a b -> p (a b)"), PS[sl][:, 0:n * 128], AF.Exp),
                     reads=[Rslot[sl]], writes=RPe[pi][0:n])
                for j in range(n):
                    kb = kb0 + j
                    if kb == nkb - 1:
                        c.op("dve", lambda: nc.vector.scalar_tensor_tensor(Pe[pi][:, j, :], Pe[pi][:, j, :], bqs[e][:, kb:kb + 1], maskT,
                                                                           op0=ALU.mult, op1=ALU.mult),
                             reads=[Rbq[e], Rcum], writes=[RPe[pi][j]])
                    else:
                        c.op("dve", lambda: nc.vector.tensor_scalar(Pe[pi][:, j, :], Pe[pi][:, j, :], bqs[e][:, kb:kb + 1], None, op0=ALU.mult),
                             reads=[Rbq[e]], writes=[RPe[pi][j]])
                    c.op("pe", lambda: nc.tensor.matmul(PS[4 + e][:, 0:130], Pe[pi][:, j, :], Va[bi][:, kb, :], start=(kb == 0), stop=(kb == nkb - 1)),
                         reads=[RPe[pi][j], RV[bi]], writes=[PR[4 + e]], signal=(kb == nkb - 1))
                if gi + LA < len(groups):
                    emit_S(gi + LA)
                if kb0 + n == nkb:
                    epilogue(h, qb)
        c.barrier()
        es.close()

    tile_phase(1)
    if stop_after <= 2:
        for to in range(NTO):
            c.dma("sp", yT[to], x1s[to])
        c.final_wait("sp")
        return nc
    if stop_after >= 3 and not skip_gdn:
        phase_gdn()
    phase_fox()
    if stop_after <= 3:
        c.final_wait("sp")
        return nc
    tile_phase(4)
    c.final_wait("sp")
    return nc


_CACHE = {}


def kernel(**inputs):
    inp = {k: np.asarray(v) for k, v in inputs.items()}
    B, S, D = inp["x"].shape
    half = S // 2
    cfg = Cfg(D=D, TP=half, TO=half)
    key = (D, S)
    if key not in _CACHE:
        _CACHE[key] = build(cfg)
    nc = _CACHE[key]
    hw = host_weights(cfg, inp)
    in_maps = []
    for b in range(B):
        for sidx in range(2):
            m = dict(hw)
            if sidx == 1:
                xs = inp["x"][b]
            else:
                xs = np.concatenate([inp["x"][b, :half], inp["x"][b, :half]], axis=0)
            m["xT"] = host_x(cfg, xs)
            m["cT"] = np.ascontiguousarray(inp["c"][b].reshape(cfg.DC, 128).T)
            m["pmask"] = np.full((128, 1), float(sidx), np.float32)
            in_maps.append(m)
    res = run_bass_kernel_spmd(nc, in_maps, core_ids=list(range(2 * B)))
    out = np.empty((B, S, D), np.float32)
    for b in range(B):
        for sidx in range(2):
            out[b, sidx * half:(sidx + 1) * half] = host_unx(cfg, np.asarray(res.results[2 * b + sidx]["yT"]))
    return out
```

```python
import math
from collections import deque
from contextlib import ExitStack
import numpy as np
import concourse.bass as bass
import concourse.mybir as mybir
from concourse.bass_utils import run_bass_kernel_spmd

F32 = mybir.dt.float32
BF16 = mybir.dt.bfloat16
AF = mybir.ActivationFunctionType
ALU = mybir.AluOpType
EPS = 1e-6
HD = 128
GDN_LIMIT = 99
GDN_SUB = 99
GDN_SUB2 = 99
GDN_SUB3 = 99


class Cfg:
    def __init__(self, D=2048, TP=0, TO=8192, dbg=False):
        self.D = D
        self.FF = 256 * ((8 * D // 3 + 255) // 256)
        self.NH = D // 256
        self.TP, self.TO = TP, TO
        self.TT = TP + TO
        self.DC = D // 128
        self.FC = self.FF // 128
        self.TILE = 512
        self.NT = self.TT // 512
        self.NTP = TP // 512
        self.NTO = TO // 512
        self.NB = self.TT // 128
        self.NBP = TP // 128
        self.NBO = TO // 128
        self.VW = min(512, self.NH * 128)
        self.NGV = self.NH * 128 // self.VW
        self.NGF = 6 * self.NH * 128 // 512
        self.NFG = self.FF // 512
        self.NGO = D // 512
        self.NGA = 9 * D // 512
        self.WELEMS = max(self.DC * 512, self.FC * 128)
        self.dbg = dbg


class Reg:
    __slots__ = ("name", "lw", "rd")

    def __init__(self, name=""):
        self.name = name
        self.lw = None
        self.rd = {}


class Ctx:
    SEM_LIMIT = 30000
    NDMA = 40

    def __init__(self, nc):
        self.nc = nc
        self.eng = {"pe": nc.tensor, "act": nc.scalar, "dve": nc.vector, "pool": nc.gpsimd, "sp": nc.sync}
        self.sems = {}
        self.csem = {}
        self.ccount = {}
        self.nsem = 0
        for e in ("pe", "act", "dve", "pool"):
            self._new_csem(e)
        self.waited = {}
        self.dma_pool = []
        for i in range(self.NDMA):
            self.dma_pool.append([self._alloc(f"dma{i}"), 0])
        self.dma_next = 0
        self.sw_pool = []
        for i in range(8):
            self.sw_pool.append([self._alloc(f"swdma{i}"), 0])
        self.sw_next = 0
        self.n_inst = 0
        self.n_wait = 0

    def _alloc(self, name):
        self.nsem += 1
        key = f"{name}_{self.nsem}"
        self.sems[key] = self.nc.alloc_semaphore(key)
        return key

    def _new_csem(self, e):
        self.csem[e] = self._alloc(f"c_{e}")
        self.ccount[e] = 0

    def _wait(self, e, deps):
        for (k, v) in deps:
            if e == "pe" and k.startswith("c_pe"):
                continue
            if self.waited.get((e, k), 0) >= v:
                continue
            self.eng[e].wait_ge(self.sems[k], v)
            self.waited[(e, k)] = v
            self.n_wait += 1

    @staticmethod
    def _deps(reads, writes):
        deps = []
        for r in reads:
            if r.lw is not None:
                deps.append(r.lw)
        for w in writes:
            if w.lw is not None:
                deps.append(w.lw)
            deps.extend(w.rd.items())
        return deps

    @staticmethod
    def _mark(tok, reads, writes):
        k, v = tok
        for r in reads:
            if r.rd.get(k, 0) < v:
                r.rd[k] = v
        for w in writes:
            w.lw = tok
            w.rd = {}

    def op(self, e, fn, reads=(), writes=(), signal=True):
        pr = [r for r in reads if r.name.startswith("ps")]
        if pr:
            reads = [r for r in reads if not r.name.startswith("ps")]
            writes = list(writes) + pr
        self._wait(e, self._deps(reads, writes))
        ins = fn()
        self.n_inst += 1
        if signal:
            self.ccount[e] += 1
            ins.then_inc(self.sems[self.csem[e]], 1)
            self._mark((self.csem[e], self.ccount[e]), reads, writes)
            if self.ccount[e] >= self.SEM_LIMIT:
                self._new_csem(e)
        else:
            self._mark((self.csem[e], self.ccount[e] + 1), reads, writes)
        return ins

    def dma(self, q, out, in_, reads=(), writes=(), **kw):
        if q == "pool":
            slot = self.sw_pool[self.sw_next]
            self.sw_next = (self.sw_next + 1) % len(self.sw_pool)
        else:
            slot = self.dma_pool[self.dma_next]
            self.dma_next = (self.dma_next + 1) % self.NDMA
        k, tot = slot
        deps = self._deps(reads, writes)
        if tot > 0:
            deps.append((k, tot))
        self._wait(q, deps)
        ins = self.eng[q].dma_start(out=out, in_=in_, **kw)
        ins.then_inc(self.sems[k], 16)
        slot[1] = tot + 16
        self.n_inst += 1
        self._mark((k, tot + 16), reads, writes)
        return ins

    def _all_tokens(self):
        toks = []
        for e in ("pe", "act", "dve", "pool"):
            if self.ccount[e] > 0:
                toks.append((self.csem[e], self.ccount[e]))
        for k, tot in self.dma_pool + self.sw_pool:
            if tot > 0:
                toks.append((k, tot))
        return toks

    def barrier(self):
        toks = self._all_tokens()
        for e in ("pe", "act", "dve", "pool", "sp"):
            for (k, v) in toks:
                if self.waited.get((e, k), 0) >= v:
                    continue
                self.eng[e].wait_ge(self.sems[k], v)
                self.waited[(e, k)] = v
                self.n_wait += 1

    def final_wait(self, e="sp"):
        for (k, v) in self._all_tokens():
            if self.waited.get((e, k), 0) >= v:
                continue
            self.eng[e].wait_ge(self.sems[k], v)
            self.waited[(e, k)] = v


def _fm_groups(w, gw):
    K, Fd = w.shape
    return np.ascontiguousarray(w.reshape(K // 128, 128, Fd // gw, gw).transpose(2, 1, 0, 3))


def _vec_fm(v):
    return np.ascontiguousarray(v.reshape(-1, 128).T)


def host_weights(cfg, inp):
    NH = cfg.NH
    FW = NH * 128
    w = {}
    w["adaw"] = _fm_groups(inp["ada_w"][0], 512)
    w["adab"] = _vec_fm(inp["ada_b"][0])
    w["normg"] = _vec_fm(inp["norm_g"][0].reshape(-1))
    w["finalg"] = _vec_fm(inp["final_norm"])
    for i in range(2):
        w[f"wg{i}"] = _fm_groups(inp["ffn_w_gate"][0, i], 512)
        w[f"wu{i}"] = _fm_groups(inp["ffn_w_up"][0, i], 512)
        w[f"wd{i}"] = _fm_groups(inp["ffn_w_down"][0, i], 128)
    win = inp["w_in"][0]
    o = 0
    q_f = win[:, o:o + FW]; o += FW
    k_f = win[:, o:o + FW]; o += FW
    v_f = win[:, o:o + FW]; o += FW
    f_f = win[:, o:o + NH]; o += NH
    qkv = win[:, o:o + 3 * FW]; o += 3 * FW
    a_g = win[:, o:o + NH]; o += NH
    b_g = win[:, o:o + NH]; o += NH
    z_g = win[:, o:o + FW]; o += FW
    assert o == win.shape[1]
    w["winf"] = _fm_groups(np.concatenate([q_f, k_f, qkv, z_g], axis=1), 512)
    w["winv"] = _fm_groups(v_f, cfg.VW)
    sm = np.concatenate([f_f, a_g, b_g], axis=1)
    w["wins"] = np.ascontiguousarray(sm.reshape(cfg.DC, 128, 3 * NH).transpose(1, 0, 2))
    w["wout"] = _fm_groups(inp["w_out"][0], 512)
    bc = lambda v: np.ascontiguousarray(np.broadcast_to(v[None, :], (128, v.shape[0])))
    w["fbias"] = bc(inp["fox_f_bias"][0])
    w["alog"] = bc(inp["gdn_A_log"][0])
    w["dtb"] = bc(inp["gdn_dt_bias"][0])
    cw = inp["gdn_conv"][0]
    w["convw"] = np.ascontiguousarray(cw.reshape(4, 3 * NH, 128).transpose(2, 1, 0))
    w["foxg"] = np.ascontiguousarray(inp["fox_out_norm"][0].reshape(128, 1))
    w["gdng"] = np.ascontiguousarray(inp["gdn_out_norm"][0].reshape(128, 1))
    return {k: np.ascontiguousarray(v, dtype=np.float32) for k, v in w.items()}


def host_x(cfg, xs):
    return np.ascontiguousarray(xs.reshape(cfg.NT, 512, cfg.DC, 128).transpose(0, 3, 2, 1))


def host_unx(cfg, yT):
    return np.ascontiguousarray(yT.transpose(0, 3, 2, 1).reshape(cfg.TO, cfg.D))


WSHAPES = None


def weight_shapes(cfg):
    D, DC, FC, NH = cfg.D, cfg.DC, cfg.FC, cfg.NH
    s = {
        "adaw": [cfg.NGA, 128, DC, 512], "adab": [128, 9 * DC], "normg": [128, 3 * DC], "finalg": [128, DC],
        "winf": [cfg.NGF, 128, DC, 512], "winv": [cfg.NGV, 128, DC, cfg.VW], "wins": [128, DC, 3 * NH],
        "wout": [cfg.NGO, 128, DC, 512], "fbias": [128, NH], "alog": [128, NH], "dtb": [128, NH],
        "convw": [128, 3 * NH, 4], "foxg": [128, 1], "gdng": [128, 1],
    }
    for i in range(2):
        s[f"wg{i}"] = [cfg.NFG, 128, DC, 512]
        s[f"wu{i}"] = [cfg.NFG, 128, DC, 512]
        s[f"wd{i}"] = [DC, 128, FC, 128]
    return s


CAST = ["wg0", "wu0", "wd0", "wg1", "wu1", "wd1", "winf", "winv", "wins", "wout"]


def build(cfg, stop_after=99, skip_gdn=False):
    nc = bass.Bass("TRN2", target_bir_lowering=False)
    c = Ctx(nc)
    D, DC, FC, NH, NT, NTP, NTO = cfg.D, cfg.DC, cfg.FC, cfg.NH, cfg.NT, cfg.NTP, cfg.NTO
    TT, TO, TP, NB, NBP, NBO = cfg.TT, cfg.TO, cfg.TP, cfg.NB, cfg.NBP, cfg.NBO
    FW = NH * 128

    def dram(name, shape, dt, kind="Internal"):
        return nc.dram_tensor(name, list(shape), dt, kind=kind).ap()

    skind = "ExternalOutput" if cfg.dbg else "Internal"
    xT = dram("xT", [NT, 128, DC, 512], F32, "ExternalInput")
    cT = dram("cT", [128, DC], F32, "ExternalInput")
    pm = dram("pmask", [128, 1], F32, "ExternalInput")
    W = {k: dram(k, s, F32, "ExternalInput") for k, s in weight_shapes(cfg).items()}
    yT = dram("yT", [NTO, 128, DC, 512], F32, "ExternalOutput")
    WB = {k: dram(k + "_bf", weight_shapes(cfg)[k], BF16) for k in CAST}
    x1s = dram("x1s", [NTO, 128, DC, 512], F32, skind)
    QfT = dram("QfT", [NH, 128, TO], BF16, skind)
    KfT = dram("KfT", [NH, 128, TT], BF16, skind)
    Vf = dram("Vf", [TT, FW], BF16, skind)
    GqT = dram("GqT", [NH, 128, TT], BF16, skind)
    GkT = dram("GkT", [NH, 128, TT], BF16, skind)
    GvT = dram("GvT", [NH, 128, TT], BF16, skind)
    ZsT = dram("ZsT", [NH, 128, TO], BF16, skind)
    gat = dram("gat", [TT, 3 * NH], F32, skind)
    gatT = dram("gatT", [3 * NH, TT], F32, skind)
    oT = dram("oT", [NTO, 128, DC, 512], BF16, skind)

    def sb(name, shape, dt):
        return nc.alloc_sbuf_tensor(name, list(shape), dt).ap()

    ones_bf = sb("ones_bf", [128, 128], BF16)
    ident_f = sb("ident_f", [128, 128], F32)
    ident_bf = sb("ident_bf", [128, 128], BF16)
    Rc = Reg("const")
    c.op("pool", lambda: nc.gpsimd.memset(ones_bf, 1.0), writes=[Rc])
    c.op("pool", lambda: nc.gpsimd.memset(ident_f, 1.0), writes=[Rc])
    c.op("pool", lambda: nc.gpsimd.affine_select(out=ident_f, in_=ident_f, pattern=[[-1, 128]], compare_op=ALU.is_equal,
                                                 fill=0.0, base=0, channel_multiplier=1), reads=[Rc], writes=[Rc])
    c.op("pool", lambda: nc.gpsimd.tensor_copy(ident_bf, ident_f), reads=[Rc], writes=[Rc])
    small = {}
    for k in ["adab", "normg", "finalg", "fbias", "alog", "dtb", "convw", "foxg", "gdng"]:
        small[k] = sb("s_" + k, weight_shapes(cfg)[k], F32)
        c.dma("sp", small[k], W[k], writes=[Rc])
    cond = sb("cond", [128, DC], F32)
    c.dma("sp", cond, cT, writes=[Rc])
    pmask = sb("pmask_sb", [128, 1], F32)
    c.dma("sp", pmask, pm, writes=[Rc])

    for k in CAST:
        shp = weight_shapes(cfg)[k]
        n = int(np.prod(shp))
        letters = "abcd"[:len(shp)]
        pat = " ".join(letters)
        cw = 2048
        while n % cw:
            cw //= 2
        src = W[k].rearrange(f"{pat} -> ({pat})").rearrange("(r c) -> r c", c=cw)
        dst = WB[k].rearrange(f"{pat} -> ({pat})").rearrange("(r c) -> r c", c=cw)
        rows = n // cw
        for r0 in range(0, rows, 4096):
            r1 = min(rows, r0 + 4096)
            c.dma("pool", dst[r0:r1, :], src[r0:r1, :])

    PS = [nc.alloc_psum_tensor(f"ps{i}", [128, 512], F32).ap() for i in range(8)]
    PR = [Reg(f"ps{i}") for i in range(8)]

    mod = sb("mod", [128, 9 * DC], F32)
    Rmod = Reg("mod")
    gs = sb("gs", [128, 3 * DC], F32)
    gate = sb("gate", [128, 3 * DC], F32)
    shift = sb("shift", [128, 3 * DC], F32)
    c.op("act", lambda: nc.scalar.activation(cond, cond, AF.Silu), reads=[Rc], writes=[Rc])
    es0 = ExitStack()
    adabuf = [es0.enter_context(nc.sbuf_tensor(f"adabuf{i}", [128, DC, 512], F32)).ap() for i in range(2)]
    adar = [Reg("adabuf0"), Reg("adabuf1")]
    for g in range(cfg.NGA):
        bi = g % 2
        c.dma("sp", adabuf[bi], W["adaw"][g], writes=[adar[bi]])
        for j in range(4):
            col = g * 4 + j
            for kc in range(DC):
                last = kc == DC - 1
                c.op("pe", lambda: nc.tensor.matmul(PS[7][:, col:col + 1], adabuf[bi][:, kc, j * 128:(j + 1) * 128],
                                                    cond[:, kc:kc + 1], start=(kc == 0), stop=last),
                     reads=[adar[bi], Rc], writes=[PR[7]], signal=last)
    c.op("dve", lambda: nc.vector.tensor_tensor(mod, PS[7][:, 0:9 * DC], small["adab"], op=ALU.add),
         reads=[PR[7], Rc], writes=[Rmod])
    for i in range(3):
        sh = mod[:, (3 * i) * DC:(3 * i + 1) * DC]
        sc = mod[:, (3 * i + 1) * DC:(3 * i + 2) * DC]
        gt = mod[:, (3 * i + 2) * DC:(3 * i + 3) * DC]
        sl = slice(i * DC, (i + 1) * DC)
        c.op("dve", lambda: nc.vector.scalar_tensor_tensor(gs[:, sl], sc, 1.0, small["normg"][:, sl], op0=ALU.add, op1=ALU.mult),
             reads=[Rmod, Rc], writes=[Rmod])
        c.op("dve", lambda: nc.vector.tensor_copy(shift[:, sl], sh), reads=[Rmod], writes=[Rmod])
        mw = 1.0 if i == 1 else 0.5
        c.op("dve", lambda: nc.vector.tensor_scalar(gate[:, sl], gt, mw, None, op0=ALU.mult), reads=[Rmod], writes=[Rmod])
    c.barrier()
    es0.close()
    if stop_after <= 0:
        c.dma("sp", yT[0, :, 0, 0:9 * DC], mod, reads=[Rmod])
        c.dma("sp", yT[0, :, 1, 0:3 * DC], gs, reads=[Rmod])
        c.final_wait("sp")
        return nc

    def tile_phase(which):
        es = ExitStack()
        def sb(name, shape, dt):
            return es.enter_context(nc.sbuf_tensor(f'{name}_p{which}', list(shape), dt)).ap()
        xt = sb("xt", [128, DC, 512], F32)
        hT = sb("hT", [128, DC, 512], BF16)
        aT = sb("aT", [128, max(FC, DC), 512], BF16)
        Rx = [Reg(f"x{k}") for k in range(DC)]
        Rh = [Reg(f"h{k}") for k in range(DC)]
        Ra = [Reg(f"a{k}") for k in range(max(FC, DC))]
        NWB = 4
        wbuf = [sb(f"wbuf{i}", [128, cfg.WELEMS], BF16) for i in range(NWB)]
        wreg = [Reg(f"wbuf{i}") for i in range(NWB)]
        tmpf = [sb(f"tmpf{i}", [128, 512], F32) for i in range(4)]
        Rt = [Reg(f"tmpf{i}") for i in range(4)]
        rstd = sb("rstd", [128, 512], F32)
        Rrstd = Reg("rstd")
        stg = [sb(f"stg{i}", [128, 512], BF16) for i in range(4)]
        Rstg = [Reg(f"stg{i}") for i in range(4)]
        cnt = {"tmp": 0, "stg": 0, "w": 0, "cb": 0}

        def nxt(kind, n):
            i = cnt[kind] % n
            cnt[kind] += 1
            return i

        class WStream:
            def __init__(self, srcs, pf=3):
                self.srcs = srcs
                self.pf = pf
                self.issued = 0
                self.taken = 0
                self.slots = deque()

            def _issue(self):
                src, n = self.srcs[self.issued]
                bi = nxt("w", NWB)
                c.dma("sp", wbuf[bi][:, 0:n], src, writes=[wreg[bi]])
                self.slots.append(bi)
                self.issued += 1

            def next(self):
                while self.issued < len(self.srcs) and self.issued < self.taken + self.pf:
                    self._issue()
                bi = self.slots.popleft()
                self.taken += 1
                return wbuf[bi], wreg[bi]

        def flat(ap):
            return ap.rearrange("p a b -> p (a b)")

        def norm_mod(i):
            xf = xt.rearrange("p k t -> p (k t)")
            sq = aT[:, 0:DC, :].rearrange("p k t -> p (k t)")
            c.op("act", lambda: nc.scalar.activation(sq, xf, AF.Square), reads=Rx, writes=Ra[0:DC])
            for kc in range(DC):
                last = kc == DC - 1
                c.op("pe", lambda: nc.tensor.matmul(PS[6], ones_bf, aT[:, kc, :], start=(kc == 0), stop=last),
                     reads=[Rc, Ra[kc]], writes=[PR[6]], signal=last)
            c.op("act", lambda: nc.scalar.activation(rstd, PS[6], AF.Ln, scale=1.0 / D, bias=EPS), reads=[PR[6]], writes=[Rrstd])
            c.op("act", lambda: nc.scalar.activation(rstd, rstd, AF.Exp, scale=-0.5), reads=[Rrstd], writes=[Rrstd])
            for kc in range(DC):
                ti = nxt("tmp", 4)
                col = i * DC + kc
                c.op("dve", lambda: nc.vector.scalar_tensor_tensor(tmpf[ti], xt[:, kc, :], gs[:, col:col + 1], rstd,
                                                                   op0=ALU.mult, op1=ALU.mult),
                     reads=[Rx[kc], Rrstd, Rmod], writes=[Rt[ti]])
                c.op("act", lambda: nc.scalar.activation(hT[:, kc, :], tmpf[ti], AF.Identity, bias=shift[:, col:col + 1], scale=1.0),
                     reads=[Rt[ti], Rmod], writes=[Rh[kc]])

        def ffn_srcs(i):
            s = []
            for fg in range(cfg.NFG):
                s.append((flat(WB[f"wg{i}"][fg]), DC * 512))
                s.append((flat(WB[f"wu{i}"][fg]), DC * 512))
            for dc in range(DC):
                s.append((flat(WB[f"wd{i}"][dc]), FC * 128))
            return s

        def ffn(ws, gi):
            for fg in range(cfg.NFG):
                wg, rg = ws.next()
                wu, ru = ws.next()
                wgv = wg[:, 0:DC * 512].rearrange("p (k f) -> p k f", f=512)
                wuv = wu[:, 0:DC * 512].rearrange("p (k f) -> p k f", f=512)
                for j in range(4):
                    fc = fg * 4 + j
                    pg, pu = fc % 2, 2 + fc % 2
                    for kc in range(DC):
                        last = kc == DC - 1
                        c.op("pe", lambda: nc.tensor.matmul(PS[pg], wgv[:, kc, j * 128:(j + 1) * 128], hT[:, kc, :],
                                                            start=(kc == 0), stop=last),
                             reads=[rg, Rh[kc]], writes=[PR[pg]], signal=last)
                    for kc in range(DC):
                        last = kc == DC - 1
                        c.op("pe", lambda: nc.tensor.matmul(PS[pu], wuv[:, kc, j * 128:(j + 1) * 128], hT[:, kc, :],
                                                            start=(kc == 0), stop=last),
                             reads=[ru, Rh[kc]], writes=[PR[pu]], signal=last)
                    ti = nxt("tmp", 4)
                    c.op("act", lambda: nc.scalar.activation(tmpf[ti], PS[pg], AF.Silu), reads=[PR[pg]], writes=[Rt[ti]])
                    c.op("dve", lambda: nc.vector.tensor_tensor(aT[:, fc, :], tmpf[ti], PS[pu], op=ALU.mult),
                         reads=[Rt[ti], PR[pu]], writes=[Ra[fc]])
            for dc in range(DC):
                wd, rd = ws.next()
                wdv = wd[:, 0:FC * 128].rearrange("p (k f) -> p k f", f=128)
                pd = 4 + dc % 2
                for fc in range(FC):
                    last = fc == FC - 1
                    c.op("pe", lambda: nc.tensor.matmul(PS[pd], wdv[:, fc, :], aT[:, fc, :], start=(fc == 0), stop=last),
                         reads=[rd, Ra[fc]], writes=[PR[pd]], signal=last)
                col = gi * DC + dc
                c.op("dve", lambda: nc.vector.scalar_tensor_tensor(xt[:, dc, :], PS[pd], gate[:, col:col + 1], xt[:, dc, :],
                                                                   op0=ALU.mult, op1=ALU.add),
                     reads=[PR[pd], Rmod, Rx[dc]], writes=[Rx[dc]])

        if which == 1:
            halo = sb("halo", [128, 3 * NH, 4], F32)
            Rhalo = [Reg(f"halo{i}") for i in range(3 * NH)]
            c.op("pool", lambda: nc.gpsimd.memset(halo, 0.0), writes=Rhalo)
            cb = [sb(f"cb{i}", [128, 516], F32) for i in range(4)]
            Rcb = [Reg(f"cb{i}") for i in range(4)]
            tails = deque()
            wins_sb = sb("wins_sb", [128, DC, 3 * NH], BF16)
            c.dma("sp", wins_sb, WB["wins"], writes=[Rc])
            gsm = sb("gsm", [128, 4, 3 * NH], F32)
            gtmp = [sb(f"gtmp{i}", [128, 4, 3 * NH], F32) for i in range(3)]
            Rg = Reg("gsm")
            nA = sb("nA", [128, NH], F32)
            c.op("act", lambda: nc.scalar.activation(nA, small["alog"], AF.Exp), reads=[Rc], writes=[Rc])
            c.op("dve", lambda: nc.vector.tensor_scalar(nA, nA, -1.0, None, op0=ALU.mult), reads=[Rc], writes=[Rc])
            gT_sb = sb("gT_sb", [3 * NH, 512], F32)
            RgT = Reg("gT")

            srcs = []
            for t in range(NT):
                srcs += ffn_srcs(0)
                for g in range(cfg.NGF):
                    srcs.append((flat(WB["winf"][g]), DC * 512))
                for g in range(cfg.NGV):
                    srcs.append((flat(WB["winv"][g]), DC * cfg.VW))
            ws = WStream(srcs)
            inv_sqrt_d = 1.0 / math.sqrt(HD)

            for t in range(NT):
                own = t >= NTP
                to = t - NTP
                tok0 = t * 512
                for kc in range(DC):
                    c.dma("sp", xt[:, kc, :], xT[t, :, kc, :], writes=[Rx[kc]])
                norm_mod(0)
                ffn(ws, 0)
                if own:
                    for kc in range(DC):
                        c.dma("sp", x1s[to, :, kc, :], xt[:, kc, :], reads=[Rx[kc]])
                if stop_after <= 1:
                    continue
                norm_mod(1)
                if NTP > 0 and t == NTP:
                    c.op("dve", lambda: nc.vector.tensor_scalar(halo.rearrange("p a b -> p (a b)"), halo.rearrange("p a b -> p (a b)"),
                                                                pmask[:, 0:1], None, op0=ALU.mult), reads=[Rc], writes=Rhalo)
                for g in range(cfg.NGF):
                    wf, rf = ws.next()
                    wfv = wf[:, 0:DC * 512].rearrange("p (k f) -> p k f", f=512)
                    for j in range(4):
                        ch = g * 4 + j
                        typ, h = ch // NH, ch % NH
                        need = own or typ in (1, 3, 4) or (typ == 2 and t == NTP - 1)
                        if not need:
                            continue
                        pp = ch % 4
                        for kc in range(DC):
                            last = kc == DC - 1
                            c.op("pe", lambda: nc.tensor.matmul(PS[pp], wfv[:, kc, j * 128:(j + 1) * 128], hT[:, kc, :],
                                                                start=(kc == 0), stop=last),
                                 reads=[rf, Rh[kc]], writes=[PR[pp]], signal=last)
                        si = nxt("stg", 4)
                        if typ == 0:
                            c.op("act", lambda: nc.scalar.activation(stg[si], PS[pp], AF.Copy, scale=inv_sqrt_d),
                                 reads=[PR[pp]], writes=[Rstg[si]])
                            c.dma("sp", QfT[h, :, to * 512:(to + 1) * 512], stg[si], reads=[Rstg[si]])
                        elif typ == 1:
                            c.op("act", lambda: nc.scalar.copy(stg[si], PS[pp]), reads=[PR[pp]], writes=[Rstg[si]])
                            c.dma("sp", KfT[h, :, tok0:tok0 + 512], stg[si], reads=[Rstg[si]])
                        elif typ == 5:
                            c.op("act", lambda: nc.scalar.activation(stg[si], PS[pp], AF.Silu), reads=[PR[pp]], writes=[Rstg[si]])
                            c.dma("sp", ZsT[h, :, to * 512:(to + 1) * 512], stg[si], reads=[Rstg[si]])
                        else:
                            cch = ch - 2 * NH
                            ci = nxt("cb", 4)
                            cbi, rcb = cb[ci], Rcb[ci]
                            c.op("act", lambda: nc.scalar.copy(cbi[:, 4:516], PS[pp]), reads=[PR[pp]], writes=[rcb])
                            c.op("pool", lambda: nc.gpsimd.tensor_copy(cbi[:, 0:4], halo[:, cch, :]), reads=[Rhalo[cch]], writes=[rcb])
                            c.op("pool", lambda: nc.gpsimd.tensor_copy(halo[:, cch, :], cbi[:, 512:516]), reads=[rcb], writes=[Rhalo[cch]])

                            def tail(cbi=cbi, rcb=rcb, cch=cch, typ=typ, h=h, tok0=tok0):
                                ti = nxt("tmp", 4)
                                acc = tmpf[ti]
                                si = nxt("stg", 4)
                                c.op("dve", lambda: nc.vector.tensor_scalar(acc, cbi[:, 1:513], small["convw"][:, cch, 0:1], None, op0=ALU.mult),
                                     reads=[rcb, Rc], writes=[Rt[ti]])
                                for k in range(1, 4):
                                    c.op("dve", lambda: nc.vector.scalar_tensor_tensor(acc, cbi[:, 1 + k:513 + k], small["convw"][:, cch, k:k + 1], acc,
                                                                                       op0=ALU.mult, op1=ALU.add),
                                         reads=[rcb, Rc, Rt[ti]], writes=[Rt[ti]])
                                c.op("act", lambda: nc.scalar.activation(acc, acc, AF.Silu), reads=[Rt[ti]], writes=[Rt[ti]])
                                if typ == 4:
                                    c.op("dve", lambda: nc.vector.tensor_copy(stg[si], acc), reads=[Rt[ti]], writes=[Rstg[si]])
                                    c.dma("sp", GvT[h, :, tok0:tok0 + 512], stg[si], reads=[Rstg[si]])
                                else:
                                    s2 = nxt("stg", 4)
                                    c.op("act", lambda: nc.scalar.activation(stg[s2], acc, AF.Square), reads=[Rt[ti]], writes=[Rstg[s2]])
                                    c.op("pe", lambda: nc.tensor.matmul(PS[6], ones_bf, stg[s2], start=True, stop=True),
                                         reads=[Rc, Rstg[s2]], writes=[PR[6]])
                                    t2 = nxt("tmp", 4)
                                    c.op("act", lambda: nc.scalar.activation(tmpf[t2], PS[6], AF.Ln, bias=EPS), reads=[PR[6]], writes=[Rt[t2]])
                                    c.op("act", lambda: nc.scalar.activation(tmpf[t2], tmpf[t2], AF.Exp, scale=-0.5), reads=[Rt[t2]], writes=[Rt[t2]])
                                    sc_ = inv_sqrt_d if typ == 2 else 1.0
                                    c.op("dve", lambda: nc.vector.scalar_tensor_tensor(stg[si], acc, sc_, tmpf[t2], op0=ALU.mult, op1=ALU.mult),
                                         reads=[Rt[ti], Rt[t2]], writes=[Rstg[si]])
                                    dst = GqT if typ == 2 else GkT
                                    c.dma("sp", dst[h, :, tok0:tok0 + 512], stg[si], reads=[Rstg[si]])
                            tails.append(tail)
                        while len(tails) > 2:
                            tails.popleft()()
                while tails:
                    tails.popleft()()
                for g in range(cfg.NGV):
                    wv, rv = ws.next()
                    wvv = wv[:, 0:DC * cfg.VW].rearrange("p (k f) -> p k f", f=cfg.VW)
                    for blk in range(4):
                        pp = blk % 2
                        for kc in range(DC):
                            last = kc == DC - 1
                            c.op("pe", lambda: nc.tensor.matmul(PS[pp][:, 0:cfg.VW], hT[:, kc, blk * 128:(blk + 1) * 128], wvv[:, kc, :],
                                                                start=(kc == 0), stop=last),
                                 reads=[rv, Rh[kc]], writes=[PR[pp]], signal=last)
                        si = nxt("stg", 4)
                        c.op("act", lambda: nc.scalar.copy(stg[si][:, 0:cfg.VW], PS[pp][:, 0:cfg.VW]), reads=[PR[pp]], writes=[Rstg[si]])
                        c.dma("sp", Vf[tok0 + blk * 128:tok0 + (blk + 1) * 128, g * cfg.VW:(g + 1) * cfg.VW], stg[si][:, 0:cfg.VW],
                              reads=[Rstg[si]])
                G3 = 3 * NH
                for blk in range(4):
                    for kc in range(DC):
                        last = kc == DC - 1
                        c.op("pe", lambda: nc.tensor.matmul(PS[7][:, blk * G3:(blk + 1) * G3], hT[:, kc, blk * 128:(blk + 1) * 128],
                                                            wins_sb[:, kc, :], start=(kc == 0), stop=last),
                             reads=[Rc, Rh[kc]], writes=[PR[7]], signal=last)
                psg = PS[7][:, 0:4 * G3].rearrange("p (b g) -> p b g", g=G3)
                yb, ab, lb = gtmp
                for blk in range(4):
                    c.op("dve", lambda: nc.vector.tensor_tensor(yb[:, blk, 0:NH], psg[:, blk, 0:NH], small["fbias"], op=ALU.add),
                         reads=[PR[7], Rc], writes=[Rg])
                    c.op("dve", lambda: nc.vector.tensor_tensor(yb[:, blk, NH:2 * NH], psg[:, blk, NH:2 * NH], small["dtb"], op=ALU.add),
                         reads=[PR[7], Rc], writes=[Rg])
                c.op("act", lambda: nc.scalar.activation(gsm[:, :, 2 * NH:3 * NH], psg[:, :, 2 * NH:3 * NH], AF.Sigmoid),
                     reads=[PR[7]], writes=[Rg])
                y2 = yb[:, :, 0:2 * NH]
                c.op("act", lambda: nc.scalar.activation(ab[:, :, 0:2 * NH], y2, AF.Abs), reads=[Rg], writes=[Rg])
                c.op("act", lambda: nc.scalar.activation(ab[:, :, 0:2 * NH], ab[:, :, 0:2 * NH], AF.Exp, scale=-1.0), reads=[Rg], writes=[Rg])
                c.op("act", lambda: nc.scalar.activation(lb[:, :, 0:2 * NH], ab[:, :, 0:2 * NH], AF.Ln, bias=1.0), reads=[Rg], writes=[Rg])
                c.op("dve", lambda: nc.vector.scalar_tensor_tensor(gsm[:, :, 0:NH], yb[:, :, 0:NH], 0.0, lb[:, :, 0:NH], op0=ALU.min, op1=ALU.subtract),
                     reads=[Rg], writes=[Rg])
                c.op("dve", lambda: nc.vector.scalar_tensor_tensor(ab[:, :, NH:2 * NH], yb[:, :, NH:2 * NH], 0.0, lb[:, :, NH:2 * NH], op0=ALU.max, op1=ALU.add),
                     reads=[Rg], writes=[Rg])
                for blk in range(4):
                    c.op("dve", lambda: nc.vector.tensor_tensor(gsm[:, blk, NH:2 * NH], ab[:, blk, NH:2 * NH], nA, op=ALU.mult),
                         reads=[Rg, Rc], writes=[Rg])
                c.dma("sp", gat[tok0:tok0 + 512, :].rearrange("(b p) g -> p b g", p=128), gsm, reads=[Rg])
                for blk in range(4):
                    c.op("pe", lambda: nc.tensor.transpose(PS[6][0:G3, blk * 128:(blk + 1) * 128], gsm[:, blk, :], ident_f),
                         reads=[Rg, Rc], writes=[PR[6]], signal=(blk == 3))
                c.op("act", lambda: nc.scalar.copy(gT_sb, PS[6][0:G3, :]), reads=[PR[6]], writes=[RgT])
                c.dma("sp", gatT[:, tok0:tok0 + 512], gT_sb, reads=[RgT])
        else:
            srcs = []
            for to in range(NTO):
                for g in range(cfg.NGO):
                    srcs.append((flat(WB["wout"][g]), DC * 512))
                srcs += ffn_srcs(1)
            ws = WStream(srcs)
            for to in range(NTO):
                for kc in range(DC):
                    c.dma("sp", xt[:, kc, :], x1s[to, :, kc, :], writes=[Rx[kc]])
                    c.dma("sp", hT[:, kc, :], oT[to, :, kc, :], writes=[Rh[kc]])
                for g in range(cfg.NGO):
                    wo, ro = ws.next()
                    wov = wo[:, 0:DC * 512].rearrange("p (k f) -> p k f", f=512)
                    for j in range(4):
                        dc = g * 4 + j
                        pd = 4 + dc % 2
                        for kc in range(DC):
                            last = kc == DC - 1
                            c.op("pe", lambda: nc.tensor.matmul(PS[pd], wov[:, kc, j * 128:(j + 1) * 128], hT[:, kc, :],
                                                                start=(kc == 0), stop=last),
                                 reads=[ro, Rh[kc]], writes=[PR[pd]], signal=last)
                        col = 1 * DC + dc
                        c.op("dve", lambda: nc.vector.scalar_tensor_tensor(xt[:, dc, :], PS[pd], gate[:, col:col + 1], xt[:, dc, :],
                                                                           op0=ALU.mult, op1=ALU.add),
                             reads=[PR[pd], Rmod, Rx[dc]], writes=[Rx[dc]])
                norm_mod(2)
                ffn(ws, 2)
                xf = xt.rearrange("p k t -> p (k t)")
                sq = aT[:, 0:DC, :].rearrange("p k t -> p (k t)")
                c.op("act", lambda: nc.scalar.activation(sq, xf, AF.Square), reads=Rx, writes=Ra[0:DC])
                for kc in range(DC):
                    last = kc == DC - 1
                    c.op("pe", lambda: nc.tensor.matmul(PS[6], ones_bf, aT[:, kc, :], start=(kc == 0), stop=last),
                         reads=[Rc, Ra[kc]], writes=[PR[6]], signal=last)
                c.op("act", lambda: nc.scalar.activation(rstd, PS[6], AF.Ln, scale=1.0 / D, bias=EPS), reads=[PR[6]], writes=[Rrstd])
                c.op("act", lambda: nc.scalar.activation(rstd, rstd, AF.Exp, scale=-0.5), reads=[Rrstd], writes=[Rrstd])
                for kc in range(DC):
                    ti = nxt("tmp", 4)
                    c.op("dve", lambda: nc.vector.scalar_tensor_tensor(tmpf[ti], xt[:, kc, :], small["finalg"][:, kc:kc + 1], rstd,
                                                                       op0=ALU.mult, op1=ALU.mult),
                         reads=[Rx[kc], Rrstd, Rc], writes=[Rt[ti]])
                    c.dma("sp", yT[to, :, kc, :], tmpf[ti], reads=[Rt[ti]])
        c.barrier()
        es.close()

    def phase_gdn():
        es = ExitStack()

        def sb(name, shape, dt):
            return es.enter_context(nc.sbuf_tensor(name + "_gd", list(shape), dt)).ap()
        R0 = Reg("gdnconst")
        maskI = sb("maskI", [128, 128], F32)
        maskU = sb("maskU", [128, 128], F32)
        Mlast = sb("Mlast", [128, 128], F32)
        selc = [sb(f"selc{i}", [128, 128], F32) for i in range(2)]
        ones_f = sb("ones_f", [128, 128], F32)
        for m_, cmp_ in ((maskI, ALU.is_ge), (maskU, ALU.is_gt)):
            c.op("pool", lambda: nc.gpsimd.memset(m_, 1.0), writes=[R0])
            c.op("pool", lambda: nc.gpsimd.affine_select(out=m_, in_=m_, pattern=[[1, 128]], compare_op=cmp_, fill=0.0, base=0,
                                                         channel_multiplier=-1), reads=[R0], writes=[R0])
            c.op("pool", lambda: nc.gpsimd.memset(m_[0:64, 64:128], 0.0), reads=[R0], writes=[R0])
        c.op("pool", lambda: nc.gpsimd.memset(Mlast, 1.0), writes=[R0])
        c.op("pool", lambda: nc.gpsimd.affine_select(out=Mlast.rearrange("p (a b) -> p a b", b=64), in_=Mlast.rearrange("p (a b) -> p a b", b=64),
                                                     pattern=[[-64, 2], [0, 64]], compare_op=ALU.is_equal, fill=0.0, base=-63,
                                                     channel_multiplier=1), reads=[R0], writes=[R0])
        for i in range(2):
            c.op("pool", lambda: nc.gpsimd.memset(selc[i], 1.0), writes=[R0])
            c.op("pool", lambda: nc.gpsimd.affine_select(out=selc[i], in_=selc[i], pattern=[[0, 128]], compare_op=ALU.is_equal, fill=0.0,
                                                         base=-(64 * i + 63), channel_multiplier=1), reads=[R0], writes=[R0])
        c.op("pool", lambda: nc.gpsimd.memset(ones_f, 1.0), writes=[R0])
        KTb = [sb(f"KTb{i}", [128, NH, 128], BF16) for i in range(2)]
        QTb = [sb(f"QTb{i}", [128, NH, 128], BF16) for i in range(2)]
        VTb = [sb(f"VTb{i}", [128, NH, 128], BF16) for i in range(2)]
        ZSb = [sb(f"ZSb{i}", [128, NH, 128], BF16) for i in range(2)]
        gb = [sb(f"gb{i}", [128, 3 * NH], F32) for i in range(2)]
        Rld = [Reg(), Reg()]
        gc = sb("gc", [128, NH], F32); eg = sb("eg", [128, NH], F32); kdec = sb("kdec", [128, NH], F32)
        egl = sb("egl", [128, 2, NH], F32); gcl = sb("gcl", [128, NH], F32); sc1 = sb("sc1", [128, NH], F32)
        Rsm = Reg("gdnsmall")
        ostg = [sb(f"ostg{i}", [128, NH, 128], BF16) for i in range(2)]
        Ros = [Reg(), Reg()]
        H = []
        for h in range(NH):
            d = {}
            for nm, shp, dt in (("Kbg", [128, 128], BF16), ("Kd", [128, 128], BF16), ("Vb", [128, 128], BF16), ("E", [128, 256], F32),
                                ("X", [128, 256], F32), ("PT", [128, 128], F32), ("AT", [128, 128], BF16), ("N", [128, 128], BF16),
                                ("u", [128, 128], F32), ("wT", [128, 128], BF16), ("vn", [128, 128], BF16), ("ot", [128, 128], F32),
                                ("o", [128, 128], F32), ("Sf", [128, 128], F32), ("Sb", [128, 128], BF16), ("dg", [128, 256], F32),
                                ("ss", [128, 1], F32)):
                d[nm] = sb(f"{nm}{h}", shp, dt)
                d["R" + nm] = Reg(f"{nm}{h}")
            H.append(d)
            c.op("pool", lambda: nc.gpsimd.memset(d["Sf"], 0.0), writes=[d["RSf"]])
            c.op("pool", lambda: nc.gpsimd.memset(d["Sb"], 0.0), writes=[d["RSb"]])

        def load(nb):
            bi = nb % 2
            sl = slice(nb * 128, (nb + 1) * 128)
            c.dma("sp", KTb[bi], GkT[:, :, sl].rearrange("h p t -> p h t"), writes=[Rld[bi]])
            if nb >= NBP:
                c.dma("sp", QTb[bi], GqT[:, :, sl].rearrange("h p t -> p h t"), writes=[Rld[bi]])
            c.dma("sp", VTb[bi], GvT[:, :, sl].rearrange("h p t -> p h t"), writes=[Rld[bi]])
            c.dma("sp", gb[bi], gat[sl, :], writes=[Rld[bi]])
            if nb >= NBP:
                so = slice((nb - NBP) * 128, (nb - NBP + 1) * 128)
                c.dma("sp", ZSb[bi], ZsT[:, :, so].rearrange("h p t -> p h t"), writes=[Rld[bi]])

        load(0)
        for nb in range(NB):
            bi = nb % 2
            own = nb >= NBP
            if nb + 1 < NB:
                load(nb + 1)
            rl = Rld[bi]
            if NBP > 0 and nb == NBP:
                for h in range(NH):
                    d = H[h]
                    c.op("dve", lambda: nc.vector.tensor_scalar(d["Sf"], d["Sf"], pmask[:, 0:1], None, op0=ALU.mult), reads=[Rc], writes=[d["RSf"]])
                    c.op("act", lambda: nc.scalar.copy(d["Sb"], d["Sf"]), reads=[d["RSf"]], writes=[d["RSb"]])
            g_ = gb[bi][:, NH:2 * NH]
            beta = gb[bi][:, 2 * NH:3 * NH]
            c.op("pe", lambda: nc.tensor.matmul(PS[0][:, 0:NH], maskI, g_, start=True, stop=True), reads=[R0, rl], writes=[PR[0]])
            c.op("act", lambda: nc.scalar.copy(gc, PS[0][:, 0:NH]), reads=[PR[0]], writes=[Rsm])
            c.op("pe", lambda: nc.tensor.matmul(PS[0][:, NH:2 * NH], Mlast, gc, start=True, stop=True), reads=[R0, Rsm], writes=[PR[0]], signal=False)
            c.op("pe", lambda: nc.tensor.matmul(PS[0][:, 2 * NH:3 * NH], selc[0], gc, start=True, stop=True), reads=[R0, Rsm], writes=[PR[0]], signal=False)
            c.op("pe", lambda: nc.tensor.matmul(PS[0][:, 3 * NH:4 * NH], selc[1], gc, start=True, stop=True), reads=[R0, Rsm], writes=[PR[0]])
            c.op("act", lambda: nc.scalar.activation(eg, gc, AF.Exp), reads=[Rsm], writes=[Rsm])
            c.op("dve", lambda: nc.vector.tensor_tensor(kdec, PS[0][:, NH:2 * NH], gc, op=ALU.subtract), reads=[PR[0], Rsm], writes=[Rsm])
            c.op("act", lambda: nc.scalar.activation(kdec, kdec, AF.Exp), reads=[Rsm], writes=[Rsm])
            c.op("act", lambda: nc.scalar.activation(egl.rearrange("p a h -> p (a h)"), PS[0][:, 2 * NH:4 * NH], AF.Exp), reads=[PR[0]], writes=[Rsm])
            c.op("act", lambda: nc.scalar.activation(gcl, beta, AF.Ln), reads=[rl], writes=[Rsm])
            c.op("dve", lambda: nc.vector.tensor_tensor(gcl, gcl, gc, op=ALU.add), reads=[Rsm], writes=[Rsm])
            c.op("dve", lambda: nc.vector.tensor_tensor(sc1, beta, eg, op=ALU.mult), reads=[rl, Rsm], writes=[Rsm])
            if GDN_LIMIT <= 0:
                continue
            for h in range(NH):
                d = H[h]
                c.op("pe", lambda: nc.tensor.matmul(PS[h][:, 0:128], KTb[bi][:, h, :], ident_bf, start=True, stop=True), reads=[rl, Rc], writes=[PR[h]], signal=False)
                c.op("pe", lambda: nc.tensor.matmul(PS[h][:, 128:256], VTb[bi][:, h, :], ident_bf, start=True, stop=True), reads=[rl, Rc], writes=[PR[h]])
                if GDN_SUB >= 2:
                    c.op("dve", lambda: nc.vector.tensor_scalar(d["Kbg"], PS[h][:, 0:128], sc1[:, h:h + 1], None, op0=ALU.mult), reads=[PR[h], Rsm], writes=[d["RKbg"]])
                if GDN_SUB >= 3:
                    c.op("dve", lambda: nc.vector.tensor_scalar(d["Kd"], PS[h][:, 0:128], kdec[:, h:h + 1], None, op0=ALU.mult), reads=[PR[h], Rsm], writes=[d["RKd"]])
                if GDN_SUB >= 4:
                    c.op("dve", lambda: nc.vector.tensor_scalar(d["Vb"], PS[h][:, 128:256], beta[:, h:h + 1], None, op0=ALU.mult), reads=[PR[h], rl], writes=[d["RVb"]])
                if GDN_SUB >= 5:
                    c.op("dve", lambda: nc.vector.tensor_scalar(d["dg"][:, 0:128], ident_f, gcl[:, h:h + 1], None, op0=ALU.mult), reads=[Rc, Rsm], writes=[d["Rdg"]])
                    c.op("dve", lambda: nc.vector.tensor_scalar(d["dg"][:, 128:256], ident_f, gc[:, h:h + 1], None, op0=ALU.mult), reads=[Rc, Rsm], writes=[d["Rdg"]])
            if GDN_LIMIT <= 1:
                continue
            for h in range(NH):
                d = H[h]
                c.op("pe", lambda: nc.tensor.matmul(PS[h][:, 128:256], KTb[bi][:, h, :], KTb[bi][:, h, :], start=True, stop=True), reads=[rl], writes=[PR[h]], signal=False)
                c.op("pe", lambda: nc.tensor.matmul(PS[h][:, 256:512], ones_f, d["dg"], start=True, stop=True), reads=[R0, d["Rdg"]], writes=[PR[h]])
                c.op("dve", lambda: nc.vector.tensor_scalar(d["E"], PS[h][:, 256:512], gc[:, h:h + 1], 0.0, op0=ALU.subtract, op1=ALU.min), reads=[PR[h], Rsm], writes=[d["RE"]])
                c.op("act", lambda: nc.scalar.activation(d["E"], d["E"], AF.Exp), reads=[d["RE"]], writes=[d["RE"]])
                c.op("pool", lambda: nc.gpsimd.tensor_tensor(d["E"][:, 0:128], d["E"][:, 0:128], maskU, op=ALU.mult), reads=[d["RE"], R0], writes=[d["RE"]])
                c.op("pool", lambda: nc.gpsimd.tensor_tensor(d["E"][:, 128:256], d["E"][:, 128:256], maskI, op=ALU.mult), reads=[d["RE"], R0], writes=[d["RE"]])
                c.op("dve", lambda: nc.vector.tensor_tensor(d["X"][:, 0:128], d["E"][:, 0:128], PS[h][:, 128:256], op=ALU.mult), reads=[d["RE"], PR[h]], writes=[d["RX"]])
                c.op("dve", lambda: nc.vector.tensor_tensor(d["X"][:, 128:256], ident_f, d["X"][:, 0:128], op=ALU.subtract), reads=[Rc, d["RX"]], writes=[d["RX"]])
            if GDN_LIMIT <= 2:
                continue
            for h in range(NH):
                d = H[h]
                if own:
                    c.op("pe", lambda: nc.tensor.matmul(PS[h][:, 0:128], KTb[bi][:, h, :], QTb[bi][:, h, :], start=True, stop=True), reads=[rl], writes=[PR[h]])
                    c.op("dve", lambda: nc.vector.tensor_tensor(d["AT"], d["E"][:, 128:256], PS[h][:, 0:128], op=ALU.mult), reads=[d["RE"], PR[h]], writes=[d["RAT"]])
                if GDN_SUB2 >= 3:
                    c.op("pe", lambda: nc.tensor.matmul(PS[h][:, 128:256], d["X"][:, 0:128], ident_f, start=True, stop=True), reads=[d["RX"], Rc], writes=[PR[h]])
                if GDN_SUB2 >= 4:
                    c.op("act", lambda: nc.scalar.copy(d["PT"], PS[h][:, 128:256]), reads=[PR[h]], writes=[d["RPT"]])
            if GDN_LIMIT <= 3:
                continue
            for k in range(5 if GDN_SUB3 >= 4 else 1):
                for h in range(NH):
                    d = H[h]
                    wid = 128 if k == 0 else 256
                    c.op("pe", lambda: nc.tensor.matmul(PS[h][:, 256:256 + wid], d["PT"], d["X"][:, 0:wid], start=True, stop=True), reads=[d["RPT"], d["RX"]], writes=[PR[h]])
                    if GDN_SUB3 >= 2:
                        c.op("pe", lambda: nc.tensor.matmul(PS[h][:, 128:256], d["X"][:, 0:128], d["PT"], start=True, stop=True), reads=[d["RPT"], d["RX"]], writes=[PR[h]])
                    if GDN_SUB3 >= 3:
                        c.op("dve", lambda: nc.vector.tensor_copy(d["X"][:, 0:128], PS[h][:, 256:384]), reads=[PR[h]], writes=[d["RX"]])
                        if k >= 1:
                            c.op("dve", lambda: nc.vector.tensor_tensor(d["X"][:, 128:256], d["X"][:, 128:256], PS[h][:, 384:512], op=ALU.add), reads=[PR[h], d["RX"]], writes=[d["RX"]])
                        c.op("act", lambda: nc.scalar.copy(d["PT"], PS[h][:, 128:256]), reads=[PR[h]], writes=[d["RPT"]])
            for h in range(NH):
                d = H[h]
                c.op("pe", lambda: nc.tensor.matmul(PS[h][:, 384:512], d["PT"], d["X"][:, 128:256], start=True, stop=True), reads=[d["RPT"], d["RX"]], writes=[PR[h]])
                c.op("dve", lambda: nc.vector.tensor_tensor(d["N"], d["X"][:, 128:256], PS[h][:, 384:512], op=ALU.add), reads=[PR[h], d["RX"]], writes=[d["RN"]])
            if GDN_LIMIT <= 4:
                continue
            for h in range(NH):
                d = H[h]
                c.op("pe", lambda: nc.tensor.matmul(PS[h][:, 0:128], d["N"], d["Vb"], start=True, stop=True), reads=[d["RN"], d["RVb"]], writes=[PR[h]], signal=False)
                c.op("pe", lambda: nc.tensor.matmul(PS[h][:, 128:256], d["Kbg"], d["N"], start=True, stop=True), reads=[d["RN"], d["RKbg"]], writes=[PR[h]])
                c.op("act", lambda: nc.scalar.copy(d["u"], PS[h][:, 0:128]), reads=[PR[h]], writes=[d["Ru"]])
                c.op("dve", lambda: nc.vector.tensor_copy(d["wT"], PS[h][:, 128:256]), reads=[PR[h]], writes=[d["RwT"]])
            if GDN_LIMIT <= 5:
                continue
            for ck in range(2):
                r = slice(64 * ck, 64 * ck + 64)
                for h in range(NH):
                    d = H[h]
                    c.op("pe", lambda: nc.tensor.matmul(PS[h][r, 256:384], d["wT"][:, r], d["Sb"], start=True, stop=True), reads=[d["RwT"], d["RSb"]], writes=[PR[h]], signal=not own)
                    if own:
                        c.op("pe", lambda: nc.tensor.matmul(PS[h][r, 384:512], QTb[bi][:, h, r], d["Sb"], start=True, stop=True), reads=[rl, d["RSb"]], writes=[PR[h]])
                    c.op("dve", lambda: nc.vector.tensor_tensor(d["vn"][r, :], d["u"][r, :], PS[h][r, 256:384], op=ALU.subtract), reads=[d["Ru"], PR[h]], writes=[d["Rvn"]])
                    if own:
                        c.op("dve", lambda: nc.vector.tensor_scalar(d["ot"][r, :], PS[h][r, 384:512], eg[r, h:h + 1], None, op0=ALU.mult), reads=[PR[h], Rsm], writes=[d["Rot"]])
                for h in range(NH):
                    d = H[h]
                    if own:
                        c.op("pe", lambda: nc.tensor.matmul(PS[h][r, 0:128], d["AT"][r, r], d["vn"][r, :], start=True, stop=True), reads=[d["RAT"], d["Rvn"]], writes=[PR[h]], signal=False)
                    c.op("pe", lambda: nc.tensor.matmul(PS[h][:, 128:256], d["Kd"][r, :], d["vn"][r, :], start=True, stop=True), reads=[d["RKd"], d["Rvn"]], writes=[PR[h]])
                    if own:
                        c.op("dve", lambda: nc.vector.tensor_tensor(d["o"][r, :], d["ot"][r, :], PS[h][r, 0:128], op=ALU.add), reads=[d["Rot"], PR[h]], writes=[d["Ro"]])
                    c.op("dve", lambda: nc.vector.scalar_tensor_tensor(d["Sf"], d["Sf"], egl[:, ck, h:h + 1], PS[h][:, 128:256], op0=ALU.mult, op1=ALU.add),
                         reads=[d["RSf"], Rsm, PR[h]], writes=[d["RSf"]])
                    c.op("act", lambda: nc.scalar.copy(d["Sb"], d["Sf"]), reads=[d["RSf"]], writes=[d["RSb"]])
            if GDN_LIMIT <= 6:
                continue
            if own:
                nbo = nb - NBP
                oi = nbo % 2
                for h in range(NH):
                    d = H[h]
                    c.op("act", lambda: nc.scalar.activation(d["ot"], d["o"], AF.Square, accum_out=d["ss"]), reads=[d["Ro"]], writes=[d["Rot"], d["Rss"]])
                    c.op("act", lambda: nc.scalar.activation(d["ss"], d["ss"], AF.Ln, scale=1.0 / 128, bias=EPS), reads=[d["Rss"]], writes=[d["Rss"]])
                    c.op("act", lambda: nc.scalar.activation(d["ss"], d["ss"], AF.Exp, scale=-0.5), reads=[d["Rss"]], writes=[d["Rss"]])
                    c.op("dve", lambda: nc.vector.tensor_scalar(d["o"], d["o"], d["ss"], None, op0=ALU.mult), reads=[d["Rss"], d["Ro"]], writes=[d["Ro"]])
                    c.op("pe", lambda: nc.tensor.matmul(PS[h][:, 256:384], d["o"], ident_f, start=True, stop=True), reads=[d["Ro"], Rc], writes=[PR[h]])
                    c.op("act", lambda: nc.scalar.activation(d["ot"], PS[h][:, 256:384], AF.Identity, scale=small["gdng"][:, 0:1]), reads=[PR[h], Rc], writes=[d["Rot"]])
                    c.op("dve", lambda: nc.vector.tensor_tensor(ostg[oi][:, h, :], d["ot"], ZSb[bi][:, h, :], op=ALU.mult), reads=[d["Rot"], rl], writes=[Ros[oi]])
                c.dma("sp", oT[nbo // 4, :, NH:2 * NH, (nbo % 4) * 128:(nbo % 4 + 1) * 128], ostg[oi], reads=[Ros[oi]])
        c.barrier()
        es.close()

    def phase_fox():
        es = ExitStack()

        def sb(name, shape, dt):
            return es.enter_context(nc.sbuf_tensor(name + "_fx", list(shape), dt)).ap()
        lf = sb("lf", [NH, TT], F32)
        cum = sb("cum", [NH, TT], F32)
        Rcum = Reg("cum")
        ones_c = sb("ones_c", [128, 1], F32)
        c.op("pool", lambda: nc.gpsimd.memset(ones_c, 1.0), writes=[Rcum])
        c.dma("sp", lf, gatT[0:NH, :], writes=[Rcum])
        c.op("dve", lambda: nc.vector.tensor_tensor_scan(cum, ones_c[0:NH, :].to_broadcast([NH, TT]), lf, 0.0, op0=ALU.mult, op1=ALU.add),
             reads=[Rcum], writes=[Rcum])
        negcum = sb("negcum", [128, NB, NH], F32)
        for blk in range(NB):
            c.op("pe", lambda: nc.tensor.transpose(PS[7][:, blk * NH:(blk + 1) * NH], cum[0:NH, blk * 128:(blk + 1) * 128], ident_f[0:NH, 0:NH]),
                 reads=[Rcum, Rc], writes=[PR[7]], signal=(blk == NB - 1))
        c.op("act", lambda: nc.scalar.activation(negcum.rearrange("p b h -> p (b h)"), PS[7][:, 0:NB * NH], AF.Copy, scale=-1.0),
             reads=[PR[7]], writes=[Rcum])
        sel = sb("sel", [NH, NH, 128], F32)
        c.op("pool", lambda: nc.gpsimd.memset(sel, 1.0), writes=[Rcum])
        c.op("pool", lambda: nc.gpsimd.affine_select(out=sel, in_=sel, pattern=[[-1, NH], [0, 128]], compare_op=ALU.is_equal,
                                                     fill=0.0, base=0, channel_multiplier=1), reads=[Rcum], writes=[Rcum])
        crefB = sb("crefB", [128, NH, NBO], F32)
        for h in range(NH):
            c.op("pe", lambda: nc.tensor.matmul(PS[6][:, h * NBO:(h + 1) * NBO], sel[:, h, :], cum[0:NH, TP + 127:TT:128], start=True, stop=True),
                 reads=[Rcum], writes=[PR[6]], signal=(h == NH - 1))
        c.op("act", lambda: nc.scalar.copy(crefB.rearrange("p h b -> p (h b)"), PS[6][:, 0:NH * NBO]), reads=[PR[6]], writes=[Rcum])
        maskT = sb("maskT", [128, 128], BF16)
        c.op("pool", lambda: nc.gpsimd.memset(maskT, 1.0), writes=[Rcum])
        c.op("pool", lambda: nc.gpsimd.affine_select(out=maskT, in_=maskT, pattern=[[1, 128]], compare_op=ALU.is_ge,
                                                     fill=0.0, base=0, channel_multiplier=-1), reads=[Rcum], writes=[Rcum])
        pneg = sb("pneg", [128, 1], F32)
        c.op("dve", lambda: nc.vector.tensor_scalar(pneg, pmask, -1.0, 30000.0, op0=ALU.add, op1=ALU.mult), reads=[Rc], writes=[Rcum])
        Kh = [sb(f"Kh{i}", [128, TT], BF16) for i in range(2)]
        Qh = [sb(f"Qh{i}", [128, TO], BF16) for i in range(2)]
        Va = [sb(f"Va{i}", [128, NB, 130], BF16) for i in range(2)]
        RK = [Reg(), Reg()]; RQ = [Reg(), Reg()]; RV = [Reg(), Reg()]
        for i in range(2):
            c.op("pool", lambda: nc.gpsimd.memset(Va[i][:, :, 128:130], 1.0), writes=[RV[i]])
        bqs = [sb(f"bq{i}", [128, NB], F32) for i in range(2)]
        Rbq = [Reg(), Reg()]
        Pb = [sb(f"Pb{i}", [128, 128], BF16) for i in range(4)]
        RP = [Reg() for _ in range(4)]
        Rslot = [Reg() for _ in range(8)]
        rs_ = [sb(f"rs{i}", [128, 1], F32) for i in range(2)]
        ss_ = [sb(f"ss{i}", [128, 1], F32) for i in range(2)]
        on_ = [sb(f"on{i}", [128, 128], F32) for i in range(2)]
        junk = sb("junk", [128, 128], F32)
        Re = [Reg(), Reg()]
        Rj = Reg()
        ostg = [sb(f"ostg{i}", [128, 512], BF16) for i in range(2)]
        Ros = [Reg(), Reg()]
        Rtr = [Reg() for _ in range(4)]

        def epilogue(h, qb):
            e = qb % 2
            Ob, RO = PS[4 + e], PR[4 + e]
            c.op("dve", lambda: nc.vector.reciprocal(rs_[e], Ob[:, 128:129]), reads=[RO], writes=[Re[e]])
            c.op("dve", lambda: nc.vector.tensor_scalar(on_[e], Ob[:, 0:128], rs_[e], None, op0=ALU.mult), reads=[RO, Re[e]], writes=[Re[e]])
            c.op("act", lambda: nc.scalar.activation(junk, on_[e], AF.Square, accum_out=ss_[e]), reads=[Re[e]], writes=[Re[e], Rj])
            c.op("act", lambda: nc.scalar.activation(ss_[e], ss_[e], AF.Ln, scale=1.0 / 128, bias=EPS), reads=[Re[e]], writes=[Re[e]])
            c.op("act", lambda: nc.scalar.activation(ss_[e], ss_[e], AF.Exp, scale=-0.5), reads=[Re[e]], writes=[Re[e]])
            c.op("dve", lambda: nc.vector.tensor_scalar(on_[e], on_[e], ss_[e], None, op0=ALU.mult), reads=[Re[e]], writes=[Re[e]])
            q4 = qb % 4
            c.op("pe", lambda: nc.tensor.transpose(PS[6][:, q4 * 128:(q4 + 1) * 128], on_[e], ident_f), reads=[Re[e], Rc], writes=[PR[6]])
            oi = (qb // 4) % 2
            c.op("act", lambda: nc.scalar.activation(ostg[oi][:, q4 * 128:(q4 + 1) * 128], PS[6][:, q4 * 128:(q4 + 1) * 128], AF.Identity,
                                                     scale=small["foxg"][:, 0:1]), reads=[PR[6], Rc], writes=[Ros[oi]])
            if q4 == 3:
                c.dma("sp", oT[qb // 4, :, h, :], ostg[oi], reads=[Ros[oi]])

        Pe = [sb(f"Pe{i}", [128, 4, 128], BF16) for i in range(4)]
        RPe = [[Reg() for _ in range(4)] for _ in range(4)]
        cnts = {"s": 0, "p": 0}
        for h in range(NH):
            bi = h % 2
            c.dma("sp", Kh[bi], KfT[h], writes=[RK[bi]])
            c.dma("sp", Qh[bi], QfT[h], writes=[RQ[bi]])
            c.dma("sp", Va[bi][:, :, 0:128], Vf[:, h * 128:(h + 1) * 128].rearrange("(b p) d -> p b d", p=128), writes=[RV[bi]])
            groups = []
            for qb in range(NBO):
                nkb = NBP + qb + 1
                for kb0 in range(0, nkb, 4):
                    groups.append((qb, kb0, min(4, nkb - kb0)))
            LA = 3
            sl_of = {}

            def emit_S(gi):
                qb, kb0, n = groups[gi]
                sl = cnts["s"] % 4
                cnts["s"] += 1
                sl_of[gi] = sl
                for j in range(n):
                    kb = kb0 + j
                    c.op("pe", lambda: nc.tensor.matmul(PS[sl][:, j * 128:(j + 1) * 128], Kh[bi][:, kb * 128:(kb + 1) * 128],
                                                        Qh[bi][:, qb * 128:(qb + 1) * 128], start=True, stop=True),
                         reads=[RK[bi], RQ[bi]], writes=[Rslot[sl]], signal=(j == n - 1))
            for gi in range(min(LA, len(groups))):
                emit_S(gi)
            for gi, (qb, kb0, n) in enumerate(groups):
                nkb = NBP + qb + 1
                e = qb % 2
                if kb0 == 0:
                    c.op("dve", lambda: nc.vector.tensor_scalar(bqs[e][:, 0:nkb], negcum[:, 0:nkb, h], crefB[:, h, qb:qb + 1], None, op0=ALU.add),
                         reads=[Rcum], writes=[Rbq[e]])
                    if NBP > 0:
                        c.op("dve", lambda: nc.vector.tensor_scalar(bqs[e][:, 0:NBP], bqs[e][:, 0:NBP], pneg[:, 0:1], None, op0=ALU.add),
                             reads=[Rcum, Rbq[e]], writes=[Rbq[e]])
                    c.op("act", lambda: nc.scalar.activation(bqs[e][:, 0:nkb], bqs[e][:, 0:nkb], AF.Exp), reads=[Rbq[e]], writes=[Rbq[e]])
                sl = sl_of.pop(gi)
                pi = cnts["p"] % 4
                cnts["p"] += 1
                c.op("act", lambda: nc.scalar.activation(Pe[pi][:, 0:n, :].rearrange("p a b -> p (a b)"), PS[sl][:, 0:n * 128], AF.Exp),
                     reads=[Rslot[sl]], writes=RPe[pi][0:n])
                for j in range(n):
                    kb = kb0 + j
                    if kb == nkb - 1:
                        c.op("dve", lambda: nc.vector.scalar_tensor_tensor(Pe[pi][:, j, :], Pe[pi][:, j, :], bqs[e][:, kb:kb + 1], maskT,
                                                                           op0=ALU.mult, op1=ALU.mult),
                             reads=[Rbq[e], Rcum], writes=[RPe[pi][j]])
                    else:
                        c.op("dve", lambda: nc.vector.tensor_scalar(Pe[pi][:, j, :], Pe[pi][:, j, :], bqs[e][:, kb:kb + 1], None, op0=ALU.mult),
                             reads=[Rbq[e]], writes=[RPe[pi][j]])
                    c.op("pe", lambda: nc.tensor.matmul(PS[4 + e][:, 0:130], Pe[pi][:, j, :], Va[bi][:, kb, :], start=(kb == 0), stop=(kb == nkb - 1)),
                         reads=[RPe[pi][j], RV[bi]], writes=[PR[4 + e]], signal=(kb == nkb - 1))
                if gi + LA < len(groups):
                    emit_S(gi + LA)
                if kb0 + n == nkb:
                    epilogue(h, qb)
        c.barrier()
        es.close()

    tile_phase(1)
    if stop_after <= 2:
        for to in range(NTO):
            c.dma("sp", yT[to], x1s[to])
        c.final_wait("sp")
        return nc
    if stop_after >= 3 and not skip_gdn:
        phase_gdn()
    phase_fox()
    if stop_after <= 3:
        c.final_wait("sp")
        return nc
    tile_phase(4)
    c.final_wait("sp")
    return nc


_CACHE = {}


def kernel(**inputs):
    inp = {k: np.asarray(v) for k, v in inputs.items()}
    B, S, D = inp["x"].shape
    half = S // 2
    cfg = Cfg(D=D, TP=half, TO=half)
    key = (D, S)
    if key not in _CACHE:
        _CACHE[key] = build(cfg)
    nc = _CACHE[key]
    hw = host_weights(cfg, inp)
    in_maps = []
    for b in range(B):
        for sidx in range(2):
            m = dict(hw)
            if sidx == 1:
                xs = inp["x"][b]
            else:
                xs = np.concatenate([inp["x"][b, :half], inp["x"][b, :half]], axis=0)
            m["xT"] = host_x(cfg, xs)
            m["cT"] = np.ascontiguousarray(inp["c"][b].reshape(cfg.DC, 128).T)
            m["pmask"] = np.full((128, 1), float(sidx), np.float32)
            in_maps.append(m)
    res = run_bass_kernel_spmd(nc, in_maps, core_ids=list(range(2 * B)))
    out = np.empty((B, S, D), np.float32)
    for b in range(B):
        for sidx in range(2):
            out[b, sidx * half:(sidx + 1) * half] = host_unx(cfg, np.asarray(res.results[2 * b + sidx]["yT"]))
    return out
```
